# Optimizing a Trainium2 kernel written in Bass

```python
import jax, jax.numpy as jnp
from jax import lax
import numpy as np

D_MODEL = 1024
BATCH = 8
SEQ = 2048
DEPTH = 4
DEC_BATCH = 16
DEC_SEQ = 16
PAST_LEN = 1024

CHUNK = 64
SB_HEADS = 6
SB_DIM = 64
SB_WIDTH = SB_HEADS * SB_DIM
SB_BLOCK = 128
POOL_WINDOWS = (2, 4, 8, 16)
POOL_GROUPS = 4
POOL_GDIM = 64
POOL_WIDTH = POOL_GROUPS * POOL_GDIM
POOL_HIST = max(POOL_WINDOWS) - 1
GLA_HEADS = 4
GLA_DK = 96
GLA_DV = 96
GLA_KW = GLA_HEADS * GLA_DK
GLA_VW = GLA_HEADS * GLA_DV
GLA_RANK = 16
GLA_TAU = 16.0
N_BRANCH = 3
MEM_LEN = 256
X_HEADS = 4
X_DIM = D_MODEL // X_HEADS
FF = -(-8 * D_MODEL // (3 * 256)) * 256
EPS = 1e-6
IN_SPLITS = (SB_WIDTH, SB_WIDTH, SB_WIDTH, POOL_WIDTH, GLA_KW, GLA_KW, GLA_VW, GLA_VW, GLA_RANK, D_MODEL, D_MODEL, D_MODEL)
IN_WIDTH = 3 * SB_WIDTH + POOL_WIDTH + 2 * GLA_KW + 2 * GLA_VW + GLA_RANK + N_BRANCH * D_MODEL

kernel_name = 'stickbreak_pool_gla_hybrid_stream_step'


def rmsnorm(x, g):
    xf = x.astype(jnp.float32)
    y = xf * lax.rsqrt(jnp.mean(xf * xf, axis=-1, keepdims=True) + EPS)
    return (y * g.astype(jnp.float32)).astype(x.dtype)


def stick_breaking(q, k, v, q_pos, k_pos):
    z = jnp.einsum('bqhd,bkhd->bhqk', q, k).astype(jnp.float32) * (q.shape[-1] ** -0.5)
    mask = (k_pos[None, :] < q_pos[:, None])[None, None]
    log_beta = jax.nn.log_sigmoid(z)
    log_fail = jnp.where(mask, jax.nn.log_sigmoid(-z), 0.0)
    after = lax.cumsum(log_fail, axis=3, reverse=True) - log_fail
    w = jnp.where(mask, jnp.exp(log_beta + after), 0.0)
    return jnp.einsum('bhqk,bkhd->bqhd', w.astype(v.dtype), v)


def sb_prompt(q, k, v):
    B, T, H, D = q.shape
    nb = T // SB_BLOCK
    pos = jnp.arange(T)
    qb = q.reshape(B, nb, SB_BLOCK, H, D).swapaxes(0, 1)
    pb = pos.reshape(nb, SB_BLOCK)
    ob = lax.map(lambda a: stick_breaking(a[0], k, v, a[1], pos), (qb, pb))
    return ob.swapaxes(0, 1).reshape(B, T, H, D)


def pool_mix(u, hist, pos0):
    B, T, C = u.shape
    full = jnp.concatenate([hist.astype(u.dtype), u], axis=1)
    ff = full.astype(jnp.float32)
    cs = jnp.concatenate([jnp.zeros((B, 1, C), jnp.float32), jnp.cumsum(ff, axis=1)], axis=1)
    pos = pos0 + jnp.arange(T)
    P = POOL_HIST
    outs = []
    for g, w in enumerate(POOL_WINDOWS):
        sl = slice(g * POOL_GDIM, (g + 1) * POOL_GDIM)
        s = cs[:, P + 1:P + 1 + T, sl] - cs[:, P + 1 - w:P + 1 - w + T, sl]
        cnt = jnp.minimum(pos + 1, w).astype(jnp.float32)
        outs.append(s / cnt[None, :, None])
    pooled = jnp.concatenate(outs, axis=-1) - ff[:, P:]
    return pooled.astype(u.dtype), full[:, -POOL_HIST:]


def gla(q, k, v, log_a, S0, L):
    B, T, H, DK = q.shape
    DV = v.shape[-1]
    n = T // L

    def to_chunks(a):
        return a.reshape(B, n, L, H, a.shape[-1]).swapaxes(0, 1).astype(jnp.float32)

    causal = jnp.tril(jnp.ones((L, L), bool))[None, :, :, None, None]

    def step(S, inp):
        qc, kc, vc, ac = inp
        b = jnp.cumsum(ac, axis=1)
        o_inter = jnp.einsum('blhk,bhkv->blhv', qc * jnp.exp(b), S)
        rel = b[:, :, None] - b[:, None, :]
        decay = jnp.exp(jnp.where(causal, rel, -jnp.inf))
        att = jnp.einsum('bthk,bshk,btshk->bhts', qc, kc, decay)
        o_intra = jnp.einsum('bhts,bshv->bthv', att, vc)
        bL = b[:, -1]
        S_new = jnp.exp(bL)[..., None] * S + jnp.einsum('bshk,bshv->bhkv', kc * jnp.exp(bL[:, None] - b), vc)
        return S_new, o_inter + o_intra

    S, o = lax.scan(step, S0.astype(jnp.float32), (to_chunks(q), to_chunks(k), to_chunks(v), to_chunks(log_a)))
    return S, o.swapaxes(0, 1).reshape(B, T, H, DV)


def token_mixers(h, p, l, sb_past, pool_hist, S0, pos0, gla_chunk):
    B, T, _ = h.shape
    idx = np.cumsum(np.array(IN_SPLITS))[:-1].tolist()
    qa, ka, va, u, qc, kc, vc, gc, lr, ga, gb, gcg = jnp.split(h @ p['w_in'][l], idx, axis=-1)
    qa = qa.reshape(B, T, SB_HEADS, SB_DIM)
    ka = ka.reshape(B, T, SB_HEADS, SB_DIM)
    va = va.reshape(B, T, SB_HEADS, SB_DIM)
    if sb_past is None:
        oa = sb_prompt(qa, ka, va)
    else:
        K = jnp.concatenate([sb_past[0].astype(ka.dtype), ka], axis=1)
        V = jnp.concatenate([sb_past[1].astype(va.dtype), va], axis=1)
        oa = stick_breaking(qa, K, V, pos0 + jnp.arange(T), jnp.arange(K.shape[1]))
    oa = oa.reshape(B, T, SB_WIDTH)
    pooled, pool_new = pool_mix(u, pool_hist, pos0)
    ob = jnp.einsum('btgc,gcd->btgd', pooled.reshape(B, T, POOL_GROUPS, POOL_GDIM), p['w_pool'][l])
    ob = ob.reshape(B, T, POOL_WIDTH) * p['pool_scale'][l]
    log_a = jax.nn.log_sigmoid((lr @ p['w_gla_a2'][l] + p['b_gla_a'][l]).astype(jnp.float32)) / GLA_TAU
    S_new, oc = gla((qc * (GLA_DK ** -0.5)).reshape(B, T, GLA_HEADS, GLA_DK),
                    kc.reshape(B, T, GLA_HEADS, GLA_DK),
                    vc.reshape(B, T, GLA_HEADS, GLA_DV),
                    log_a.reshape(B, T, GLA_HEADS, GLA_DK), S0, gla_chunk)
    oc = rmsnorm(oc, p['gla_norm'][l].reshape(GLA_HEADS, GLA_DV)).reshape(B, T, GLA_VW)
    oc = (oc * jax.nn.silu(gc.astype(jnp.float32))).astype(h.dtype)
    merged = (jax.nn.sigmoid(ga) * (oa @ p['w_branch_a'][l])
              + jax.nn.sigmoid(gb) * (ob @ p['w_branch_b'][l])
              + jax.nn.sigmoid(gcg) * (oc @ p['w_branch_c'][l]))
    out = (merged @ p['w_mix_out'][l]).astype(h.dtype)
    return out, ka, va, pool_new, S_new


def mem_kv(mem, g, wk, wv):
    B, M, _ = mem.shape
    m = rmsnorm(mem, g)
    return (m @ wk).reshape(B, M, X_HEADS, X_DIM), (m @ wv).reshape(B, M, X_HEADS, X_DIM)


def cross_attn(h, mk, mv, wq, wo):
    B, T, _ = h.shape
    q = (h @ wq).reshape(B, T, X_HEADS, X_DIM)
    s = jnp.einsum('bthd,bmhd->bhtm', q, mk.astype(q.dtype)).astype(jnp.float32) * (X_DIM ** -0.5)
    pr = jax.nn.softmax(s, axis=-1).astype(h.dtype)
    o = jnp.einsum('bhtm,bmhd->bthd', pr, mv.astype(h.dtype)).reshape(B, T, D_MODEL)
    return o @ wo


def swiglu(h, w_in, w_out):
    g, u = jnp.split(h @ w_in, 2, axis=-1)
    return (jax.nn.silu(g) * u) @ w_out


def layer(x, p, l, sb_past, pool_hist, S0, mk, mv, pos0, gla_chunk):
    h = rmsnorm(x, p['norm_mix_pre'][l])
    mix, k_new, v_new, pool_new, S_new = token_mixers(h, p, l, sb_past, pool_hist, S0, pos0, gla_chunk)
    x = x + rmsnorm(mix, p['norm_mix_post'][l]).astype(x.dtype)
    h = rmsnorm(x, p['norm_x_pre'][l])
    x = x + rmsnorm(cross_attn(h, mk, mv, p['w_xq'][l], p['w_xo'][l]), p['norm_x_post'][l]).astype(x.dtype)
    h = rmsnorm(x, p['norm_ffn_pre'][l])
    x = x + rmsnorm(swiglu(h, p['w_ffn_in'][l], p['w_ffn_out'][l]), p['norm_ffn_post'][l]).astype(x.dtype)
    return x, k_new, v_new, pool_new, S_new


def setup_inputs(seed: int = 0) -> dict:
    key = jax.random.key(seed)
    ks = jax.random.split(key, 40)
    f32 = jnp.float32

    def nrm(k, shape, fan_in):
        return jax.random.normal(k, shape, f32) * (fan_in ** -0.5)

    def gain(k, shape):
        return 1.0 + 0.05 * jax.random.normal(k, shape, f32)

    L = DEPTH
    return {
        'x_prompt': jax.random.normal(ks[0], (BATCH, SEQ, D_MODEL), f32),
        'x_sample': jax.random.normal(ks[1], (DEC_BATCH, DEC_SEQ, D_MODEL), f32),
        'mem_prompt': jax.random.normal(ks[2], (BATCH, MEM_LEN, D_MODEL), f32),
        'cache_sb_k': jax.random.normal(ks[3], (L, DEC_BATCH, PAST_LEN, SB_HEADS, SB_DIM), f32),
        'cache_sb_v': jax.random.normal(ks[4], (L, DEC_BATCH, PAST_LEN, SB_HEADS, SB_DIM), f32),
        'state_pool': jax.random.normal(ks[5], (L, DEC_BATCH, POOL_HIST, POOL_WIDTH), f32),
        'state_gla': jax.random.normal(ks[6], (L, DEC_BATCH, GLA_HEADS, GLA_DK, GLA_DV), f32),
        'cache_mem_k': jax.random.normal(ks[7], (L, DEC_BATCH, MEM_LEN, X_HEADS, X_DIM), f32),
        'cache_mem_v': jax.random.normal(ks[8], (L, DEC_BATCH, MEM_LEN, X_HEADS, X_DIM), f32),
        'w_in': nrm(ks[9], (L, D_MODEL, IN_WIDTH), D_MODEL),
        'w_gla_a2': nrm(ks[10], (L, GLA_RANK, GLA_KW), GLA_RANK),
        'b_gla_a': 0.1 * jax.random.normal(ks[11], (L, GLA_KW), f32),
        'gla_norm': gain(ks[12], (L, GLA_VW)),
        'w_pool': nrm(ks[13], (L, POOL_GROUPS, POOL_GDIM, POOL_GDIM), POOL_GDIM),
        'pool_scale': 1.0 + 0.1 * jax.random.normal(ks[14], (L, POOL_WIDTH), f32),
        'w_branch_a': nrm(ks[15], (L, SB_WIDTH, D_MODEL), SB_WIDTH),
        'w_branch_b': nrm(ks[16], (L, POOL_WIDTH, D_MODEL), POOL_WIDTH),
        'w_branch_c': nrm(ks[17], (L, GLA_VW, D_MODEL), GLA_VW),
        'w_mix_out': nrm(ks[18], (L, D_MODEL, D_MODEL), D_MODEL),
        'mem_norm': gain(ks[19], (L, D_MODEL)),
        'w_xq': nrm(ks[20], (L, D_MODEL, D_MODEL), D_MODEL),
        'w_xk': nrm(ks[21], (L, D_MODEL, D_MODEL), D_MODEL),
        'w_xv': nrm(ks[22], (L, D_MODEL, D_MODEL), D_MODEL),
        'w_xo': nrm(ks[23], (L, D_MODEL, D_MODEL), D_MODEL),
        'w_ffn_in': nrm(ks[24], (L, D_MODEL, 2 * FF), D_MODEL),
        'w_ffn_out': nrm(ks[25], (L, FF, D_MODEL), FF),
        'norm_mix_pre': gain(ks[26], (L, D_MODEL)),
        'norm_mix_post': gain(ks[27], (L, D_MODEL)),
        'norm_x_pre': gain(ks[28], (L, D_MODEL)),
        'norm_x_post': gain(ks[29], (L, D_MODEL)),
        'norm_ffn_pre': gain(ks[30], (L, D_MODEL)),
        'norm_ffn_post': gain(ks[31], (L, D_MODEL)),
    }


def reference(x_prompt, x_sample, mem_prompt, cache_sb_k, cache_sb_v, state_pool, state_gla,
              cache_mem_k, cache_mem_v, w_in, w_gla_a2, b_gla_a, gla_norm, w_pool, pool_scale,
              w_branch_a, w_branch_b, w_branch_c, w_mix_out, mem_norm, w_xq, w_xk, w_xv, w_xo,
              w_ffn_in, w_ffn_out, norm_mix_pre, norm_mix_post, norm_x_pre, norm_x_post,
              norm_ffn_pre, norm_ffn_post):
    p = dict(w_in=w_in, w_gla_a2=w_gla_a2, b_gla_a=b_gla_a, gla_norm=gla_norm, w_pool=w_pool,
             pool_scale=pool_scale, w_branch_a=w_branch_a, w_branch_b=w_branch_b,
             w_branch_c=w_branch_c, w_mix_out=w_mix_out, w_xq=w_xq, w_xo=w_xo,
             w_ffn_in=w_ffn_in, w_ffn_out=w_ffn_out, norm_mix_pre=norm_mix_pre,
             norm_mix_post=norm_mix_post, norm_x_pre=norm_x_pre, norm_x_post=norm_x_post,
             norm_ffn_pre=norm_ffn_pre, norm_ffn_post=norm_ffn_post)
    B = x_prompt.shape[0]
    Ts = x_sample.shape[1]
    yp, ys = x_prompt, x_sample
    kp, vp, poolp, glap, mkp, mvp = [], [], [], [], [], []
    kss, vss, pools, glas = [], [], [], []
    pool_zero = jnp.zeros((B, POOL_HIST, POOL_WIDTH), x_prompt.dtype)
    gla_zero = jnp.zeros((B, GLA_HEADS, GLA_DK, GLA_DV), jnp.float32)
    for l in range(DEPTH):
        mk, mv = mem_kv(mem_prompt, mem_norm[l], w_xk[l], w_xv[l])
        yp, k_new, v_new, pool_new, S_new = layer(yp, p, l, None, pool_zero, gla_zero, mk, mv, 0, CHUNK)
        kp.append(k_new); vp.append(v_new); poolp.append(pool_new); glap.append(S_new)
        mkp.append(mk); mvp.append(mv)
        ys, k_new, v_new, pool_new, S_new = layer(ys, p, l, (cache_sb_k[l], cache_sb_v[l]), state_pool[l],
                                                 state_gla[l], cache_mem_k[l], cache_mem_v[l], PAST_LEN, Ts)
        kss.append(k_new); vss.append(v_new); pools.append(pool_new); glas.append(S_new)
    return (yp, ys,
            jnp.stack(kp), jnp.stack(vp), jnp.stack(poolp), jnp.stack(glap), jnp.stack(mkp), jnp.stack(mvp),
            jnp.stack(kss), jnp.stack(vss), jnp.stack(pools), jnp.stack(glas))
```

```python
import numpy as np
from contextlib import ExitStack
import concourse.bass as bass
import concourse.mybir as mybir
from concourse.bass_utils import run_bass_kernel_spmd

F32 = mybir.dt.float32
BF16 = mybir.dt.bfloat16
AF = mybir.ActivationFunctionType
ALU = mybir.AluOpType

NL = 4
EPS = 1e-6
NT = 2080
CW = 544
O_Q, O_K, O_V, O_U = 0, 384, 768, 1152
O_QC, O_KC, O_VC, O_GC, O_LR, O_G = 1408, 1792, 2176, 2560, 2944, 2960
C_ID, C_TRI, C_ONE, C_MASK, C_TBD, C_MBD, C_INV, C_END = 0, 128, 256, 384, 896, 1024, 1152, 1182


class Res:
    __slots__ = ("w", "r", "name")

    def __init__(self, name=""):
        self.w = None
        self.r = {}
        self.name = name


class PRes(Res):
    __slots__ = ()


class ARes(Res):
    __slots__ = ("lo", "hi")

    def __init__(self, name, lo, hi):
        Res.__init__(self, name)
        self.lo = lo
        self.hi = hi


class DmaSem:
    def __init__(self, sem):
        self.sem = sem
        self.count = 0


class Prog:
    def __init__(self, nc, es):
        self.nc = nc
        self.es = es
        self.engs = {"sp": nc.sync, "act": nc.scalar, "pool": nc.gpsimd, "dve": nc.vector, "pe": nc.tensor}
        self.q = {e: [] for e in self.engs}
        self.sem = {e: es.enter_context(nc.semaphore("S_" + e)) for e in ("act", "pool", "dve", "pe")}
        self.cnt = {e: 0 for e in self.sem}
        self.seen = {e: {} for e in self.engs}
        self.out_sems = []
        self.nsem = 0
        self.arena = []

    def dma_sem(self, out=False):
        self.nsem += 1
        s = DmaSem(self.es.enter_context(self.nc.semaphore("D%d" % self.nsem)))
        if out:
            self.out_sems.append(s)
        return s

    def op(self, eng, fn, reads=(), writes=(), inc=True, dma=None):
        waits = {}
        own = self.sem.get(eng)

        def need(ev):
            if ev is None:
                return
            s, v = ev
            if s is own and dma is None and (eng == "pe" or v > self.cnt[eng]):
                return
            k = id(s)
            if k not in waits or waits[k][1] < v:
                waits[k] = (s, v)

        pr = [r for r in reads if isinstance(r, PRes)]
        if pr:
            reads = [r for r in reads if not isinstance(r, PRes)]
            writes = list(writes) + pr
        for r in reads:
            need(r.w)
        for w in writes:
            need(w.w)
            for ev in w.r.values():
                need(ev)
            if isinstance(w, ARes):
                for o in self.arena:
                    if o is not w and o.lo < w.hi and w.lo < o.hi:
                        need(o.w)
                        for ev in o.r.values():
                            need(ev)
        wl = []
        for k, (s, v) in waits.items():
            if self.seen[eng].get(k, 0) < v:
                self.seen[eng][k] = v
                wl.append((s, v))
        if dma is not None:
            dma.count += 16
            ev = (dma.sem, dma.count)
            incv = 16
        else:
            if inc:
                self.cnt[eng] += 1
                ev = (self.sem[eng], self.cnt[eng])
            else:
                ev = (self.sem[eng], self.cnt[eng] + 1)
            incv = 1
        for r in reads:
            k = id(ev[0])
            if k not in r.r or r.r[k][1] < ev[1]:
                r.r[k] = ev
        for w in writes:
            w.w = ev
            w.r = {}
        self.q[eng].append((wl, fn, (ev[0], incv) if (inc or dma is not None) else None))

    def mm(self, out, lhsT, rhs, start, stop, reads, writes, inc, sgc=False):
        if sgc:
            self.op("pe", lambda e: e.matmul(out, lhsT=lhsT, rhs=rhs, start=start, stop=stop, skip_group_check=True), reads, writes, inc)
        else:
            self.op("pe", lambda e: e.matmul(out, lhsT=lhsT, rhs=rhs, start=start, stop=stop), reads, writes, inc)

    def tr(self, out, in_, ident, reads, writes, inc):
        self.op("pe", lambda e: e.transpose(out, in_, ident), reads, writes, inc)

    def act(self, out, in_, func, reads, writes, **kw):
        self.op("act", lambda e: e.activation(out=out, in_=in_, func=func, **kw), reads, writes)

    def dve(self, name, reads, writes, **kw):
        self.op("dve", lambda e: getattr(e, name)(**kw), reads, writes)

    def emit(self):
        nc = self.nc
        finals = [(s.sem, s.count) for s in self.out_sems if s.count > 0]
        with nc.Block() as block:
            def run(name):
                def f(eng):
                    for wl, fn, inc in self.q[name]:
                        for s, v in wl:
                            eng.wait_ge(s, v)
                        ins = fn(eng)
                        if inc is not None:
                            ins.then_inc(inc[0], inc[1])
                    if name == "sp":
                        for s, v in finals:
                            eng.wait_ge(s, v)
                return f

            block.sync(run("sp"))
            block.scalar(run("act"))
            block.gpsimd(run("pool"))
            block.vector(run("dve"))
            block.tensor(run("pe"))


def subs(c):
    return [(0, 512)] + ([(512, 32)] if c == 3 else [])


_LASTP = [None]


class StopBuild(Exception):
    pass


STOP = [None]


def ck(name):
    if STOP[0] == name:
        raise StopBuild(name)


def build(nl=NL):
    nc = bass.Bass("TRN2", target_bir_lowering=False)
    es = ExitStack()

    def din(name, shape):
        return nc.dram_tensor(name, list(shape), F32, kind="ExternalInput").ap()

    def dout(name, shape):
        return nc.dram_tensor(name, list(shape), F32, kind="ExternalOutput").ap()

    xp = din("xp", [2048, 1024]); xs = din("xs", [32, 1024]); mem = din("mem", [256, 1024])
    csk = din("csk", [nl, 2, 1024, 384]); csv = din("csv", [nl, 2, 1024, 384])
    spool = din("spool", [nl, 2, 15, 256]); sgla = din("sgla", [nl, 2, 4, 96, 96])
    cmk = din("cmk", [nl, 2, 256, 1024]); cmv = din("cmv", [nl, 2, 256, 1024])
    w_in = din("w_in", [nl, 1024, 6032]); w_a2 = din("w_gla_a2", [nl, 16, 384]); b_a = din("b_gla_a", [nl, 384])
    gla_norm = din("gla_norm", [nl, 384]); w_pool = din("w_pool", [nl, 4, 64, 64]); pool_scale = din("pool_scale", [nl, 256])
    w_ba = din("w_branch_a", [nl, 384, 1024]); w_bb = din("w_branch_b", [nl, 256, 1024]); w_bc = din("w_branch_c", [nl, 384, 1024])
    w_mix = din("w_mix_out", [nl, 1024, 1024]); mem_norm = din("mem_norm", [nl, 1024])
    w_xq = din("w_xq", [nl, 1024, 1024]); w_xk = din("w_xk", [nl, 1024, 1024]); w_xv = din("w_xv", [nl, 1024, 1024]); w_xo = din("w_xo", [nl, 1024, 1024])
    w_fi = din("w_ffn_in", [nl, 1024, 5632]); w_fo = din("w_ffn_out", [nl, 2816, 1024])
    gnames = ["norm_mix_pre", "norm_mix_post", "norm_x_pre", "norm_x_post", "norm_ffn_pre", "norm_ffn_post"]
    gains = [din(n, [nl, 1024]) for n in gnames]
    cst = din("cst", [128, C_END])

    yp = dout("yp", [2048, 1024]); ys = dout("ys", [32, 1024])
    okp = dout("okp", [nl, 2048, 384]); ovp = dout("ovp", [nl, 2048, 384])
    opp = dout("opp", [nl, 15, 256]); ogp = dout("ogp", [nl, 4, 96, 96])
    omk = dout("omk", [nl, 256, 1024]); omv = dout("omv", [nl, 256, 1024])
    oks = dout("oks", [nl, 32, 384]); ovs = dout("ovs", [nl, 32, 384])
    ops_ = dout("ops", [nl, 2, 15, 256]); ogs = dout("ogs", [nl, 2, 4, 96, 96])

    with es:
        P = Prog(nc, es)
        _LASTP[0] = P

        def sb(name, shape, dt):
            return es.enter_context(nc.sbuf_tensor(name, shape, dt))

        xT = sb("xT", [128, 8, NT], F32)
        RX = [Res("x%d" % c) for c in range(4)]
        kT = sb("kT", [128, 3, NT], BF16); RKT = Res("kT")
        vtok = sb("vtok", [128, 16, 384], BF16); RVT = Res("vtok")
        vtoks = sb("vtoks", [16, 2, 384], BF16); RVTS = Res("vtoks")
        hT = sb("hT", [128, 8, CW], BF16); RH = Res("hT")
        rstd = sb("rstd", [128, CW], F32); RRS = Res("rstd")
        big = sb("big", [128, 11, CW], BF16); RBIG = Res("big")
        oaT = sb("oaT", [128, 3, CW], BF16); ROA = Res("oaT")
        obT = sb("obT", [128, 2, CW], BF16); ROB = Res("obT")
        ocT = sb("ocT", [96, 4, CW], BF16); ROC = Res("ocT")
        cstb = sb("cstb", [128, C_INV], BF16); RC = Res("cst")
        identf = sb("identf", [128, 128], F32)
        invcf = sb("invcf", [128, 30], F32)
        gall = sb("gall", [128, 7, 4, 8], F32)
        gln = sb("gln", [96, 4, 4], F32)
        psc = sb("psc", [128, 4, 2], F32)
        epsb = sb("epsb", [128, 1], F32)
        mkT = sb("mkT", [128, 8, 256], BF16); RMK = Res("mkT")
        mvt = sb("mvt", [128, 2, 1024], BF16); RMV = Res("mvt")
        wpbd = sb("wpbd", [128, 2, 128], BF16); RWP = Res("wpbd")
        lrT = sb("lrT", [32, CW], BF16); RLR = Res("lrT")
        wa2 = sb("wa2", [32, 384], BF16); RWA2 = Res("wa2")
        ebl = sb("ebl", [96, 4, 8], F32); REBL = Res("ebl")
        Sst = sb("Sst", [96, 4, 96], F32); RS = Res("S")
        uhalo = sb("uhalo", [128, 2, 15], F32); RUH = Res("uhalo")
        NS = 3
        SLOT = 4352
        ring = [sb("ring%d" % i, [128, SLOT], BF16) for i in range(NS)]
        RRING = [Res("ring%d" % i) for i in range(NS)]
        DRING = [P.dma_sem() for _ in range(NS)]
        AR = 40960
        arena = sb("arena", [128, AR // 2], BF16)
        print("sbuf remaining", nc.sbuf_bytes_remaining)

        def av(name, off, shape, dt):
            esz = 4 if dt == F32 else 2
            nel = 1
            for d in shape[1:]:
                nel *= d
            assert off % 4 == 0 and off + nel * esz <= AR, (name, off, nel * esz)
            a = arena[:shape[0], off // 2:off // 2 + nel * esz // 2]
            if dt == F32:
                a = a.bitcast(F32)
            if len(shape) == 3:
                a = a.rearrange("p (a b) -> p a b", a=shape[1])
            elif len(shape) == 4:
                a = a.rearrange("p (a b c) -> p a b c", a=shape[1], b=shape[2])
            r = ARes(name, off, off + nel * esz)
            P.arena.append(r)
            return a, r

        stg, RST = av("stg", 0, [128, 8, CW], F32)
        sqb, RSQ = av("sqb", 17408, [128, 8, CW], BF16)
        memT, RMEM = av("memT", 26112, [128, 8, 256], F32)
        mnT, RMN = av("mnT", 34304, [128, 8, 256], BF16)
        cstf, RCF = av("cstf", 0, [128, C_END], F32)
        qT, RQ = av("qT", 0, [128, 3, CW], BF16)
        ez, REZ, spb, RSP, enb, REN, wb, RWB = [], [], [], [], [], [], [], []
        for i in range(2):
            a, r = av("ez%d" % i, 3264 + 2048 * i, [128, 512], F32); ez.append(a); REZ.append(r)
            a, r = av("sp%d" % i, 7360 + 1024 * i, [128, 512], BF16); spb.append(a); RSP.append(r)
            a, r = av("en%d" % i, 9408 + 2048 * i, [128, 512], F32); enb.append(a); REN.append(r)
            a, r = av("wb%d" % i, 13504 + 1024 * i, [128, 512], BF16); wb.append(a); RWB.append(r)
        Rb, RR = av("Rb", 15552, [128, 512], BF16)
        kcs, RKCS = av("kcs", 16576, [128, 8, 384], BF16)
        kcsT, RKCST = av("kcsT", 22720, [128, 3, 1024], BF16)
        vcs, RVCS = av("vcs", 28864, [128, 8, 384], BF16)
        PW = 15 + 512 + 62
        uT, RU = av("uT", 0, [128, 2, PW], F32)
        pA, RPA = av("pA", 4712, [128, 2, PW], F32)
        pB, RPB = av("pB", 9424, [128, 2, PW], F32)
        pC, RPC = av("pC", 14136, [128, PW], F32)
        pD, RPD = av("pD", 16492, [128, PW], F32)
        pooled, RPL = av("pooled", 18848, [128, 2, CW], BF16)
        ptmp, RPT_ = av("ptmp", 21024, [128, 16], F32)
        HWD = 288
        qtT, RQT = av("qtT", 0, [96, 4, HWD], BF16)
        ktT, RKTT_ = av("ktT", 2304, [96, 4, HWD], BF16)
        sgc, RSG = av("sgc", 4608, [96, 4, HWD], BF16)
        eq, REQ = av("eq", 6912, [96, 4, HWD], F32)
        eqi, REQI = av("eqi", 11520, [96, 4, HWD], BF16)
        vct, RVCT = av("vct", 13824, [128, 2, 384], BF16)
        lat, RLA = av("lat", 15360, [128, 2, 384], BF16)
        ktt, RKTT = av("ktt", 16896, [128, 2, 384], BF16)
        ekt, REKT = av("ekt", 18432, [128, 384], F32)
        gt1, RGT1 = av("gt1", 19968, [128, 512], F32)
        vcts, RVCTS = av("vcts", 22016, [16, 2, 384], BF16)
        lats, RLAS = av("lats", 23552, [16, 2, 384], BF16)
        ktts, RKTTS = av("ktts", 25088, [16, 2, 384], BF16)
        Sbf, RSBF = av("Sbf", 26624, [96, 6, 4, 96], BF16)
        attb, RATT = av("attb", 31232, [128, 4, 128], BF16)
        osb, ROS = av("osb", 32256, [96, 4, 128], F32)
        osq, ROSQ = av("osq", 34304, [96, 4, 128], BF16)
        grs, RGRS = av("grs", 35328, [96, 512], F32)
        Stm, RSTM = av("Stm", 37376, [96, 4, 96], F32)
        Ss, RSS = av("Ss", 38912, [96, 4, 96], F32)
        sg3, RSG3 = [], []
        for i in range(3):
            a, r = av("sg%d" % i, 26112 + 2048 * i, [128, 512], F32); sg3.append(a); RSG3.append(r)
        mt1, RMT1 = av("mt1", 32256, [128, 512], F32)
        mt2, RMT2 = av("mt2", 34304, [128, 512], F32)
        pTb, RPT = av("pTb", 26112, [128, 2, 512], BF16)
        rden, RRD = av("rden", 28160, [128, 512], F32)
        mkl, RMKL = av("mkl", 0, [128, 2, 1024], BF16)
        mkTs, RMKTS = av("mkTs", 30208, [128, 8, 256], BF16)
        mvs, RMVS = av("mvs", 34304, [128, 2, 1024], BF16)
        ft1, RFT1 = av("ft1", 26112, [128, 512], F32)

        ps = [es.enter_context(nc.psum_tensor("ps%d" % i, [128, 512], F32)) for i in range(8)]
        RP = [PRes("ps%d" % i) for i in range(8)]
        bank_i = [0]

        held = set()

        def bank(hold=False):
            while True:
                b = bank_i[0] % 8
                bank_i[0] += 1
                if b not in held:
                    break
            if hold:
                held.add(b)
            return b

        ring_i = [0]
        wres = {}
        slot_key = [None] * NS

        def get_w(key, loads):
            if key in wres:
                return wres[key]
            si = ring_i[0] % NS
            ring_i[0] += 1
            if slot_key[si] is not None:
                del wres[slot_key[si]]
            slot_key[si] = key
            wres[key] = si
            for dstf, src in loads:
                dst = dstf(ring[si])
                P.op("pool", lambda e, dst=dst, src=src: e.dma_start(out=dst, in_=src), writes=[RRING[si]], dma=DRING[si])
            return si

        def v3(kc, ncols, off=0):
            return lambda t: t[:, off:off + kc * ncols].rearrange("p (k c) -> p k c", k=kc)

        def rows(w2d):
            return w2d.rearrange("(k p) c -> p k c", p=128)

        dsm = {}
        P._dsm = dsm

        def dsem(name, out=False):
            if name not in dsm:
                dsm[name] = P.dma_sem(out=out)
            return dsm[name]

        def spdma(dst, src, reads, writes, name, out=False, nc_ok=False):
            if nc_ok:
                P.op("sp", lambda e: e.dma_start(out=dst, in_=src, allow_slow_non_contiguous=True), reads, writes, dma=dsem(name, out))
            else:
                P.op("sp", lambda e: e.dma_start(out=dst, in_=src), reads, writes, dma=dsem(name, out))

        def pooldma(dst, src, writes, name):
            P.op("pool", lambda e: e.dma_start(out=dst, in_=src), writes=writes, dma=dsem(name))

        ident = identf[:, :]
        identb = cstb[:, C_ID:C_ID + 128]
        trib = cstb[:, C_TRI:C_TRI + 128]
        onesb = cstb[:, C_ONE:C_ONE + 128]
        maskb = cstb[:, C_MASK:C_MASK + 512]
        tbd = cstb[:, C_TBD:C_TBD + 128]
        mbd = cstb[:, C_MBD:C_MBD + 128]
        invc = invcf[:, :].rearrange("p (j t) -> p j t", j=2)

        spdma(cstf[:, :], cst, [], [RCF], "cstf")
        P.act(cstb[:, :], cstf[:, 0:C_INV], AF.Copy, [RCF], [RC])
        P.act(identf[:, :], cstf[:, C_ID:C_ID + 128], AF.Copy, [RCF], [RC])
        P.act(invcf[:, :], cstf[:, C_INV:C_INV + 30], AF.Copy, [RCF], [RC])
        P.op("dve", lambda e: e.memset(epsb[:], EPS), writes=[RC])
        P.op("dve", lambda e: e.memset(lrT[:], 1.0), writes=[RLR])
        P.op("dve", lambda e: e.memset(wpbd[:], 0.0), writes=[RWP])
        for i, g in enumerate(gains + [mem_norm]):
            spdma(gall[:, i, 0:nl, :], g.rearrange("l (k p) -> p l k", p=128), [], [RC], "cst", nc_ok=True)
        spdma(gln[:, 0:nl, :], gla_norm.rearrange("l (h v) -> v l h", v=96), [], [RC], "cst", nc_ok=True)
        spdma(psc[:, 0:nl, :], pool_scale.rearrange("l (j p) -> p l j", p=128), [], [RC], "cst", nc_ok=True)

        def gain(i, l):
            return gall[:, i, l, :]

        def stats(src3, rsrc, lo, n, scale):
            b = bank()
            for k in range(8):
                P.mm(ps[b][:, 0:n], onesb, src3[:, k, lo:lo + n], k == 0, k == 7, [rsrc, RC], [RP[b]], k == 7)
            P.act(rstd[:, lo:lo + n], ps[b][:, 0:n], AF.Ln, [RP[b], RC], [RRS], scale=scale, bias=epsb[:, 0:1])
            P.act(rstd[:, lo:lo + n], rstd[:, lo:lo + n], AF.Exp, [RRS], [RRS], scale=-0.5)

        def norm_to_h(xv, rx, sbl, g):
            for lo, n in sbl:
                P.act(hT[:, :, lo:lo + n], xv[:, :, lo:lo + n], AF.Square, [rx], [RH])
                stats(hT, RH, lo, n, 1.0 / 1024)
                for k in range(8):
                    P.dve("scalar_tensor_tensor", [rx, RRS, RC], [RH], out=hT[:, k, lo:lo + n], in0=xv[:, k, lo:lo + n],
                          scalar=g[:, k:k + 1], in1=rstd[:, lo:lo + n], op0=ALU.mult, op1=ALU.mult)

        def post_norm_add(c, g):
            xv = xT[:, :, 512 * c:512 * c + CW]
            for lo, n in subs(c):
                stats(sqb, RSQ, lo, n, 1.0 / 1024)
                for k in range(8):
                    P.dve("tensor_tensor", [RST, RRS], [RST], out=stg[:, k, lo:lo + n], in0=stg[:, k, lo:lo + n], in1=rstd[:, lo:lo + n], op=ALU.mult)
                    P.dve("scalar_tensor_tensor", [RST, RC, RX[c]], [RX[c]], out=xv[:, k, lo:lo + n], in0=stg[:, k, lo:lo + n],
                          scalar=g[:, k:k + 1], in1=xv[:, k, lo:lo + n], op0=ALU.mult, op1=ALU.add)

        def proj_to_stg(c, key, wsrc2d, src3, rsrc):
            for hf in range(2):
                si = get_w((key, hf), [(v3(8, 512), rows(wsrc2d)[:, :, hf * 512:(hf + 1) * 512])])
                wv = v3(8, 512)(ring[si])
                for lo, n in subs(c):
                    for mm_ in range(4):
                        m = hf * 4 + mm_
                        b = bank()
                        for k in range(8):
                            P.mm(ps[b][:, 0:n], wv[:, k, mm_ * 128:(mm_ + 1) * 128], src3[:, k, lo:lo + n], k == 0, k == 7, [RRING[si], rsrc], [RP[b]], k == 7)
                        P.act(stg[:, m, lo:lo + n], ps[b][:, 0:n], AF.Copy, [RP[b]], [RST])
                        P.act(sqb[:, m, lo:lo + n], stg[:, m, lo:lo + n], AF.Square, [RST], [RSQ])

        for t in range(17):
            rn = 128 if t < 16 else 32
            src = xp[t * 128:(t + 1) * 128, :] if t < 16 else xs
            spdma(stg[:rn, 0:2, 0:512], src.rearrange("p (a b) -> p a b", a=2), [], [RST], "xin")
            c = min(t // 4, 3)
            col = t * 128
            for hb in range(2):
                b = bank()
                for j in range(4):
                    P.tr(ps[b][:, j * rn:(j + 1) * rn], stg[:rn, hb, j * 128:(j + 1) * 128], ident[:rn, :rn], [RST, RC], [RP[b]], j == 3)
                P.act(xT[:, hb * 4:(hb + 1) * 4, col:col + rn], ps[b][:, 0:4 * rn].rearrange("p (a b) -> p a b", a=4), AF.Copy, [RP[b]], [RX[c]])

        def layer(l):
            wr = rows(w_in[l])
            for t in range(2):
                spdma(stg[:, 0:2, 0:512], mem[t * 128:(t + 1) * 128, :].rearrange("p (a b) -> p a b", a=2), [], [RST], "xin")
                for hb in range(2):
                    b = bank()
                    for j in range(4):
                        P.tr(ps[b][:, j * 128:(j + 1) * 128], stg[:, hb, j * 128:(j + 1) * 128], ident, [RST, RC], [RP[b]], j == 3)
                    P.act(memT[:, hb * 4:(hb + 1) * 4, t * 128:(t + 1) * 128], ps[b][:, 0:512].rearrange("p (a b) -> p a b", a=4), AF.Copy, [RP[b]], [RMEM])
            ck("mem1")
            P.act(mnT[:, :, :], memT[:, :, :], AF.Square, [RMEM], [RMN])
            stats(mnT, RMN, 0, 256, 1.0 / 1024)
            for k in range(8):
                P.dve("scalar_tensor_tensor", [RMEM, RC, RRS], [RMN], out=mnT[:, k, :], in0=memT[:, k, :], scalar=gain(6, l)[:, k:k + 1],
                      in1=rstd[:, 0:256], op0=ALU.mult, op1=ALU.mult)
            ck("mem2")
            for which, wsrc, odram in ((0, w_xk, omk), (1, w_xv, omv)):
                if which == 1:
                    ck("mem3")
                for hf in range(2):
                    si = get_w(("xkv", which, hf), [(v3(8, 512), rows(wsrc[l])[:, :, hf * 512:(hf + 1) * 512])])
                    wv = v3(8, 512)(ring[si])
                    if which == 0:
                        for jj in range(4):
                            b = bank()
                            for k in range(8):
                                P.mm(ps[b][:, 0:256], wv[:, k, jj * 128:(jj + 1) * 128], mnT[:, k, :], k == 0, k == 7, [RRING[si], RMN], [RP[b]], k == 7)
                            P.act(mkT[:, hf * 4 + jj, :], ps[b][:, 0:256], AF.Copy, [RP[b]], [RMK])
                    for t in range(2):
                        b = bank()
                        for k in range(8):
                            P.mm(ps[b][:, :], mnT[:, k, t * 128:(t + 1) * 128], wv[:, k, :], k == 0, k == 7, [RRING[si], RMN], [RP[b]], k == 7)
                        P.act(stg[:, t, 0:512], ps[b][:, :], AF.Copy, [RP[b]], [RST])
                        if which == 1:
                            P.dve("tensor_copy", [RP[b]], [RMV], out=mvt[:, t, hf * 512:(hf + 1) * 512], in_=ps[b][:, :])
                    spdma(odram[l, :, hf * 512:(hf + 1) * 512].rearrange("(t p) f -> p t f", p=128), stg[:, 0:2, 0:512], [RST], [], "omem", out=True)
            ck("memkv")
            for c in range(4):
                xv = xT[:, :, 512 * c:512 * c + CW]
                norm_to_h(xv, RX[c], subs(c), gain(0, l))
                sk = get_w(("wk", l), [(v3(8, 384), wr[:, :, O_K:O_K + 384])])
                wk = v3(8, 384)(ring[sk])
                for lo, n in subs(c):
                    for j in range(3):
                        b = bank()
                        for k in range(8):
                            P.mm(ps[b][:, 0:n], wk[:, k, j * 128:(j + 1) * 128], hT[:, k, lo:lo + n], k == 0, k == 7, [RRING[sk], RH], [RP[b]], k == 7)
                        P.act(kT[:, j, 512 * c + lo:512 * c + lo + n], ps[b][:, 0:n], AF.Copy, [RP[b]], [RKT])
                sv_ = get_w(("wv", l), [(v3(8, 384), wr[:, :, O_V:O_V + 384])])
                wvv = v3(8, 384)(ring[sv_])
                tiles = [(t * 128, 128, 4 * c + t) for t in range(4)]
                if c == 3:
                    tiles += [(512, 16, 16), (528, 16, 17)]
                for lo, n, gt in tiles:
                    for which, wsl, rsl in ((0, wk, sk), (1, wvv, sv_)):
                        b = bank()
                        for k in range(8):
                            P.mm(ps[b][:n, 0:384], hT[:, k, lo:lo + n], wsl[:, k, :], k == 0, k == 7, [RRING[rsl], RH], [RP[b]], k == 7)
                        P.act(stg[:n, which, 0:384], ps[b][:n, 0:384], AF.Copy, [RP[b]], [RST])
                        if which == 1:
                            if gt < 16:
                                P.dve("tensor_copy", [RP[b]], [RVT], out=vtok[:, gt, :], in_=ps[b][:, 0:384])
                            else:
                                P.dve("tensor_copy", [RP[b]], [RVTS], out=vtoks[:, gt - 16, :], in_=ps[b][:16, 0:384])
                    if gt < 16:
                        spdma(okp[l, gt * 128:(gt + 1) * 128, :], stg[:, 0, 0:384], [RST], [], "okv", out=True)
                        spdma(ovp[l, gt * 128:(gt + 1) * 128, :], stg[:, 1, 0:384], [RST], [], "okv", out=True)
                    else:
                        bb = gt - 16
                        spdma(oks[l, bb * 16:(bb + 1) * 16, :], stg[:16, 0, 0:384], [RST], [], "okv", out=True)
                        spdma(ovs[l, bb * 16:(bb + 1) * 16, :], stg[:16, 1, 0:384], [RST], [], "okv", out=True)
            ck("prekv")
            for g4 in range(4):
                j, hh = g4 // 2, g4 % 2
                pooldma(wpbd[hh * 64:(hh + 1) * 64, j, hh * 64:(hh + 1) * 64], w_pool[l, g4], [RWP], "wp")
            pooldma(wa2[0:16, :], w_a2[l], [RWA2], "wa2")
            pooldma(wa2[16:17, :], b_a[l:l + 1, :], [RWA2], "wa2")
            P.op("dve", lambda e: e.memset(Sst[:], 0.0), writes=[RS])
            for c in range(4):
                chunk(l, c, wr)

        def chunk(l, c, wr):
            xv = xT[:, :, 512 * c:512 * c + CW]
            sbl = subs(c)
            norm_to_h(xv, RX[c], sbl, gain(0, l))
            si = get_w(("wq", l), [(v3(8, 384), wr[:, :, O_Q:O_Q + 384])])
            wq = v3(8, 384)(ring[si])
            for lo, n in sbl:
                for j in range(3):
                    b = bank()
                    for k in range(8):
                        P.mm(ps[b][:, 0:n], wq[:, k, j * 128:(j + 1) * 128], hT[:, k, lo:lo + n], k == 0, k == 7, [RRING[si], RH], [RP[b]], k == 7)
                    P.act(qT[:, j, lo:lo + n], ps[b][:, 0:n], AF.Copy, [RP[b]], [RQ])
            ck("q")
            rot = [0]
            nkt = 4 * c + 4
            for h in range(6):
                j, pb = h // 2, 64 * (h % 2)
                bo = bank(hold=True)
                for kt in range(nkt - 1, -1, -1):
                    i = kt - 4 * c
                    c0 = 128 * i if i > 0 else 0
                    ncl = 512 - c0
                    r = rot[0] % 2
                    rot[0] += 1
                    bz = bank()
                    P.mm(ps[bz][:, 0:ncl], kT[pb:pb + 64, j, kt * 128:(kt + 1) * 128], qT[pb:pb + 64, j, c0:512], True, True, [RKT, RQ], [RP[bz]], True)
                    P.act(ez[r][:, 0:ncl], ps[bz][:, 0:ncl], AF.Exp, [RP[bz]], [REZ[r]], scale=0.125)
                    if i >= 0:
                        P.dve("tensor_tensor", [REZ[r], RC], [REZ[r]], out=ez[r][:, 0:ncl], in0=ez[r][:, 0:ncl], in1=maskb[:, 0:ncl], op=ALU.mult)
                    P.act(spb[r][:, 0:ncl], ez[r][:, 0:ncl], AF.Ln, [REZ[r]], [RSP[r]], bias=1.0)
                    bc = bank()
                    lastk = (kt == nkt - 1)
                    P.mm(ps[bc][:, 0:ncl], trib, spb[r][:, 0:ncl], True, lastk, [RSP[r], RC], [RP[bc]], lastk)
                    if not lastk:
                        P.mm(ps[bc][:, 0:ncl], onesb, Rb[:, c0:512], False, True, [RR, RC], [RP[bc]], True)
                    P.act(enb[r][:, 0:ncl], ps[bc][:, 0:ncl], AF.Exp, [RP[bc]], [REN[r]], scale=-1.0)
                    P.dve("tensor_tensor", [REZ[r], REN[r]], [RWB[r]], out=wb[r][:, 0:ncl], in0=ez[r][:, 0:ncl], in1=enb[r][:, 0:ncl], op=ALU.mult)
                    if kt > 0:
                        if lastk:
                            P.op("dve", lambda e: e.memset(Rb[:, 0:384], 0.0), writes=[RR])
                            P.dve("tensor_copy", [RSP[r]], [RR], out=Rb[:, c0:512], in_=spb[r][:, 0:ncl])
                        else:
                            P.dve("tensor_tensor", [RSP[r], RR], [RR], out=Rb[:, c0:512], in0=Rb[:, c0:512], in1=spb[r][:, 0:ncl], op=ALU.add)
                    lv = vtok[:, kt, j * 128:(j + 1) * 128]
                    P.mm(ps[bo][:, c0:512], lv, wb[r][:, 0:ncl], kt == nkt - 1, kt == 0, [RVT, RWB[r]], [RP[bo]], kt == 0, sgc=True)
                P.act(oaT[pb:pb + 64, j, 0:512], ps[bo][pb:pb + 64, 0:512], AF.Copy, [RP[bo]], [ROA])
                held.discard(bo)
            ck("sb%d" % c)
            if c == 3:
                for b_ in range(2):
                    sb_sample(l, b_)
            ck("sbs%d" % c)
            pool_branch(l, c, wr)
            ck("pool%d" % c)
            for hf in range(2):
                gla_half(l, c, hf, wr)
            ck("gla%d" % c)
            merged = big[:, 0:8, :]
            for m in range(8):
                loads = []
                for br in range(3):
                    loads.append((lambda t, br=br: t[:, 0:3072].rearrange("p (k r c) -> p k r c", k=8, r=3)[:, :, br, :],
                                  wr[:, :, O_G + br * 1024 + m * 128:O_G + br * 1024 + (m + 1) * 128]))
                loads.append((lambda t: t[:, 3072:3456].rearrange("p (k c) -> p k c", k=3), rows(w_ba[l])[:, :, m * 128:(m + 1) * 128]))
                loads.append((lambda t: t[:, 3456:3712].rearrange("p (k c) -> p k c", k=2), rows(w_bb[l])[:, :, m * 128:(m + 1) * 128]))
                loads.append((lambda t: t[:96, 3712:4224].rearrange("p (k c) -> p k c", k=4), w_bc[l].rearrange("(h v) c -> v h c", v=96)[:, :, m * 128:(m + 1) * 128]))
                si = get_w(("mrg", l, m), loads)
                gv = ring[si][:, 0:3072].rearrange("p (k r c) -> p k r c", k=8, r=3)
                a_v = ring[si][:, 3072:3456].rearrange("p (k c) -> p k c", k=3)
                b_v = ring[si][:, 3456:3712].rearrange("p (k c) -> p k c", k=2)
                c_v = ring[si][:96, 3712:4224].rearrange("p (k c) -> p k c", k=4)
                for lo, n in sbl:
                    bg = [bank() for _ in range(3)]
                    bb = [bank() for _ in range(3)]
                    for br in range(3):
                        for k in range(8):
                            P.mm(ps[bg[br]][:, 0:n], gv[:, k, br, :], hT[:, k, lo:lo + n], k == 0, k == 7, [RRING[si], RH], [RP[bg[br]]], k == 7)
                    for k in range(3):
                        P.mm(ps[bb[0]][:, 0:n], a_v[:, k, :], oaT[:, k, lo:lo + n], k == 0, k == 2, [RRING[si], ROA], [RP[bb[0]]], k == 2)
                    for k in range(2):
                        P.mm(ps[bb[1]][:, 0:n], b_v[:, k, :], obT[:, k, lo:lo + n], k == 0, k == 1, [RRING[si], ROB], [RP[bb[1]]], k == 1)
                    for k in range(4):
                        P.mm(ps[bb[2]][:, 0:n], c_v[:, k, :], ocT[:, k, lo:lo + n], k == 0, k == 3, [RRING[si], ROC], [RP[bb[2]]], k == 3)
                    for br in range(3):
                        P.act(sg3[br][:, 0:n], ps[bg[br]][:, 0:n], AF.Sigmoid, [RP[bg[br]]], [RSG3[br]])
                    P.dve("tensor_tensor", [RSG3[0], RP[bb[0]]], [RMT1], out=mt1[:, 0:n], in0=sg3[0][:, 0:n], in1=ps[bb[0]][:, 0:n], op=ALU.mult)
                    P.dve("tensor_tensor", [RSG3[1], RP[bb[1]]], [RMT2], out=mt2[:, 0:n], in0=sg3[1][:, 0:n], in1=ps[bb[1]][:, 0:n], op=ALU.mult)
                    P.dve("tensor_tensor", [RMT1, RMT2], [RMT1], out=mt1[:, 0:n], in0=mt1[:, 0:n], in1=mt2[:, 0:n], op=ALU.add)
                    P.dve("tensor_tensor", [RSG3[2], RP[bb[2]]], [RMT2], out=mt2[:, 0:n], in0=sg3[2][:, 0:n], in1=ps[bb[2]][:, 0:n], op=ALU.mult)
                    P.dve("tensor_tensor", [RMT1, RMT2], [RBIG], out=merged[:, m, lo:lo + n], in0=mt1[:, 0:n], in1=mt2[:, 0:n], op=ALU.add)
            proj_to_stg(c, ("mix", l), w_mix[l], merged, RBIG)
            post_norm_add(c, gain(1, l))
            ck("mix%d" % c)
            xattn(l, c)
            ck("xattn%d" % c)
            norm_to_h(xv, RX[c], sbl, gain(4, l))
            hid = big
            wfr = rows(w_fi[l])
            wor = rows(w_fo[l])
            for jh in range(2):
                for j0, nj in ((0, 2), (2, 2), (4, 2), (6, 2), (8, 2), (10, 1)):
                    jg = jh * 11 + j0
                    si = get_w(("fi", l, jg), [
                        (lambda t, nj=nj: t[:, 0:16 * nj * 128].rearrange("p (k r c) -> p k r c", k=8, r=2)[:, :, 0, :], wfr[:, :, jg * 128:(jg + nj) * 128]),
                        (lambda t, nj=nj: t[:, 0:16 * nj * 128].rearrange("p (k r c) -> p k r c", k=8, r=2)[:, :, 1, :], wfr[:, :, 2816 + jg * 128:2816 + (jg + nj) * 128])])
                    wv = ring[si][:, 0:16 * nj * 128].rearrange("p (k r c) -> p k r c", k=8, r=2)
                    for jj in range(nj):
                        for lo, n in sbl:
                            b1, b2 = bank(), bank()
                            for k in range(8):
                                P.mm(ps[b1][:, 0:n], wv[:, k, 0, jj * 128:(jj + 1) * 128], hT[:, k, lo:lo + n], k == 0, k == 7, [RRING[si], RH], [RP[b1]], k == 7)
                            for k in range(8):
                                P.mm(ps[b2][:, 0:n], wv[:, k, 1, jj * 128:(jj + 1) * 128], hT[:, k, lo:lo + n], k == 0, k == 7, [RRING[si], RH], [RP[b2]], k == 7)
                            P.act(ft1[:, 0:n], ps[b1][:, 0:n], AF.Silu, [RP[b1]], [RFT1])
                            P.dve("tensor_tensor", [RFT1, RP[b2]], [RBIG], out=hid[:, j0 + jj, lo:lo + n], in0=ft1[:, 0:n], in1=ps[b2][:, 0:n], op=ALU.mult)
                for mb in range(4):
                    si = get_w(("fo", l, jh, mb), [(v3(11, 256), wor[:, jh * 11:(jh + 1) * 11, mb * 256:(mb + 1) * 256])])
                    wv = v3(11, 256)(ring[si])
                    for lo, n in sbl:
                        for mm_ in range(2):
                            m = mb * 2 + mm_
                            b = bank()
                            for k in range(11):
                                P.mm(ps[b][:, 0:n], wv[:, k, mm_ * 128:(mm_ + 1) * 128], hid[:, k, lo:lo + n], k == 0, k == 10, [RRING[si], RBIG], [RP[b]], k == 10)
                            if jh == 0:
                                P.act(stg[:, m, lo:lo + n], ps[b][:, 0:n], AF.Copy, [RP[b]], [RST])
                            else:
                                P.dve("tensor_tensor", [RST, RP[b]], [RST], out=stg[:, m, lo:lo + n], in0=stg[:, m, lo:lo + n], in1=ps[b][:, 0:n], op=ALU.add)
                                P.act(sqb[:, m, lo:lo + n], stg[:, m, lo:lo + n], AF.Square, [RST], [RSQ])
            post_norm_add(c, gain(5, l))
            ck("ffn%d" % c)

        def sb_sample(l, b_):
            pooldma(kcs[:, :, :], csk[l, b_].rearrange("(t p) f -> p t f", p=128), [RKCS], "kcs")
            pooldma(vcs[:, :, :], csv[l, b_].rearrange("(t p) f -> p t f", p=128), [RVCS], "vcs")
            for t in range(8):
                bk = bank()
                pv = ps[bk][:, :].bitcast(BF16)
                for j in range(3):
                    P.tr(pv[:, j * 128:(j + 1) * 128], kcs[:, t, j * 128:(j + 1) * 128], identb, [RKCS, RC], [RP[bk]], j == 2)
                P.act(kcsT[:, :, t * 128:(t + 1) * 128], pv[:, 0:384].rearrange("p (a b) -> p a b", a=3), AF.Copy, [RP[bk]], [RKCST])
            ck("ss1")
            bo = bank(hold=True)
            qc0 = 512 + 16 * b_
            NCOL = 96
            Rn = Rb[:16, 256:256 + NCOL]
            for kt in range(8, -1, -1):
                np_ = 16 if kt == 8 else 128
                bzs = [bank(), bank()]
                for par in range(2):
                    for hh in range(3):
                        h = 2 * hh + par
                        j, pb = h // 2, 64 * (h % 2)
                        qv = qT[pb:pb + 64, j, qc0:qc0 + 16]
                        if kt == 8:
                            kv = kT[pb:pb + 64, j, 2048 + 16 * b_:2048 + 16 * b_ + 16]
                        else:
                            kv = kcsT[pb:pb + 64, j, kt * 128:(kt + 1) * 128]
                        P.mm(ps[bzs[par]][:np_, hh * 16:hh * 16 + 16], kv, qv, True, True, [RKT, RKCST, RQ], [RP[bzs[par]]], hh == 2)
                if kt == 7:
                    ck("ss2")
                if kt == 8:
                    ck("ss3")
                r = 0
                for par in range(2):
                    P.act(ez[r][:np_, par * 48:par * 48 + 48], ps[bzs[par]][:np_, 0:48], AF.Exp, [RP[bzs[par]]], [REZ[r]], scale=0.125)
                if kt == 8:
                    for h in range(6):
                        P.dve("tensor_tensor", [REZ[r], RC], [REZ[r]], out=ez[r][:16, h * 16:h * 16 + 16], in0=ez[r][:16, h * 16:h * 16 + 16],
                              in1=maskb[:16, 0:16], op=ALU.mult)
                P.act(spb[r][:np_, 0:NCOL], ez[r][:np_, 0:NCOL], AF.Ln, [REZ[r]], [RSP[r]], bias=1.0)
                bc = bank()
                P.mm(ps[bc][:np_, 0:NCOL], trib[:np_, :np_], spb[r][:np_, 0:NCOL], True, kt == 8, [RSP[r], RC], [RP[bc]], kt == 8)
                if kt <= 7:
                    P.mm(ps[bc][:, 0:NCOL], onesb[:16, :], Rn, False, kt == 7, [RR, RC], [RP[bc]], kt == 7)
                if kt < 7:
                    P.mm(ps[bc][:, 0:NCOL], onesb, Rb[:, 0:NCOL], False, True, [RR, RC], [RP[bc]], True)
                ck("ss4")
                P.act(enb[r][:np_, 0:NCOL], ps[bc][:np_, 0:NCOL], AF.Exp, [RP[bc]], [REN[r]], scale=-1.0)
                P.dve("tensor_tensor", [REZ[r], REN[r]], [RWB[r]], out=wb[r][:np_, 0:NCOL], in0=ez[r][:np_, 0:NCOL], in1=enb[r][:np_, 0:NCOL], op=ALU.mult)
                if kt == 8:
                    P.dve("tensor_copy", [RSP[r]], [RR], out=Rn, in_=spb[r][:16, 0:NCOL])
                elif kt == 7:
                    P.dve("tensor_copy", [RSP[r]], [RR], out=Rb[:, 0:NCOL], in_=spb[r][:, 0:NCOL])
                elif kt > 0:
                    P.dve("tensor_tensor", [RSP[r], RR], [RR], out=Rb[:, 0:NCOL], in0=Rb[:, 0:NCOL], in1=spb[r][:, 0:NCOL], op=ALU.add)
                ck("ss5")
                for h in range(6):
                    j = h // 2
                    if kt == 8:
                        lv = vtoks[:16, b_, j * 128:(j + 1) * 128]
                    else:
                        lv = vcs[:, kt, j * 128:(j + 1) * 128]
                    hc = (h % 2) * 48 + (h // 2) * 16
                    P.mm(ps[bo][:, hc:hc + 16], lv, wb[r][:np_, hc:hc + 16], (kt == 8 and h == 0), kt == 0, [RVTS, RVCS, RWB[r]], [RP[bo]], (h == 5 and kt == 0), sgc=True)
            for h in range(6):
                j, pb = h // 2, 64 * (h % 2)
                hc = (h % 2) * 48 + (h // 2) * 16
                P.act(oaT[pb:pb + 64, j, qc0:qc0 + 16], ps[bo][pb:pb + 64, hc:hc + 16], AF.Copy, [RP[bo]], [ROA])
            held.discard(bo)

        def pool_branch(l, c, wr):
            si = get_w(("wu", l), [(v3(8, 256), wr[:, :, O_U:O_U + 256])])
            wu = v3(8, 256)(ring[si])
            W = 527 if c < 3 else PW
            if c > 0:
                P.act(uT[:, :, 0:15], uhalo[:, :, :], AF.Copy, [RUH], [RU])
            else:
                P.op("dve", lambda e: e.memset(uT[:, :, 0:15], 0.0), writes=[RU])
            segs = [(0, 512, 15)]
            if c == 3:
                segs += [(512, 16, 527 + 15), (528, 16, 527 + 31 + 15)]
                for b_ in range(2):
                    off = 527 + 31 * b_
                    for j in range(2):
                        spdma(uT[:, j, off:off + 15], spool[l, b_, :, j * 128:(j + 1) * 128].rearrange("t p -> p t"), [], [RU], "hist", nc_ok=True)
            for lo, n, dst in segs:
                for j in range(2):
                    b = bank()
                    for k in range(8):
                        P.mm(ps[b][:, 0:n], wu[:, k, j * 128:(j + 1) * 128], hT[:, k, lo:lo + n], k == 0, k == 7, [RRING[si], RH], [RP[b]], k == 7)
                    P.act(uT[:, j, dst:dst + n], ps[b][:, 0:n], AF.Copy, [RP[b]], [RU])
            P.act(uhalo[:, :, :], uT[:, :, 512:527], AF.Copy, [RU], [RUH])
            if c == 3:
                outs = [(512 - 15, opp[l]), (513, ops_[l, 0]), (529, ops_[l, 1])]
                for lo, od in outs:
                    b = bank()
                    for k in range(8):
                        P.mm(ps[b][:15, 0:256], hT[:, k, lo:lo + 15], wu[:, k, :], k == 0, k == 7, [RRING[si], RH], [RP[b]], k == 7)
                    P.act(pD[:15, 0:256], ps[b][:15, 0:256], AF.Copy, [RP[b]], [RPD])
                    spdma(od, pD[:15, 0:256], [RPD], [], "opool", out=True)
            P.dve("tensor_tensor", [RU], [RPA], out=pA[:, :, 1:W], in0=uT[:, :, 1:W], in1=uT[:, :, 0:W - 1], op=ALU.add)
            P.dve("tensor_tensor", [RPA], [RPB], out=pB[:, :, 3:W], in0=pA[:, :, 3:W], in1=pA[:, :, 1:W - 2], op=ALU.add)
            P.dve("tensor_tensor", [RPB], [RPC], out=pC[:, 7:W], in0=pB[:, 1, 7:W], in1=pB[:, 1, 3:W - 4], op=ALU.add)
            P.dve("tensor_tensor", [RPC], [RPD], out=pD[:, 15:W], in0=pC[:, 15:W], in1=pC[:, 7:W - 8], op=ALU.add)
            sel = [(0, 64, 0, pA[0:64, 0, :], 0.5, RPA), (64, 128, 0, pB[64:128, 0, :], 0.25, RPB), (0, 64, 1, pC[0:64, :], 0.125, RPC), (64, 128, 1, pD[64:128, :], 0.0625, RPD)]
            for p0, p1, j, sv, iw, rsv in sel:
                for lo, n, src in segs:
                    P.dve("scalar_tensor_tensor", [rsv, RU], [RPL], out=pooled[p0:p1, j, lo:lo + n], in0=sv[:, src:src + n], scalar=iw,
                          in1=uT[p0:p1, j, src:src + n], op0=ALU.mult, op1=ALU.subtract)
                if c == 0:
                    P.dve("tensor_tensor", [rsv, RC], [RPT_], out=ptmp[p0:p1, 0:15], in0=sv[:, 15:30], in1=invc[p0:p1, j, :], op=ALU.mult)
                    P.dve("tensor_tensor", [RPT_, RU], [RPL], out=pooled[p0:p1, j, 0:15], in0=ptmp[p0:p1, 0:15], in1=uT[p0:p1, j, 15:30], op=ALU.subtract)
            for lo, n in subs(c):
                for j in range(2):
                    b = bank()
                    P.mm(ps[b][:, 0:n], wpbd[:, j, :], pooled[:, j, lo:lo + n], True, True, [RWP, RPL], [RP[b]], True)
                    P.act(obT[:, j, lo:lo + n], ps[b][:, 0:n], AF.Copy, [RP[b], RC], [ROB], scale=psc[:, l, j:j + 1])

        def gla_half(l, c, hf, wr):
            base = 256 * hf
            sbl = [(base, 256, 0)]
            tiles = [(base, 128, 0, 0), (base + 128, 128, 1, 128)]
            samp = (c == 3 and hf == 1)
            if samp:
                sbl += [(512, 32, 256)]
                tiles += [(512, 16, 2, 256), (528, 16, 3, 272)]
            ncl = 288 if samp else 256

            def wG():
                return get_w(("gG", l), [(v3(8, 400), wr[:, :, O_GC:O_GC + 400])])

            def wK():
                return get_w(("gK", l), [(v3(8, 384), wr[:, :, O_KC:O_KC + 384])])

            def wV():
                return get_w(("gV", l), [(v3(8, 384), wr[:, :, O_VC:O_VC + 384])])

            def wQ():
                return get_w(("gQ", l), [(v3(8, 384), wr[:, :, O_QC:O_QC + 384])])

            s = wG(); w = v3(8, 400)(ring[s])
            for lo, n, lc in sbl:
                b = bank()
                for k in range(8):
                    P.mm(ps[b][:16, 0:n], w[:, k, 384:400], hT[:, k, lo:lo + n], k == 0, k == 7, [RRING[s], RH], [RP[b]], k == 7)
                P.act(lrT[:16, lo:lo + n], ps[b][:16, 0:n], AF.Copy, [RP[b]], [RLR])
            ck("g1")
            for lo, n, ti, lc in tiles:
                sm = ti >= 2
                bi = ti - 2
                vc_d = vcts[:, bi, :] if sm else vct[:, ti, :]
                la_d = lats[:, bi, :] if sm else lat[:, ti, :]
                kt_d = ktts[:, bi, :] if sm else ktt[:, ti, :]
                b = bank()
                P.mm(ps[b][:n, 0:384], lrT[0:17, lo:lo + n], wa2[0:17, :], True, True, [RLR, RWA2], [RP[b]], True)
                P.act(gt1[:n, 0:384], ps[b][:n, 0:384], AF.Exp, [RP[b]], [RGT1], scale=-1.0)
                P.act(la_d, gt1[:n, 0:384], AF.Ln, [RGT1], [RLAS if sm else RLA], bias=1.0)
                rla = RLAS if sm else RLA
                b = bank()
                P.mm(ps[b][:n, 0:384], tbd[:n, :n], la_d, True, True, [rla, RC], [RP[b]], True)
                P.act(ekt[:n, :], ps[b][:n, 0:384], AF.Exp, [RP[b]], [REKT])
                s = wK(); w = v3(8, 384)(ring[s])
                b = bank()
                for k in range(8):
                    P.mm(ps[b][:n, 0:384], hT[:, k, lo:lo + n], w[:, k, :], k == 0, k == 7, [RRING[s], RH], [RP[b]], k == 7)
                P.dve("tensor_tensor", [REKT, RP[b]], [RKTTS if sm else RKTT], out=kt_d, in0=ps[b][:n, 0:384], in1=ekt[:n, :], op=ALU.mult)
                s = wV(); w = v3(8, 384)(ring[s])
                b = bank()
                for k in range(8):
                    P.mm(ps[b][:n, 0:384], hT[:, k, lo:lo + n], w[:, k, :], k == 0, k == 7, [RRING[s], RH], [RP[b]], k == 7)
                P.act(vc_d, ps[b][:n, 0:384], AF.Copy, [RP[b]], [RVCTS if sm else RVCT])
                b = bank()
                for h in range(4):
                    P.mm(ps[b][:96, h * 128:h * 128 + n], la_d[:, h * 96:(h + 1) * 96], tbd[:n, :n], True, True, [rla, RC], [RP[b]], h == 3)
                pvw = ps[b][:96, :].rearrange("p (h t) -> p h t", h=4)[:, :, 0:n]
                P.act(eq[:, :, lc:lc + n], pvw, AF.Exp, [RP[b]], [REQ], scale=-1.0)
                P.act(eqi[:, :, lc:lc + n], pvw, AF.Exp, [RP[b]], [REQI])
            ck("g2")
            for lo, n, lc in sbl:
                for h in range(4):
                    s = wQ(); w = v3(8, 384)(ring[s])
                    b = bank()
                    for k in range(8):
                        P.mm(ps[b][:96, 0:n], w[:, k, h * 96:(h + 1) * 96], hT[:, k, lo:lo + n], k == 0, k == 7, [RRING[s], RH], [RP[b]], k == 7)
                    P.dve("scalar_tensor_tensor", [RP[b], REQ], [RQT], out=qtT[:, h, lc:lc + n], in0=ps[b][:96, 0:n], scalar=96.0 ** -0.5, in1=eq[:, h, lc:lc + n], op0=ALU.mult, op1=ALU.mult)
                    s = wK(); w = v3(8, 384)(ring[s])
                    b = bank()
                    for k in range(8):
                        P.mm(ps[b][:96, 0:n], w[:, k, h * 96:(h + 1) * 96], hT[:, k, lo:lo + n], k == 0, k == 7, [RRING[s], RH], [RP[b]], k == 7)
                    P.dve("tensor_tensor", [RP[b], REQI], [RKTT_], out=ktT[:, h, lc:lc + n], in0=ps[b][:96, 0:n], in1=eqi[:, h, lc:lc + n], op=ALU.mult)
                    s = wG(); w = v3(8, 400)(ring[s])
                    b = bank()
                    for k in range(8):
                        P.mm(ps[b][:96, 0:n], w[:, k, h * 96:(h + 1) * 96], hT[:, k, lo:lo + n], k == 0, k == 7, [RRING[s], RH], [RP[b]], k == 7)
                    P.act(sgc[:, h, lc:lc + n], ps[b][:96, 0:n], AF.Silu, [RP[b]], [RSG])
            ck("g3")
            P.act(ebl[:, :, 0:4], eq[:, :, 63:256:64], AF.Copy, [REQ], [REBL])
            if samp:
                P.act(ebl[:, :, 4:6], eq[:, :, 271:288:16], AF.Copy, [REQ], [REBL])
            for g in range(4):
                ti, h2 = g // 2, g % 2
                P.act(Sbf[:, g], Sst[:], AF.Copy, [RS], [RSBF])
                b = bank()
                for h in range(4):
                    P.mm(ps[b][:96, h * 96:(h + 1) * 96], ktt[h2 * 64:(h2 + 1) * 64, ti, h * 96:(h + 1) * 96], vct[h2 * 64:(h2 + 1) * 64, ti, h * 96:(h + 1) * 96],
                         True, True, [RKTT, RVCT], [RP[b]], h == 3)
                P.dve("tensor_tensor", [RS, RP[b]], [RSTM], out=Stm[:, :, :], in0=Sst[:], in1=ps[b][:96, 0:384].rearrange("p (h v) -> p h v", h=4), op=ALU.add)
                for h in range(4):
                    P.dve("tensor_scalar", [RSTM, REBL], [RS], out=Sst[:, h, :], in0=Stm[:, h, :], scalar1=ebl[:, h, g:g + 1], scalar2=None, op0=ALU.mult)
            if samp:
                spdma(ogp[l].rearrange("h k v -> k h v"), Sst[:], [RS], [], "ogp", out=True)
                for b_ in range(2):
                    spdma(Ss[:, :, :], sgla[l, b_].rearrange("h k v -> k h v"), [], [RSS], "sgla")
                    P.act(Sbf[:, 4 + b_], Ss[:, :, :], AF.Copy, [RSS], [RSBF])
                    b = bank()
                    for h in range(4):
                        P.mm(ps[b][:96, h * 96:(h + 1) * 96], ktts[:, b_, h * 96:(h + 1) * 96], vcts[:, b_, h * 96:(h + 1) * 96], True, True, [RKTTS, RVCTS], [RP[b]], h == 3)
                    P.dve("tensor_tensor", [RSS, RP[b]], [RSTM], out=Stm[:, :, :], in0=Ss[:, :, :], in1=ps[b][:96, 0:384].rearrange("p (h v) -> p h v", h=4), op=ALU.add)
                    for h in range(4):
                        P.dve("tensor_scalar", [RSTM, REBL], [RSS], out=Ss[:, h, :], in0=Stm[:, h, :], scalar1=ebl[:, h, 4 + b_:5 + b_], scalar2=None, op0=ALU.mult)
                    spdma(ogs[l, b_].rearrange("h k v -> k h v"), Ss[:, :, :], [RSS], [], "ogs", out=True)
            ck("g4")
            for lo, n, ti, lc in tiles:
                sm = ti >= 2
                bi = ti - 2
                vc_d = vcts[:, bi, :] if sm else vct[:, ti, :]
                rvc = RVCTS if sm else RVCT
                b = bank()
                for h in range(4):
                    P.mm(ps[b][:n, h * 128:h * 128 + n], ktT[:, h, lc:lc + n], qtT[:, h, lc:lc + n], True, True, [RKTT_, RQT], [RP[b]], h == 3)
                for h in range(4):
                    P.dve("tensor_tensor", [RP[b], RC], [RATT], out=attb[:n, h, 0:n], in0=ps[b][:n, h * 128:h * 128 + n], in1=mbd[:n, 0:n], op=ALU.mult)
                ck("g5")
                bo = bank()
                for h in range(4):
                    P.mm(ps[bo][:96, h * 128:h * 128 + n], vc_d[:, h * 96:(h + 1) * 96], attb[:n, h, 0:n], True, False, [rvc, RATT], [RP[bo]], False)
                    if sm:
                        P.mm(ps[bo][:96, h * 128:h * 128 + n], Sbf[:, 4 + bi, h, :], qtT[:, h, lc:lc + n], False, True, [RSBF, RQT], [RP[bo]], h == 3)
                    else:
                        for h2 in range(2):
                            g = ti * 2 + h2
                            P.mm(ps[bo][:96, h * 128 + h2 * 64:h * 128 + h2 * 64 + 64], Sbf[:, g, h, :], qtT[:, h, lc + h2 * 64:lc + h2 * 64 + 64], False, h2 == 1,
                                 [RSBF, RQT], [RP[bo]], h == 3 and h2 == 1)
                ck("g6")
                ov = ps[bo][:96, :].rearrange("p (h t) -> p h t", h=4)[:, :, 0:n]
                P.act(osb[:, :, 0:n], ov, AF.Copy, [RP[bo]], [ROS])
                P.act(osq[:, :, 0:n], osb[:, :, 0:n], AF.Square, [ROS], [ROSQ])
                bs = bank()
                for h in range(4):
                    P.mm(ps[bs][:96, h * 128:h * 128 + n], onesb[:96, :96], osq[:, h, 0:n], True, True, [ROSQ, RC], [RP[bs]], h == 3)
                sv = ps[bs][:96, :].rearrange("p (h t) -> p h t", h=4)[:, :, 0:n]
                gv = grs[:, :].rearrange("p (h t) -> p h t", h=4)[:, :, 0:n]
                P.act(gv, sv, AF.Ln, [RP[bs], RC], [RGRS], scale=1.0 / 96, bias=epsb[:96, 0:1])
                P.act(gv, gv, AF.Exp, [RGRS], [RGRS], scale=-0.5)
                ck("g7")
                P.dve("tensor_tensor", [ROS, RGRS], [ROS], out=osb[:, :, 0:n], in0=osb[:, :, 0:n], in1=gv, op=ALU.mult)
                for h in range(4):
                    P.dve("scalar_tensor_tensor", [ROS, RC, RSG], [ROC], out=ocT[:, h, lo:lo + n], in0=osb[:, h, 0:n], scalar=gln[:, l, h:h + 1],
                          in1=sgc[:, h, lc:lc + n], op0=ALU.mult, op1=ALU.mult)

        def xattn(l, c):
            xv = xT[:, :, 512 * c:512 * c + CW]
            norm_to_h(xv, RX[c], subs(c), gain(2, l))
            qx = big[:, 0:8, :]
            for hf in range(2):
                si = get_w(("xq", l, hf), [(v3(8, 512), rows(w_xq[l])[:, :, hf * 512:(hf + 1) * 512])])
                wv = v3(8, 512)(ring[si])
                for lo, n in subs(c):
                    for mm_ in range(4):
                        b = bank()
                        for k in range(8):
                            P.mm(ps[b][:, 0:n], wv[:, k, mm_ * 128:(mm_ + 1) * 128], hT[:, k, lo:lo + n], k == 0, k == 7, [RRING[si], RH], [RP[b]], k == 7)
                        P.act(qx[:, hf * 4 + mm_, lo:lo + n], ps[b][:, 0:n], AF.Copy, [RP[b]], [RBIG])
            ox = hT
            blocks = [(0, 512, None)]
            if c == 3:
                blocks += [(512, 16, 0), (528, 16, 1)]
            for lo, n, sb_ in blocks:
                if sb_ is None:
                    mk_, mv_, rmk, rmv = mkT, mvt, RMK, RMV
                else:
                    pooldma(mkl[:, :, :], cmk[l, sb_].rearrange("(t p) f -> p t f", p=128), [RMKL], "mkl")
                    pooldma(mvs[:, :, :], cmv[l, sb_].rearrange("(t p) f -> p t f", p=128), [RMVS], "mvs")
                    for t in range(2):
                        for hb in range(2):
                            bk = bank()
                            pv = ps[bk][:, :].bitcast(BF16)
                            for j in range(4):
                                d = hb * 4 + j
                                P.tr(pv[:, j * 128:(j + 1) * 128], mkl[:, t, d * 128:(d + 1) * 128], identb, [RMKL, RC], [RP[bk]], j == 3)
                            P.act(mkTs[:, hb * 4:(hb + 1) * 4, t * 128:(t + 1) * 128], pv[:, 0:512].rearrange("p (a b) -> p a b", a=4), AF.Copy, [RP[bk]], [RMKTS])
                    mk_, mv_, rmk, rmv = mkTs, mvs, RMKTS, RMVS
                for h in range(4):
                    for mt_ in range(2):
                        b = bank()
                        for dd in range(2):
                            P.mm(ps[b][:, 0:n], mk_[:, 2 * h + dd, mt_ * 128:(mt_ + 1) * 128], qx[:, 2 * h + dd, lo:lo + n], dd == 0, dd == 1, [rmk, RBIG], [RP[b]], dd == 1)
                        P.act(pTb[:, mt_, 0:n], ps[b][:, 0:n], AF.Exp, [RP[b]], [RPT], scale=1.0 / 16)
                    bd = bank()
                    for mt_ in range(2):
                        P.mm(ps[bd][:, 0:n], onesb, pTb[:, mt_, 0:n], mt_ == 0, mt_ == 1, [RPT, RC], [RP[bd]], mt_ == 1)
                    P.act(rden[:, 0:n], ps[bd][:, 0:n], AF.Ln, [RP[bd]], [RRD])
                    P.act(rden[:, 0:n], rden[:, 0:n], AF.Exp, [RRD], [RRD], scale=-1.0)
                    for dd in range(2):
                        b = bank()
                        for mt_ in range(2):
                            P.mm(ps[b][:, 0:n], mv_[:, mt_, (2 * h + dd) * 128:(2 * h + dd + 1) * 128], pTb[:, mt_, 0:n], mt_ == 0, mt_ == 1, [rmv, RPT], [RP[b]], mt_ == 1)
                        P.dve("tensor_tensor", [RP[b], RRD], [RH], out=ox[:, 2 * h + dd, lo:lo + n], in0=ps[b][:, 0:n], in1=rden[:, 0:n], op=ALU.mult)
            proj_to_stg(c, ("xo", l), w_xo[l], ox, RH)
            post_norm_add(c, gain(3, l))

        try:
            ck("setup")
            for l in range(nl):
                layer(l)
        except StopBuild as ex:
            print("STOPPED at", ex)

        for t in range(17 if STOP[0] is None else 0):
            rn = 128 if t < 16 else 32
            c = min(t // 4, 3)
            col = t * 128
            for hb in range(2):
                b = bank()
                for j in range(4):
                    k = hb * 4 + j
                    P.tr(ps[b][:rn, j * 128:(j + 1) * 128], xT[:, k, col:col + rn], ident, [RX[c], RC], [RP[b]], j == 3)
                P.act(stg[:rn, hb, 0:512], ps[b][:rn, :], AF.Copy, [RP[b]], [RST])
            dst = yp[t * 128:(t + 1) * 128, :] if t < 16 else ys
            spdma(dst.rearrange("p (a b) -> p a b", a=2), stg[:rn, 0:2, 0:512], [RST], [], "yout", out=True)
        P.emit()
    return nc


def _consts():
    c = np.zeros((128, C_END), np.float32)
    p = np.arange(128)[:, None]
    q = np.arange(128)[None, :]
    c[:, C_ID:C_ID + 128] = (p == q)
    c[:, C_TRI:C_TRI + 128] = (p >= q)
    c[:, C_ONE:C_ONE + 128] = 1.0
    c[:, C_MASK:C_MASK + 512] = (p < np.arange(512)[None, :])
    same = (p // 64) == (q // 64)
    c[:, C_TBD:C_TBD + 128] = np.where((p <= q) & same, 1.0 / 16.0, 0.0)
    c[:, C_MBD:C_MBD + 128] = ((p <= q) & same)
    inv = np.zeros((128, 2, 15), np.float32)
    for pp in range(128):
        for j in range(2):
            w = 2 ** (2 * j + (1 if pp >= 64 else 0) + 1)
            inv[pp, j] = 1.0 / np.minimum(np.arange(15) + 1, w)
    c[:, C_INV:C_INV + 30] = inv.reshape(128, 30)
    return c


_NC_CACHE = {}


def kernel(x_prompt, x_sample, mem_prompt, cache_sb_k, cache_sb_v, state_pool, state_gla,
           cache_mem_k, cache_mem_v, w_in, w_gla_a2, b_gla_a, gla_norm, w_pool, pool_scale,
           w_branch_a, w_branch_b, w_branch_c, w_mix_out, mem_norm, w_xq, w_xk, w_xv, w_xo,
           w_ffn_in, w_ffn_out, norm_mix_pre, norm_mix_post, norm_x_pre, norm_x_post,
           norm_ffn_pre, norm_ffn_post, _nl=NL):
    f = lambda a: np.ascontiguousarray(np.asarray(a, dtype=np.float32))
    fl = lambda a: np.ascontiguousarray(np.asarray(a, dtype=np.float32)[:_nl])
    if _nl not in _NC_CACHE:
        _NC_CACHE[_nl] = build(_nl)
    nc = _NC_CACHE[_nl]
    shared = dict(w_in=fl(w_in), w_gla_a2=fl(w_gla_a2), b_gla_a=fl(b_gla_a), gla_norm=fl(gla_norm), w_pool=fl(w_pool),
                  pool_scale=fl(pool_scale), w_branch_a=fl(w_branch_a), w_branch_b=fl(w_branch_b), w_branch_c=fl(w_branch_c),
                  w_mix_out=fl(w_mix_out), mem_norm=fl(mem_norm), w_xq=fl(w_xq), w_xk=fl(w_xk), w_xv=fl(w_xv), w_xo=fl(w_xo),
                  w_ffn_in=fl(w_ffn_in), w_ffn_out=fl(w_ffn_out), norm_mix_pre=fl(norm_mix_pre), norm_mix_post=fl(norm_mix_post),
                  norm_x_pre=fl(norm_x_pre), norm_x_post=fl(norm_x_post), norm_ffn_pre=fl(norm_ffn_pre), norm_ffn_post=fl(norm_ffn_post),
                  cst=_consts())
    x_prompt = f(x_prompt); x_sample = f(x_sample); mem_prompt = f(mem_prompt)
    cache_sb_k = fl(cache_sb_k); cache_sb_v = fl(cache_sb_v); state_pool = fl(state_pool); state_gla = fl(state_gla)
    cache_mem_k = fl(cache_mem_k); cache_mem_v = fl(cache_mem_v)
    in_maps = []
    for i in range(8):
        s2 = slice(2 * i, 2 * i + 2)
        d = dict(shared)
        d.update(xp=x_prompt[i], xs=np.ascontiguousarray(x_sample[s2].reshape(32, 1024)), mem=mem_prompt[i],
                 csk=np.ascontiguousarray(cache_sb_k[:, s2].reshape(_nl, 2, 1024, 384)),
                 csv=np.ascontiguousarray(cache_sb_v[:, s2].reshape(_nl, 2, 1024, 384)),
                 spool=np.ascontiguousarray(state_pool[:, s2]), sgla=np.ascontiguousarray(state_gla[:, s2]),
                 cmk=np.ascontiguousarray(cache_mem_k[:, s2].reshape(_nl, 2, 256, 1024)),
                 cmv=np.ascontiguousarray(cache_mem_v[:, s2].reshape(_nl, 2, 256, 1024)))
        in_maps.append(d)
    res = run_bass_kernel_spmd(nc, in_maps, core_ids=list(range(8)))
    R = res.results
    def cat(k, ax=0):
        a = np.stack([r[k] for r in R], axis=ax)
        if ax == 1 and a.shape[0] < 4:
            a = np.concatenate([a, np.zeros((4 - a.shape[0],) + a.shape[1:], a.dtype)], axis=0)
        return a
    y_p = cat("yp")
    y_s = cat("ys").reshape(16, 16, 1024)
    kp = cat("okp", 1).reshape(4, 8, 2048, 6, 64)
    vp = cat("ovp", 1).reshape(4, 8, 2048, 6, 64)
    pp = cat("opp", 1)
    gp = cat("ogp", 1)
    mk = cat("omk", 1).reshape(4, 8, 256, 4, 256)
    mv = cat("omv", 1).reshape(4, 8, 256, 4, 256)
    ks = cat("oks", 1).reshape(4, 16, 16, 6, 64)
    vs = cat("ovs", 1).reshape(4, 16, 16, 6, 64)
    pls = cat("ops", 1).reshape(4, 16, 15, 256)
    gs = cat("ogs", 1).reshape(4, 16, 4, 96, 96)
    return (y_p, y_s, kp, vp, pp, gp, mk, mv, ks, vs, pls, gs)
```

```python
import numpy as np
from contextlib import ExitStack
import concourse.bass as bass
import concourse.mybir as mybir
from concourse.bass_utils import run_bass_kernel_spmd

F32 = mybir.dt.float32
BF16 = mybir.dt.bfloat16
AF = mybir.ActivationFunctionType
ALU = mybir.AluOpType

NL = 4
EPS = 1e-6
NT = 2080
CW = 544
O_Q, O_K, O_V, O_U = 0, 384, 768, 1152
O_QC, O_KC, O_VC, O_GC, O_LR, O_G = 1408, 1792, 2176, 2560, 2944, 2960
C_ID, C_TRI, C_ONE, C_MASK, C_TBD, C_MBD, C_INV, C_END = 0, 128, 256, 384, 896, 1024, 1152, 1182


class Res:
    __slots__ = ("w", "r", "name")

    def __init__(self, name=""):
        self.w = None
        self.r = {}
        self.name = name


class PRes(Res):
    __slots__ = ()


class ARes(Res):
    __slots__ = ("lo", "hi")

    def __init__(self, name, lo, hi):
        Res.__init__(self, name)
        self.lo = lo
        self.hi = hi


class DmaSem:
    def __init__(self, sem):
        self.sem = sem
        self.count = 0


class Prog:
    def __init__(self, nc, es):
        self.nc = nc
        self.es = es
        self.engs = {"sp": nc.sync, "act": nc.scalar, "pool": nc.gpsimd, "dve": nc.vector, "pe": nc.tensor}
        self.q = {e: [] for e in self.engs}
        self.sem = {e: es.enter_context(nc.semaphore("S_" + e)) for e in ("act", "pool", "dve", "pe")}
        self.cnt = {e: 0 for e in self.sem}
        self.seen = {e: {} for e in self.engs}
        self.out_sems = []
        self.nsem = 0
        self.arena = []

    def dma_sem(self, out=False):
        self.nsem += 1
        s = DmaSem(self.es.enter_context(self.nc.semaphore("D%d" % self.nsem)))
        if out:
            self.out_sems.append(s)
        return s

    def op(self, eng, fn, reads=(), writes=(), inc=True, dma=None):
        waits = {}
        own = self.sem.get(eng)

        def need(ev):
            if ev is None:
                return
            s, v = ev
            if s is own and dma is None and (eng == "pe" or v > self.cnt[eng]):
                return
            k = id(s)
            if k not in waits or waits[k][1] < v:
                waits[k] = (s, v)

        pr = [r for r in reads if isinstance(r, PRes)]
        if pr:
            reads = [r for r in reads if not isinstance(r, PRes)]
            writes = list(writes) + pr
        for r in reads:
            need(r.w)
        for w in writes:
            need(w.w)
            for ev in w.r.values():
                need(ev)
            if isinstance(w, ARes):
                for o in self.arena:
                    if o is not w and o.lo < w.hi and w.lo < o.hi:
                        need(o.w)
                        for ev in o.r.values():
                            need(ev)
        wl = []
        for k, (s, v) in waits.items():
            if self.seen[eng].get(k, 0) < v:
                self.seen[eng][k] = v
                wl.append((s, v))
        if dma is not None:
            dma.count += 16
            ev = (dma.sem, dma.count)
            incv = 16
        else:
            if inc:
                self.cnt[eng] += 1
                ev = (self.sem[eng], self.cnt[eng])
            else:
                ev = (self.sem[eng], self.cnt[eng] + 1)
            incv = 1
        for r in reads:
            k = id(ev[0])
            if k not in r.r or r.r[k][1] < ev[1]:
                r.r[k] = ev
        for w in writes:
            w.w = ev
            w.r = {}
        self.q[eng].append((wl, fn, (ev[0], incv) if (inc or dma is not None) else None))

    def mm(self, out, lhsT, rhs, start, stop, reads, writes, inc, sgc=False):
        if sgc:
            self.op("pe", lambda e: e.matmul(out, lhsT=lhsT, rhs=rhs, start=start, stop=stop, skip_group_check=True), reads, writes, inc)
        else:
            self.op("pe", lambda e: e.matmul(out, lhsT=lhsT, rhs=rhs, start=start, stop=stop), reads, writes, inc)

    def tr(self, out, in_, ident, reads, writes, inc):
        self.op("pe", lambda e: e.transpose(out, in_, ident), reads, writes, inc)

    def act(self, out, in_, func, reads, writes, **kw):
        self.op("act", lambda e: e.activation(out=out, in_=in_, func=func, **kw), reads, writes)

    def dve(self, name, reads, writes, **kw):
        self.op("dve", lambda e: getattr(e, name)(**kw), reads, writes)

    def emit(self):
        nc = self.nc
        finals = [(s.sem, s.count) for s in self.out_sems if s.count > 0]
        with nc.Block() as block:
            def run(name):
                def f(eng):
                    for wl, fn, inc in self.q[name]:
                        for s, v in wl:
                            eng.wait_ge(s, v)
                        ins = fn(eng)
                        if inc is not None:
                            ins.then_inc(inc[0], inc[1])
                    if name == "sp":
                        for s, v in finals:
                            eng.wait_ge(s, v)
                return f

            block.sync(run("sp"))
            block.scalar(run("act"))
            block.gpsimd(run("pool"))
            block.vector(run("dve"))
            block.tensor(run("pe"))


def subs(c):
    return [(0, 512)] + ([(512, 32)] if c == 3 else [])


_LASTP = [None]


class StopBuild(Exception):
    pass


STOP = [None]


def ck(name):
    if STOP[0] == name:
        raise StopBuild(name)


def build(nl=NL):
    nc = bass.Bass("TRN2", target_bir_lowering=False)
    es = ExitStack()

    def din(name, shape):
        return nc.dram_tensor(name, list(shape), F32, kind="ExternalInput").ap()

    def dout(name, shape):
        return nc.dram_tensor(name, list(shape), F32, kind="ExternalOutput").ap()

    xp = din("xp", [2048, 1024]); xs = din("xs", [32, 1024]); mem = din("mem", [256, 1024])
    csk = din("csk", [nl, 2, 1024, 384]); csv = din("csv", [nl, 2, 1024, 384])
    spool = din("spool", [nl, 2, 15, 256]); sgla = din("sgla", [nl, 2, 4, 96, 96])
    cmk = din("cmk", [nl, 2, 256, 1024]); cmv = din("cmv", [nl, 2, 256, 1024])
    w_in = din("w_in", [nl, 1024, 6032]); w_a2 = din("w_gla_a2", [nl, 16, 384]); b_a = din("b_gla_a", [nl, 384])
    gla_norm = din("gla_norm", [nl, 384]); w_pool = din("w_pool", [nl, 4, 64, 64]); pool_scale = din("pool_scale", [nl, 256])
    w_ba = din("w_branch_a", [nl, 384, 1024]); w_bb = din("w_branch_b", [nl, 256, 1024]); w_bc = din("w_branch_c", [nl, 384, 1024])
    w_mix = din("w_mix_out", [nl, 1024, 1024]); mem_norm = din("mem_norm", [nl, 1024])
    w_xq = din("w_xq", [nl, 1024, 1024]); w_xk = din("w_xk", [nl, 1024, 1024]); w_xv = din("w_xv", [nl, 1024, 1024]); w_xo = din("w_xo", [nl, 1024, 1024])
    w_fi = din("w_ffn_in", [nl, 1024, 5632]); w_fo = din("w_ffn_out", [nl, 2816, 1024])
    gnames = ["norm_mix_pre", "norm_mix_post", "norm_x_pre", "norm_x_post", "norm_ffn_pre", "norm_ffn_post"]
    gains = [din(n, [nl, 1024]) for n in gnames]
    cst = din("cst", [128, C_END])

    yp = dout("yp", [2048, 1024]); ys = dout("ys", [32, 1024])
    okp = dout("okp", [nl, 2048, 384]); ovp = dout("ovp", [nl, 2048, 384])
    opp = dout("opp", [nl, 15, 256]); ogp = dout("ogp", [nl, 4, 96, 96])
    omk = dout("omk", [nl, 256, 1024]); omv = dout("omv", [nl, 256, 1024])
    oks = dout("oks", [nl, 32, 384]); ovs = dout("ovs", [nl, 32, 384])
    ops_ = dout("ops", [nl, 2, 15, 256]); ogs = dout("ogs", [nl, 2, 4, 96, 96])

    with es:
        P = Prog(nc, es)
        _LASTP[0] = P

        def sb(name, shape, dt):
            return es.enter_context(nc.sbuf_tensor(name, shape, dt))

        xT = sb("xT", [128, 8, NT], F32)
        RX = [Res("x%d" % c) for c in range(4)]
        kT = sb("kT", [128, 3, NT], BF16); RKT = Res("kT")
        vtok = sb("vtok", [128, 16, 384], BF16); RVT = Res("vtok")
        vtoks = sb("vtoks", [16, 2, 384], BF16); RVTS = Res("vtoks")
        hT = sb("hT", [128, 8, CW], BF16); RH = Res("hT")
        rstd = sb("rstd", [128, CW], F32); RRS = Res("rstd")
        big = sb("big", [128, 11, CW], BF16); RBIG = Res("big")
        oaT = sb("oaT", [128, 3, CW], BF16); ROA = Res("oaT")
        obT = sb("obT", [128, 2, CW], BF16); ROB = Res("obT")
        ocT = sb("ocT", [96, 4, CW], BF16); ROC = Res("ocT")
        cstb = sb("cstb", [128, C_INV], BF16); RC = Res("cst")
        identf = sb("identf", [128, 128], F32)
        invcf = sb("invcf", [128, 30], F32)
        gall = sb("gall", [128, 7, 4, 8], F32)
        gln = sb("gln", [96, 4, 4], F32)
        psc = sb("psc", [128, 4, 2], F32)
        epsb = sb("epsb", [128, 1], F32)
        mkT = sb("mkT", [128, 8, 256], BF16); RMK = Res("mkT")
        mvt = sb("mvt", [128, 2, 1024], BF16); RMV = Res("mvt")
        wpbd = sb("wpbd", [128, 2, 128], BF16); RWP = Res("wpbd")
        lrT = sb("lrT", [32, CW], BF16); RLR = Res("lrT")
        wa2 = sb("wa2", [32, 384], BF16); RWA2 = Res("wa2")
        ebl = sb("ebl", [96, 4, 8], F32); REBL = Res("ebl")
        Sst = sb("Sst", [96, 4, 96], F32); RS = Res("S")
        uhalo = sb("uhalo", [128, 2, 15], F32); RUH = Res("uhalo")
        NS = 3
        SLOT = 4352
        ring = [sb("ring%d" % i, [128, SLOT], BF16) for i in range(NS)]
        RRING = [Res("ring%d" % i) for i in range(NS)]
        DRING = [P.dma_sem() for _ in range(NS)]
        AR = 40960
        arena = sb("arena", [128, AR // 2], BF16)
        print("sbuf remaining", nc.sbuf_bytes_remaining)

        def av(name, off, shape, dt):
            esz = 4 if dt == F32 else 2
            nel = 1
            for d in shape[1:]:
                nel *= d
            assert off % 4 == 0 and off + nel * esz <= AR, (name, off, nel * esz)
            a = arena[:shape[0], off // 2:off // 2 + nel * esz // 2]
            if dt == F32:
                a = a.bitcast(F32)
            if len(shape) == 3:
                a = a.rearrange("p (a b) -> p a b", a=shape[1])
            elif len(shape) == 4:
                a = a.rearrange("p (a b c) -> p a b c", a=shape[1], b=shape[2])
            r = ARes(name, off, off + nel * esz)
            P.arena.append(r)
            return a, r

        stg, RST = av("stg", 0, [128, 8, CW], F32)
        sqb, RSQ = av("sqb", 17408, [128, 8, CW], BF16)
        memT, RMEM = av("memT", 26112, [128, 8, 256], F32)
        mnT, RMN = av("mnT", 34304, [128, 8, 256], BF16)
        cstf, RCF = av("cstf", 0, [128, C_END], F32)
        qT, RQ = av("qT", 0, [128, 3, CW], BF16)
        ez, REZ, spb, RSP, enb, REN, wb, RWB = [], [], [], [], [], [], [], []
        for i in range(3):
            a, r = av("ez%d" % i, 3264 + 2048 * i, [128, 512], F32); ez.append(a); REZ.append(r)
            a, r = av("sp%d" % i, 9408 + 1024 * i, [128, 512], BF16); spb.append(a); RSP.append(r)
            a, r = av("wb%d" % i, 16576 + 1024 * i, [128, 512], BF16); wb.append(a); RWB.append(r)
        for i in range(2):
            a, r = av("en%d" % i, 12480 + 2048 * i, [128, 512], F32); enb.append(a); REN.append(r)
        Rb, RR = av("Rb", 19648, [128, 512], BF16)
        kcs, RKCS = av("kcs", 20672, [128, 8, 384], BF16)
        kcsT, RKCST = av("kcsT", 26816, [128, 3, 1024], BF16)
        vcs, RVCS = av("vcs", 32960, [128, 8, 384], BF16)
        PW = 15 + 512 + 62
        uT, RU = av("uT", 0, [128, 2, PW], F32)
        pA, RPA = av("pA", 4712, [128, 2, PW], F32)
        pB, RPB = av("pB", 9424, [128, 2, PW], F32)
        pC, RPC = av("pC", 14136, [128, PW], F32)
        pD, RPD = av("pD", 16492, [128, PW], F32)
        pooled, RPL = av("pooled", 18848, [128, 2, CW], BF16)
        ptmp, RPT_ = av("ptmp", 21024, [128, 16], F32)
        HWD = 288
        qtT, RQT = av("qtT", 0, [96, 4, HWD], BF16)
        ktT, RKTT_ = av("ktT", 2304, [96, 4, HWD], BF16)
        sgc, RSG = av("sgc", 4608, [96, 4, HWD], BF16)
        eq, REQ = av("eq", 6912, [96, 4, HWD], F32)
        eqi, REQI = av("eqi", 11520, [96, 4, HWD], BF16)
        vct, RVCT = av("vct", 13824, [128, 2, 384], BF16)
        lat, RLA = av("lat", 15360, [128, 2, 384], BF16)
        ktt, RKTT = av("ktt", 16896, [128, 2, 384], BF16)
        ekt, REKT = av("ekt", 18432, [128, 384], F32)
        gt1, RGT1 = av("gt1", 19968, [128, 512], F32)
        vcts, RVCTS = av("vcts", 22016, [16, 2, 384], BF16)
        lats, RLAS = av("lats", 23552, [16, 2, 384], BF16)
        ktts, RKTTS = av("ktts", 25088, [16, 2, 384], BF16)
        Sbf, RSBF = av("Sbf", 26624, [96, 6, 4, 96], BF16)
        attb, RATT = av("attb", 31232, [128, 4, 128], BF16)
        osb, ROS = av("osb", 32256, [96, 4, 128], F32)
        osq, ROSQ = av("osq", 34304, [96, 4, 128], BF16)
        grs, RGRS = av("grs", 35328, [96, 512], F32)
        attb2, RATT2 = av("attb2", 20480, [128, 4, 128], BF16)
        osb2, ROS2 = av("osb2", 18432, [96, 4, 128], F32)
        osq2, ROSQ2 = av("osq2", 15360, [96, 4, 128], BF16)
        grs2, RGRS2 = av("grs2", 16384, [96, 512], F32)
        Stm, RSTM = av("Stm", 37376, [96, 4, 96], F32)
        Ss, RSS = av("Ss", 38912, [96, 4, 96], F32)
        sg3, RSG3 = [], []
        for i in range(3):
            a, r = av("sg%d" % i, 26112 + 2048 * i, [128, 512], F32); sg3.append(a); RSG3.append(r)
        mt1, RMT1 = av("mt1", 32256, [128, 512], F32)
        mt2, RMT2 = av("mt2", 34304, [128, 512], F32)
        pTb, RPT = av("pTb", 26112, [128, 2, 512], BF16)
        rden, RRD = av("rden", 28160, [128, 512], F32)
        pTb2, RPT2 = av("pTb2", 17408, [128, 2, 512], BF16)
        rden2, RRD2 = av("rden2", 19456, [128, 512], F32)
        mkl, RMKL = av("mkl", 0, [128, 2, 1024], BF16)
        mkTs, RMKTS = av("mkTs", 30208, [128, 8, 256], BF16)
        mvs, RMVS = av("mvs", 34304, [128, 2, 1024], BF16)
        ft1, RFT1 = av("ft1", 26112, [128, 512], F32)

        ps = [es.enter_context(nc.psum_tensor("ps%d" % i, [128, 512], F32)) for i in range(8)]
        RP = [PRes("ps%d" % i) for i in range(8)]
        bank_i = [0]

        held = set()

        def bank(hold=False):
            while True:
                b = bank_i[0] % 8
                bank_i[0] += 1
                if b not in held:
                    break
            if hold:
                held.add(b)
            return b

        ring_i = [0]
        wres = {}
        slot_key = [None] * NS

        def get_w(key, loads):
            if key in wres:
                return wres[key]
            si = ring_i[0] % NS
            ring_i[0] += 1
            if slot_key[si] is not None:
                del wres[slot_key[si]]
            slot_key[si] = key
            wres[key] = si
            for dstf, src in loads:
                dst = dstf(ring[si])
                P.op("pool", lambda e, dst=dst, src=src: e.dma_start(out=dst, in_=src), writes=[RRING[si]], dma=DRING[si])
            return si

        def v3(kc, ncols, off=0):
            return lambda t: t[:, off:off + kc * ncols].rearrange("p (k c) -> p k c", k=kc)

        def rows(w2d):
            return w2d.rearrange("(k p) c -> p k c", p=128)

        dsm = {}
        P._dsm = dsm

        def dsem(name, out=False):
            if name not in dsm:
                dsm[name] = P.dma_sem(out=out)
            return dsm[name]

        def spdma(dst, src, reads, writes, name, out=False, nc_ok=False):
            if nc_ok:
                P.op("sp", lambda e: e.dma_start(out=dst, in_=src, allow_slow_non_contiguous=True), reads, writes, dma=dsem(name, out))
            else:
                P.op("sp", lambda e: e.dma_start(out=dst, in_=src), reads, writes, dma=dsem(name, out))

        def pooldma(dst, src, writes, name):
            P.op("pool", lambda e: e.dma_start(out=dst, in_=src), writes=writes, dma=dsem(name))

        ident = identf[:, :]
        identb = cstb[:, C_ID:C_ID + 128]
        trib = cstb[:, C_TRI:C_TRI + 128]
        onesb = cstb[:, C_ONE:C_ONE + 128]
        maskb = cstb[:, C_MASK:C_MASK + 512]
        tbd = cstb[:, C_TBD:C_TBD + 128]
        mbd = cstb[:, C_MBD:C_MBD + 128]
        invc = invcf[:, :].rearrange("p (j t) -> p j t", j=2)

        spdma(cstf[:, :], cst, [], [RCF], "cstf")
        P.act(cstb[:, :], cstf[:, 0:C_INV], AF.Copy, [RCF], [RC])
        P.act(identf[:, :], cstf[:, C_ID:C_ID + 128], AF.Copy, [RCF], [RC])
        P.act(invcf[:, :], cstf[:, C_INV:C_INV + 30], AF.Copy, [RCF], [RC])
        P.op("dve", lambda e: e.memset(epsb[:], EPS), writes=[RC])
        P.op("dve", lambda e: e.memset(lrT[:], 1.0), writes=[RLR])
        P.op("dve", lambda e: e.memset(wpbd[:], 0.0), writes=[RWP])
        for i, g in enumerate(gains + [mem_norm]):
            spdma(gall[:, i, 0:nl, :], g.rearrange("l (k p) -> p l k", p=128), [], [RC], "cst", nc_ok=True)
        spdma(gln[:, 0:nl, :], gla_norm.rearrange("l (h v) -> v l h", v=96), [], [RC], "cst", nc_ok=True)
        spdma(psc[:, 0:nl, :], pool_scale.rearrange("l (j p) -> p l j", p=128), [], [RC], "cst", nc_ok=True)

        def gain(i, l):
            return gall[:, i, l, :]

        def stats(src3, rsrc, lo, n, scale):
            b = bank()
            for k in range(8):
                P.mm(ps[b][:, 0:n], onesb, src3[:, k, lo:lo + n], k == 0, k == 7, [rsrc, RC], [RP[b]], k == 7)
            P.act(rstd[:, lo:lo + n], ps[b][:, 0:n], AF.Ln, [RP[b], RC], [RRS], scale=scale, bias=epsb[:, 0:1])
            P.act(rstd[:, lo:lo + n], rstd[:, lo:lo + n], AF.Exp, [RRS], [RRS], scale=-0.5)

        def norm_to_h(xv, rx, sbl, g):
            for lo, n in sbl:
                P.act(hT[:, :, lo:lo + n], xv[:, :, lo:lo + n], AF.Square, [rx], [RH])
                stats(hT, RH, lo, n, 1.0 / 1024)
                for k in range(8):
                    P.dve("scalar_tensor_tensor", [rx, RRS, RC], [RH], out=hT[:, k, lo:lo + n], in0=xv[:, k, lo:lo + n],
                          scalar=g[:, k:k + 1], in1=rstd[:, lo:lo + n], op0=ALU.mult, op1=ALU.mult)

        def post_norm_add(c, g):
            xv = xT[:, :, 512 * c:512 * c + CW]
            for lo, n in subs(c):
                stats(sqb, RSQ, lo, n, 1.0 / 1024)
                for k in range(8):
                    P.dve("tensor_tensor", [RST, RRS], [RST], out=stg[:, k, lo:lo + n], in0=stg[:, k, lo:lo + n], in1=rstd[:, lo:lo + n], op=ALU.mult)
                    P.dve("scalar_tensor_tensor", [RST, RC, RX[c]], [RX[c]], out=xv[:, k, lo:lo + n], in0=stg[:, k, lo:lo + n],
                          scalar=g[:, k:k + 1], in1=xv[:, k, lo:lo + n], op0=ALU.mult, op1=ALU.add)

        def proj_to_stg(c, key, wsrc2d, src3, rsrc):
            for hf in range(2):
                si = get_w((key, hf), [(v3(8, 512), rows(wsrc2d)[:, :, hf * 512:(hf + 1) * 512])])
                wv = v3(8, 512)(ring[si])
                for lo, n in subs(c):
                    for mm_ in range(4):
                        m = hf * 4 + mm_
                        b = bank()
                        for k in range(8):
                            P.mm(ps[b][:, 0:n], wv[:, k, mm_ * 128:(mm_ + 1) * 128], src3[:, k, lo:lo + n], k == 0, k == 7, [RRING[si], rsrc], [RP[b]], k == 7)
                        P.act(stg[:, m, lo:lo + n], ps[b][:, 0:n], AF.Copy, [RP[b]], [RST])
                        P.act(sqb[:, m, lo:lo + n], stg[:, m, lo:lo + n], AF.Square, [RST], [RSQ])

        for t in range(17):
            rn = 128 if t < 16 else 32
            src = xp[t * 128:(t + 1) * 128, :] if t < 16 else xs
            spdma(stg[:rn, 0:2, 0:512], src.rearrange("p (a b) -> p a b", a=2), [], [RST], "xin")
            c = min(t // 4, 3)
            col = t * 128
            for hb in range(2):
                b = bank()
                for j in range(4):
                    P.tr(ps[b][:, j * rn:(j + 1) * rn], stg[:rn, hb, j * 128:(j + 1) * 128], ident[:rn, :rn], [RST, RC], [RP[b]], j == 3)
                P.act(xT[:, hb * 4:(hb + 1) * 4, col:col + rn], ps[b][:, 0:4 * rn].rearrange("p (a b) -> p a b", a=4), AF.Copy, [RP[b]], [RX[c]])

        def layer(l):
            wr = rows(w_in[l])
            for t in range(2):
                spdma(stg[:, 0:2, 0:512], mem[t * 128:(t + 1) * 128, :].rearrange("p (a b) -> p a b", a=2), [], [RST], "xin")
                for hb in range(2):
                    b = bank()
                    for j in range(4):
                        P.tr(ps[b][:, j * 128:(j + 1) * 128], stg[:, hb, j * 128:(j + 1) * 128], ident, [RST, RC], [RP[b]], j == 3)
                    P.act(memT[:, hb * 4:(hb + 1) * 4, t * 128:(t + 1) * 128], ps[b][:, 0:512].rearrange("p (a b) -> p a b", a=4), AF.Copy, [RP[b]], [RMEM])
            ck("mem1")
            P.act(mnT[:, :, :], memT[:, :, :], AF.Square, [RMEM], [RMN])
            stats(mnT, RMN, 0, 256, 1.0 / 1024)
            for k in range(8):
                P.dve("scalar_tensor_tensor", [RMEM, RC, RRS], [RMN], out=mnT[:, k, :], in0=memT[:, k, :], scalar=gain(6, l)[:, k:k + 1],
                      in1=rstd[:, 0:256], op0=ALU.mult, op1=ALU.mult)
            ck("mem2")
            for which, wsrc, odram in ((0, w_xk, omk), (1, w_xv, omv)):
                if which == 1:
                    ck("mem3")
                for hf in range(2):
                    si = get_w(("xkv", which, hf), [(v3(8, 512), rows(wsrc[l])[:, :, hf * 512:(hf + 1) * 512])])
                    wv = v3(8, 512)(ring[si])
                    if which == 0:
                        for jj in range(4):
                            b = bank()
                            for k in range(8):
                                P.mm(ps[b][:, 0:256], wv[:, k, jj * 128:(jj + 1) * 128], mnT[:, k, :], k == 0, k == 7, [RRING[si], RMN], [RP[b]], k == 7)
                            P.act(mkT[:, hf * 4 + jj, :], ps[b][:, 0:256], AF.Copy, [RP[b]], [RMK])
                    for t in range(2):
                        b = bank()
                        for k in range(8):
                            P.mm(ps[b][:, :], mnT[:, k, t * 128:(t + 1) * 128], wv[:, k, :], k == 0, k == 7, [RRING[si], RMN], [RP[b]], k == 7)
                        P.act(stg[:, t, 0:512], ps[b][:, :], AF.Copy, [RP[b]], [RST])
                        if which == 1:
                            P.dve("tensor_copy", [RP[b]], [RMV], out=mvt[:, t, hf * 512:(hf + 1) * 512], in_=ps[b][:, :])
                    spdma(odram[l, :, hf * 512:(hf + 1) * 512].rearrange("(t p) f -> p t f", p=128), stg[:, 0:2, 0:512], [RST], [], "omem", out=True)
            ck("memkv")
            for c in range(4):
                xv = xT[:, :, 512 * c:512 * c + CW]
                norm_to_h(xv, RX[c], subs(c), gain(0, l))
                sk = get_w(("wk", l), [(v3(8, 384), wr[:, :, O_K:O_K + 384])])
                wk = v3(8, 384)(ring[sk])
                for lo, n in subs(c):
                    for j in range(3):
                        b = bank()
                        for k in range(8):
                            P.mm(ps[b][:, 0:n], wk[:, k, j * 128:(j + 1) * 128], hT[:, k, lo:lo + n], k == 0, k == 7, [RRING[sk], RH], [RP[b]], k == 7)
                        P.act(kT[:, j, 512 * c + lo:512 * c + lo + n], ps[b][:, 0:n], AF.Copy, [RP[b]], [RKT])
                sv_ = get_w(("wv", l), [(v3(8, 384), wr[:, :, O_V:O_V + 384])])
                wvv = v3(8, 384)(ring[sv_])
                tiles = [(t * 128, 128, 4 * c + t) for t in range(4)]
                if c == 3:
                    tiles += [(512, 16, 16), (528, 16, 17)]
                for lo, n, gt in tiles:
                    for which, wsl, rsl in ((0, wk, sk), (1, wvv, sv_)):
                        b = bank()
                        for k in range(8):
                            P.mm(ps[b][:n, 0:384], hT[:, k, lo:lo + n], wsl[:, k, :], k == 0, k == 7, [RRING[rsl], RH], [RP[b]], k == 7)
                        P.act(stg[:n, which, 0:384], ps[b][:n, 0:384], AF.Copy, [RP[b]], [RST])
                        if which == 1:
                            if gt < 16:
                                P.dve("tensor_copy", [RP[b]], [RVT], out=vtok[:, gt, :], in_=ps[b][:, 0:384])
                            else:
                                P.dve("tensor_copy", [RP[b]], [RVTS], out=vtoks[:, gt - 16, :], in_=ps[b][:16, 0:384])
                    if gt < 16:
                        spdma(okp[l, gt * 128:(gt + 1) * 128, :], stg[:, 0, 0:384], [RST], [], "okv", out=True)
                        spdma(ovp[l, gt * 128:(gt + 1) * 128, :], stg[:, 1, 0:384], [RST], [], "okv", out=True)
                    else:
                        bb = gt - 16
                        spdma(oks[l, bb * 16:(bb + 1) * 16, :], stg[:16, 0, 0:384], [RST], [], "okv", out=True)
                        spdma(ovs[l, bb * 16:(bb + 1) * 16, :], stg[:16, 1, 0:384], [RST], [], "okv", out=True)
            ck("prekv")
            for g4 in range(4):
                j, hh = g4 // 2, g4 % 2
                pooldma(wpbd[hh * 64:(hh + 1) * 64, j, hh * 64:(hh + 1) * 64], w_pool[l, g4], [RWP], "wp")
            pooldma(wa2[0:16, :], w_a2[l], [RWA2], "wa2")
            pooldma(wa2[16:17, :], b_a[l:l + 1, :], [RWA2], "wa2")
            P.op("dve", lambda e: e.memset(Sst[:], 0.0), writes=[RS])
            for c in range(4):
                chunk(l, c, wr)

        def chunk(l, c, wr):
            xv = xT[:, :, 512 * c:512 * c + CW]
            sbl = subs(c)
            norm_to_h(xv, RX[c], sbl, gain(0, l))
            si = get_w(("wq", l), [(v3(8, 384), wr[:, :, O_Q:O_Q + 384])])
            wq = v3(8, 384)(ring[si])
            for lo, n in sbl:
                for j in range(3):
                    b = bank()
                    for k in range(8):
                        P.mm(ps[b][:, 0:n], wq[:, k, j * 128:(j + 1) * 128], hT[:, k, lo:lo + n], k == 0, k == 7, [RRING[si], RH], [RP[b]], k == 7)
                    P.act(qT[:, j, lo:lo + n], ps[b][:, 0:n], AF.Copy, [RP[b]], [RQ])
            ck("q")
            nkt = 4 * c + 4
            kts = list(range(nkt - 1, -1, -1))
            for h in range(6):
                j, pb = h // 2, 64 * (h % 2)
                bo = bank(hold=True)
                st = {}

                def S1(ti):
                    kt = kts[ti]
                    i = kt - 4 * c
                    c0 = 128 * i if i > 0 else 0
                    ncl = 512 - c0
                    r = ti % 3
                    bz = bank()
                    P.mm(ps[bz][:, 0:ncl], kT[pb:pb + 64, j, kt * 128:(kt + 1) * 128], qT[pb:pb + 64, j, c0:512], True, True, [RKT, RQ], [RP[bz]], True)
                    P.act(ez[r][:, 0:ncl], ps[bz][:, 0:ncl], AF.Exp, [RP[bz]], [REZ[r]], scale=0.125)
                    if i >= 0:
                        P.dve("tensor_tensor", [REZ[r], RC], [REZ[r]], out=ez[r][:, 0:ncl], in0=ez[r][:, 0:ncl], in1=maskb[:, 0:ncl], op=ALU.mult)
                    P.act(spb[r][:, 0:ncl], ez[r][:, 0:ncl], AF.Ln, [REZ[r]], [RSP[r]], bias=1.0)
                    st[ti] = (kt, c0, ncl, r)

                def S2(ti):
                    kt, c0, ncl, r = st[ti]
                    r2 = ti % 2
                    bc = bank()
                    lastk = (kt == nkt - 1)
                    P.mm(ps[bc][:, 0:ncl], trib, spb[r][:, 0:ncl], True, lastk, [RSP[r], RC], [RP[bc]], lastk)
                    if not lastk:
                        P.mm(ps[bc][:, 0:ncl], onesb, Rb[:, c0:512], False, True, [RR, RC], [RP[bc]], True)
                    P.act(enb[r2][:, 0:ncl], ps[bc][:, 0:ncl], AF.Exp, [RP[bc]], [REN[r2]], scale=-1.0)
                    P.dve("tensor_tensor", [REZ[r], REN[r2]], [RWB[r]], out=wb[r][:, 0:ncl], in0=ez[r][:, 0:ncl], in1=enb[r2][:, 0:ncl], op=ALU.mult)
                    if kt > 0:
                        if lastk:
                            P.op("dve", lambda e: e.memset(Rb[:, 0:384], 0.0), writes=[RR])
                            P.dve("tensor_copy", [RSP[r]], [RR], out=Rb[:, c0:512], in_=spb[r][:, 0:ncl])
                        else:
                            P.dve("tensor_tensor", [RSP[r], RR], [RR], out=Rb[:, c0:512], in0=Rb[:, c0:512], in1=spb[r][:, 0:ncl], op=ALU.add)

                def S3(ti):
                    kt, c0, ncl, r = st[ti]
                    P.mm(ps[bo][:, c0:512], vtok[:, kt, j * 128:(j + 1) * 128], wb[r][:, 0:ncl], kt == nkt - 1, kt == 0, [RVT, RWB[r]], [RP[bo]], kt == 0, sgc=True)

                nt_ = len(kts)
                for step in range(nt_ + 2):
                    if step < nt_:
                        S1(step)
                    if 0 <= step - 1 < nt_:
                        S2(step - 1)
                    if 0 <= step - 2 < nt_:
                        S3(step - 2)
                P.act(oaT[pb:pb + 64, j, 0:512], ps[bo][pb:pb + 64, 0:512], AF.Copy, [RP[bo]], [ROA])
                held.discard(bo)
            ck("sb%d" % c)
            if c == 3:
                for b_ in range(2):
                    sb_sample(l, b_)
            ck("sbs%d" % c)
            pool_branch(l, c, wr)
            ck("pool%d" % c)
            for hf in range(2):
                gla_half(l, c, hf, wr)
            ck("gla%d" % c)
            merged = big[:, 0:8, :]
            for m in range(8):
                loads = []
                for br in range(3):
                    loads.append((lambda t, br=br: t[:, 0:3072].rearrange("p (k r c) -> p k r c", k=8, r=3)[:, :, br, :],
                                  wr[:, :, O_G + br * 1024 + m * 128:O_G + br * 1024 + (m + 1) * 128]))
                loads.append((lambda t: t[:, 3072:3456].rearrange("p (k c) -> p k c", k=3), rows(w_ba[l])[:, :, m * 128:(m + 1) * 128]))
                loads.append((lambda t: t[:, 3456:3712].rearrange("p (k c) -> p k c", k=2), rows(w_bb[l])[:, :, m * 128:(m + 1) * 128]))
                loads.append((lambda t: t[:96, 3712:4224].rearrange("p (k c) -> p k c", k=4), w_bc[l].rearrange("(h v) c -> v h c", v=96)[:, :, m * 128:(m + 1) * 128]))
                si = get_w(("mrg", l, m), loads)
                gv = ring[si][:, 0:3072].rearrange("p (k r c) -> p k r c", k=8, r=3)
                a_v = ring[si][:, 3072:3456].rearrange("p (k c) -> p k c", k=3)
                b_v = ring[si][:, 3456:3712].rearrange("p (k c) -> p k c", k=2)
                c_v = ring[si][:96, 3712:4224].rearrange("p (k c) -> p k c", k=4)
                for lo, n in sbl:
                    bg = [bank() for _ in range(3)]
                    bb = [bank() for _ in range(3)]
                    for br in range(3):
                        for k in range(8):
                            P.mm(ps[bg[br]][:, 0:n], gv[:, k, br, :], hT[:, k, lo:lo + n], k == 0, k == 7, [RRING[si], RH], [RP[bg[br]]], k == 7)
                    for k in range(3):
                        P.mm(ps[bb[0]][:, 0:n], a_v[:, k, :], oaT[:, k, lo:lo + n], k == 0, k == 2, [RRING[si], ROA], [RP[bb[0]]], k == 2)
                    for k in range(2):
                        P.mm(ps[bb[1]][:, 0:n], b_v[:, k, :], obT[:, k, lo:lo + n], k == 0, k == 1, [RRING[si], ROB], [RP[bb[1]]], k == 1)
                    for k in range(4):
                        P.mm(ps[bb[2]][:, 0:n], c_v[:, k, :], ocT[:, k, lo:lo + n], k == 0, k == 3, [RRING[si], ROC], [RP[bb[2]]], k == 3)
                    for br in range(3):
                        P.act(sg3[br][:, 0:n], ps[bg[br]][:, 0:n], AF.Sigmoid, [RP[bg[br]]], [RSG3[br]])
                    P.dve("tensor_tensor", [RSG3[0], RP[bb[0]]], [RMT1], out=mt1[:, 0:n], in0=sg3[0][:, 0:n], in1=ps[bb[0]][:, 0:n], op=ALU.mult)
                    P.dve("tensor_tensor", [RSG3[1], RP[bb[1]]], [RMT2], out=mt2[:, 0:n], in0=sg3[1][:, 0:n], in1=ps[bb[1]][:, 0:n], op=ALU.mult)
                    P.dve("tensor_tensor", [RMT1, RMT2], [RMT1], out=mt1[:, 0:n], in0=mt1[:, 0:n], in1=mt2[:, 0:n], op=ALU.add)
                    P.dve("tensor_tensor", [RSG3[2], RP[bb[2]]], [RMT2], out=mt2[:, 0:n], in0=sg3[2][:, 0:n], in1=ps[bb[2]][:, 0:n], op=ALU.mult)
                    P.dve("tensor_tensor", [RMT1, RMT2], [RBIG], out=merged[:, m, lo:lo + n], in0=mt1[:, 0:n], in1=mt2[:, 0:n], op=ALU.add)
            proj_to_stg(c, ("mix", l), w_mix[l], merged, RBIG)
            post_norm_add(c, gain(1, l))
            ck("mix%d" % c)
            xattn(l, c)
            ck("xattn%d" % c)
            norm_to_h(xv, RX[c], sbl, gain(4, l))
            hid = big
            wfr = rows(w_fi[l])
            wor = rows(w_fo[l])
            for jh in range(2):
                for j0, nj in ((0, 2), (2, 2), (4, 2), (6, 2), (8, 2), (10, 1)):
                    jg = jh * 11 + j0
                    si = get_w(("fi", l, jg), [
                        (lambda t, nj=nj: t[:, 0:16 * nj * 128].rearrange("p (k r c) -> p k r c", k=8, r=2)[:, :, 0, :], wfr[:, :, jg * 128:(jg + nj) * 128]),
                        (lambda t, nj=nj: t[:, 0:16 * nj * 128].rearrange("p (k r c) -> p k r c", k=8, r=2)[:, :, 1, :], wfr[:, :, 2816 + jg * 128:2816 + (jg + nj) * 128])])
                    wv = ring[si][:, 0:16 * nj * 128].rearrange("p (k r c) -> p k r c", k=8, r=2)
                    for jj in range(nj):
                        for lo, n in sbl:
                            b1, b2 = bank(), bank()
                            for k in range(8):
                                P.mm(ps[b1][:, 0:n], wv[:, k, 0, jj * 128:(jj + 1) * 128], hT[:, k, lo:lo + n], k == 0, k == 7, [RRING[si], RH], [RP[b1]], k == 7)
                            for k in range(8):
                                P.mm(ps[b2][:, 0:n], wv[:, k, 1, jj * 128:(jj + 1) * 128], hT[:, k, lo:lo + n], k == 0, k == 7, [RRING[si], RH], [RP[b2]], k == 7)
                            P.act(ft1[:, 0:n], ps[b1][:, 0:n], AF.Silu, [RP[b1]], [RFT1])
                            P.dve("tensor_tensor", [RFT1, RP[b2]], [RBIG], out=hid[:, j0 + jj, lo:lo + n], in0=ft1[:, 0:n], in1=ps[b2][:, 0:n], op=ALU.mult)
                for mb in range(4):
                    si = get_w(("fo", l, jh, mb), [(v3(11, 256), wor[:, jh * 11:(jh + 1) * 11, mb * 256:(mb + 1) * 256])])
                    wv = v3(11, 256)(ring[si])
                    for lo, n in sbl:
                        for mm_ in range(2):
                            m = mb * 2 + mm_
                            b = bank()
                            for k in range(11):
                                P.mm(ps[b][:, 0:n], wv[:, k, mm_ * 128:(mm_ + 1) * 128], hid[:, k, lo:lo + n], k == 0, k == 10, [RRING[si], RBIG], [RP[b]], k == 10)
                            if jh == 0:
                                P.act(stg[:, m, lo:lo + n], ps[b][:, 0:n], AF.Copy, [RP[b]], [RST])
                            else:
                                P.dve("tensor_tensor", [RST, RP[b]], [RST], out=stg[:, m, lo:lo + n], in0=stg[:, m, lo:lo + n], in1=ps[b][:, 0:n], op=ALU.add)
                                P.act(sqb[:, m, lo:lo + n], stg[:, m, lo:lo + n], AF.Square, [RST], [RSQ])
            post_norm_add(c, gain(5, l))
            ck("ffn%d" % c)

        def sb_sample(l, b_):
            pooldma(kcs[:, :, :], csk[l, b_].rearrange("(t p) f -> p t f", p=128), [RKCS], "kcs")
            pooldma(vcs[:, :, :], csv[l, b_].rearrange("(t p) f -> p t f", p=128), [RVCS], "vcs")
            for t in range(8):
                bk = bank()
                pv = ps[bk][:, :].bitcast(BF16)
                for j in range(3):
                    P.tr(pv[:, j * 128:(j + 1) * 128], kcs[:, t, j * 128:(j + 1) * 128], identb, [RKCS, RC], [RP[bk]], j == 2)
                P.act(kcsT[:, :, t * 128:(t + 1) * 128], pv[:, 0:384].rearrange("p (a b) -> p a b", a=3), AF.Copy, [RP[bk]], [RKCST])
            ck("ss1")
            bo = bank(hold=True)
            qc0 = 512 + 16 * b_
            NCOL = 96
            Rn = Rb[:16, 256:256 + NCOL]
            for kt in range(8, -1, -1):
                np_ = 16 if kt == 8 else 128
                bzs = [bank(), bank()]
                for par in range(2):
                    for hh in range(3):
                        h = 2 * hh + par
                        j, pb = h // 2, 64 * (h % 2)
                        qv = qT[pb:pb + 64, j, qc0:qc0 + 16]
                        if kt == 8:
                            kv = kT[pb:pb + 64, j, 2048 + 16 * b_:2048 + 16 * b_ + 16]
                        else:
                            kv = kcsT[pb:pb + 64, j, kt * 128:(kt + 1) * 128]
                        P.mm(ps[bzs[par]][:np_, hh * 16:hh * 16 + 16], kv, qv, True, True, [RKT, RKCST, RQ], [RP[bzs[par]]], hh == 2)
                if kt == 7:
                    ck("ss2")
                if kt == 8:
                    ck("ss3")
                r = 0
                for par in range(2):
                    P.act(ez[r][:np_, par * 48:par * 48 + 48], ps[bzs[par]][:np_, 0:48], AF.Exp, [RP[bzs[par]]], [REZ[r]], scale=0.125)
                if kt == 8:
                    for h in range(6):
                        P.dve("tensor_tensor", [REZ[r], RC], [REZ[r]], out=ez[r][:16, h * 16:h * 16 + 16], in0=ez[r][:16, h * 16:h * 16 + 16],
                              in1=maskb[:16, 0:16], op=ALU.mult)
                P.act(spb[r][:np_, 0:NCOL], ez[r][:np_, 0:NCOL], AF.Ln, [REZ[r]], [RSP[r]], bias=1.0)
                bc = bank()
                P.mm(ps[bc][:np_, 0:NCOL], trib[:np_, :np_], spb[r][:np_, 0:NCOL], True, kt == 8, [RSP[r], RC], [RP[bc]], kt == 8)
                if kt <= 7:
                    P.mm(ps[bc][:, 0:NCOL], onesb[:16, :], Rn, False, kt == 7, [RR, RC], [RP[bc]], kt == 7)
                if kt < 7:
                    P.mm(ps[bc][:, 0:NCOL], onesb, Rb[:, 0:NCOL], False, True, [RR, RC], [RP[bc]], True)
                ck("ss4")
                P.act(enb[r][:np_, 0:NCOL], ps[bc][:np_, 0:NCOL], AF.Exp, [RP[bc]], [REN[r]], scale=-1.0)
                P.dve("tensor_tensor", [REZ[r], REN[r]], [RWB[r]], out=wb[r][:np_, 0:NCOL], in0=ez[r][:np_, 0:NCOL], in1=enb[r][:np_, 0:NCOL], op=ALU.mult)
                if kt == 8:
                    P.dve("tensor_copy", [RSP[r]], [RR], out=Rn, in_=spb[r][:16, 0:NCOL])
                elif kt == 7:
                    P.dve("tensor_copy", [RSP[r]], [RR], out=Rb[:, 0:NCOL], in_=spb[r][:, 0:NCOL])
                elif kt > 0:
                    P.dve("tensor_tensor", [RSP[r], RR], [RR], out=Rb[:, 0:NCOL], in0=Rb[:, 0:NCOL], in1=spb[r][:, 0:NCOL], op=ALU.add)
                ck("ss5")
                for h in range(6):
                    j = h // 2
                    if kt == 8:
                        lv = vtoks[:16, b_, j * 128:(j + 1) * 128]
                    else:
                        lv = vcs[:, kt, j * 128:(j + 1) * 128]
                    hc = (h % 2) * 48 + (h // 2) * 16
                    P.mm(ps[bo][:, hc:hc + 16], lv, wb[r][:np_, hc:hc + 16], (kt == 8 and h == 0), kt == 0, [RVTS, RVCS, RWB[r]], [RP[bo]], (h == 5 and kt == 0), sgc=True)
            for h in range(6):
                j, pb = h // 2, 64 * (h % 2)
                hc = (h % 2) * 48 + (h // 2) * 16
                P.act(oaT[pb:pb + 64, j, qc0:qc0 + 16], ps[bo][pb:pb + 64, hc:hc + 16], AF.Copy, [RP[bo]], [ROA])
            held.discard(bo)

        def pool_branch(l, c, wr):
            si = get_w(("wu", l), [(v3(8, 256), wr[:, :, O_U:O_U + 256])])
            wu = v3(8, 256)(ring[si])
            W = 527 if c < 3 else PW
            if c > 0:
                P.act(uT[:, :, 0:15], uhalo[:, :, :], AF.Copy, [RUH], [RU])
            else:
                P.op("dve", lambda e: e.memset(uT[:, :, 0:15], 0.0), writes=[RU])
            segs = [(0, 512, 15)]
            if c == 3:
                segs += [(512, 16, 527 + 15), (528, 16, 527 + 31 + 15)]
                for b_ in range(2):
                    off = 527 + 31 * b_
                    for j in range(2):
                        spdma(uT[:, j, off:off + 15], spool[l, b_, :, j * 128:(j + 1) * 128].rearrange("t p -> p t"), [], [RU], "hist", nc_ok=True)
            for lo, n, dst in segs:
                for j in range(2):
                    b = bank()
                    for k in range(8):
                        P.mm(ps[b][:, 0:n], wu[:, k, j * 128:(j + 1) * 128], hT[:, k, lo:lo + n], k == 0, k == 7, [RRING[si], RH], [RP[b]], k == 7)
                    P.act(uT[:, j, dst:dst + n], ps[b][:, 0:n], AF.Copy, [RP[b]], [RU])
            P.act(uhalo[:, :, :], uT[:, :, 512:527], AF.Copy, [RU], [RUH])
            if c == 3:
                outs = [(512 - 15, opp[l]), (513, ops_[l, 0]), (529, ops_[l, 1])]
                for lo, od in outs:
                    b = bank()
                    for k in range(8):
                        P.mm(ps[b][:15, 0:256], hT[:, k, lo:lo + 15], wu[:, k, :], k == 0, k == 7, [RRING[si], RH], [RP[b]], k == 7)
                    P.act(pD[:15, 0:256], ps[b][:15, 0:256], AF.Copy, [RP[b]], [RPD])
                    spdma(od, pD[:15, 0:256], [RPD], [], "opool", out=True)
            P.dve("tensor_tensor", [RU], [RPA], out=pA[:, :, 1:W], in0=uT[:, :, 1:W], in1=uT[:, :, 0:W - 1], op=ALU.add)
            P.dve("tensor_tensor", [RPA], [RPB], out=pB[:, :, 3:W], in0=pA[:, :, 3:W], in1=pA[:, :, 1:W - 2], op=ALU.add)
            P.dve("tensor_tensor", [RPB], [RPC], out=pC[:, 7:W], in0=pB[:, 1, 7:W], in1=pB[:, 1, 3:W - 4], op=ALU.add)
            P.dve("tensor_tensor", [RPC], [RPD], out=pD[:, 15:W], in0=pC[:, 15:W], in1=pC[:, 7:W - 8], op=ALU.add)
            sel = [(0, 64, 0, pA[0:64, 0, :], 0.5, RPA), (64, 128, 0, pB[64:128, 0, :], 0.25, RPB), (0, 64, 1, pC[0:64, :], 0.125, RPC), (64, 128, 1, pD[64:128, :], 0.0625, RPD)]
            for p0, p1, j, sv, iw, rsv in sel:
                for lo, n, src in segs:
                    P.dve("scalar_tensor_tensor", [rsv, RU], [RPL], out=pooled[p0:p1, j, lo:lo + n], in0=sv[:, src:src + n], scalar=iw,
                          in1=uT[p0:p1, j, src:src + n], op0=ALU.mult, op1=ALU.subtract)
                if c == 0:
                    P.dve("tensor_tensor", [rsv, RC], [RPT_], out=ptmp[p0:p1, 0:15], in0=sv[:, 15:30], in1=invc[p0:p1, j, :], op=ALU.mult)
                    P.dve("tensor_tensor", [RPT_, RU], [RPL], out=pooled[p0:p1, j, 0:15], in0=ptmp[p0:p1, 0:15], in1=uT[p0:p1, j, 15:30], op=ALU.subtract)
            for lo, n in subs(c):
                for j in range(2):
                    b = bank()
                    P.mm(ps[b][:, 0:n], wpbd[:, j, :], pooled[:, j, lo:lo + n], True, True, [RWP, RPL], [RP[b]], True)
                    P.act(obT[:, j, lo:lo + n], ps[b][:, 0:n], AF.Copy, [RP[b], RC], [ROB], scale=psc[:, l, j:j + 1])

        def gla_half(l, c, hf, wr):
            base = 256 * hf
            sbl = [(base, 256, 0)]
            tiles = [(base, 128, 0, 0), (base + 128, 128, 1, 128)]
            samp = (c == 3 and hf == 1)
            if samp:
                sbl += [(512, 32, 256)]
                tiles += [(512, 16, 2, 256), (528, 16, 3, 272)]
            ncl = 288 if samp else 256

            def wG():
                return get_w(("gG", l), [(v3(8, 400), wr[:, :, O_GC:O_GC + 400])])

            def wK():
                return get_w(("gK", l), [(v3(8, 384), wr[:, :, O_KC:O_KC + 384])])

            def wV():
                return get_w(("gV", l), [(v3(8, 384), wr[:, :, O_VC:O_VC + 384])])

            def wQ():
                return get_w(("gQ", l), [(v3(8, 384), wr[:, :, O_QC:O_QC + 384])])

            s = wG(); w = v3(8, 400)(ring[s])
            for lo, n, lc in sbl:
                b = bank()
                for k in range(8):
                    P.mm(ps[b][:16, 0:n], w[:, k, 384:400], hT[:, k, lo:lo + n], k == 0, k == 7, [RRING[s], RH], [RP[b]], k == 7)
                P.act(lrT[:16, lo:lo + n], ps[b][:16, 0:n], AF.Copy, [RP[b]], [RLR])
            ck("g1")
            for lo, n, ti, lc in tiles:
                sm = ti >= 2
                bi = ti - 2
                vc_d = vcts[:, bi, :] if sm else vct[:, ti, :]
                la_d = lats[:, bi, :] if sm else lat[:, ti, :]
                kt_d = ktts[:, bi, :] if sm else ktt[:, ti, :]
                b = bank()
                P.mm(ps[b][:n, 0:384], lrT[0:17, lo:lo + n], wa2[0:17, :], True, True, [RLR, RWA2], [RP[b]], True)
                P.act(gt1[:n, 0:384], ps[b][:n, 0:384], AF.Exp, [RP[b]], [RGT1], scale=-1.0)
                P.act(la_d, gt1[:n, 0:384], AF.Ln, [RGT1], [RLAS if sm else RLA], bias=1.0)
                rla = RLAS if sm else RLA
                b = bank()
                P.mm(ps[b][:n, 0:384], tbd[:n, :n], la_d, True, True, [rla, RC], [RP[b]], True)
                P.act(ekt[:n, :], ps[b][:n, 0:384], AF.Exp, [RP[b]], [REKT])
                s = wK(); w = v3(8, 384)(ring[s])
                b = bank()
                for k in range(8):
                    P.mm(ps[b][:n, 0:384], hT[:, k, lo:lo + n], w[:, k, :], k == 0, k == 7, [RRING[s], RH], [RP[b]], k == 7)
                P.dve("tensor_tensor", [REKT, RP[b]], [RKTTS if sm else RKTT], out=kt_d, in0=ps[b][:n, 0:384], in1=ekt[:n, :], op=ALU.mult)
                s = wV(); w = v3(8, 384)(ring[s])
                b = bank()
                for k in range(8):
                    P.mm(ps[b][:n, 0:384], hT[:, k, lo:lo + n], w[:, k, :], k == 0, k == 7, [RRING[s], RH], [RP[b]], k == 7)
                P.act(vc_d, ps[b][:n, 0:384], AF.Copy, [RP[b]], [RVCTS if sm else RVCT])
                b = bank()
                for h in range(4):
                    P.mm(ps[b][:96, h * 128:h * 128 + n], la_d[:, h * 96:(h + 1) * 96], tbd[:n, :n], True, True, [rla, RC], [RP[b]], h == 3)
                pvw = ps[b][:96, :].rearrange("p (h t) -> p h t", h=4)[:, :, 0:n]
                P.act(eq[:, :, lc:lc + n], pvw, AF.Exp, [RP[b]], [REQ], scale=-1.0)
                P.act(eqi[:, :, lc:lc + n], pvw, AF.Exp, [RP[b]], [REQI])
            ck("g3")
            P.act(ebl[:, :, 0:4], eq[:, :, 63:256:64], AF.Copy, [REQ], [REBL])
            if samp:
                P.act(ebl[:, :, 4:6], eq[:, :, 271:288:16], AF.Copy, [REQ], [REBL])
            ub = []
            for g in range(4):
                ti, h2 = g // 2, g % 2
                b = bank(hold=True)
                ub.append(b)
                for h in range(4):
                    P.mm(ps[b][:96, h * 96:(h + 1) * 96], ktt[h2 * 64:(h2 + 1) * 64, ti, h * 96:(h + 1) * 96], vct[h2 * 64:(h2 + 1) * 64, ti, h * 96:(h + 1) * 96],
                         True, True, [RKTT, RVCT], [RP[b]], h == 3)
            for g in range(4):
                b = ub[g]
                P.act(Sbf[:, g], Sst[:], AF.Copy, [RS], [RSBF])
                P.dve("tensor_tensor", [RS, RP[b]], [RSTM], out=Stm[:, :, :], in0=Sst[:], in1=ps[b][:96, 0:384].rearrange("p (h v) -> p h v", h=4), op=ALU.add)
                held.discard(b)
                for h in range(4):
                    P.dve("tensor_scalar", [RSTM, REBL], [RS], out=Sst[:, h, :], in0=Stm[:, h, :], scalar1=ebl[:, h, g:g + 1], scalar2=None, op0=ALU.mult)
            if samp:
                spdma(ogp[l].rearrange("h k v -> k h v"), Sst[:], [RS], [], "ogp", out=True)
                for b_ in range(2):
                    spdma(Ss[:, :, :], sgla[l, b_].rearrange("h k v -> k h v"), [], [RSS], "sgla")
                    P.act(Sbf[:, 4 + b_], Ss[:, :, :], AF.Copy, [RSS], [RSBF])
                    b = bank()
                    for h in range(4):
                        P.mm(ps[b][:96, h * 96:(h + 1) * 96], ktts[:, b_, h * 96:(h + 1) * 96], vcts[:, b_, h * 96:(h + 1) * 96], True, True, [RKTTS, RVCTS], [RP[b]], h == 3)
                    P.dve("tensor_tensor", [RSS, RP[b]], [RSTM], out=Stm[:, :, :], in0=Ss[:, :, :], in1=ps[b][:96, 0:384].rearrange("p (h v) -> p h v", h=4), op=ALU.add)
                    for h in range(4):
                        P.dve("tensor_scalar", [RSTM, REBL], [RSS], out=Ss[:, h, :], in0=Stm[:, h, :], scalar1=ebl[:, h, 4 + b_:5 + b_], scalar2=None, op0=ALU.mult)
                    spdma(ogs[l, b_].rearrange("h k v -> k h v"), Ss[:, :, :], [RSS], [], "ogs", out=True)
            ck("g2")
            for lo, n, lc in sbl:
                for h in range(4):
                    s = wQ(); w = v3(8, 384)(ring[s])
                    b = bank()
                    for k in range(8):
                        P.mm(ps[b][:96, 0:n], w[:, k, h * 96:(h + 1) * 96], hT[:, k, lo:lo + n], k == 0, k == 7, [RRING[s], RH], [RP[b]], k == 7)
                    P.dve("scalar_tensor_tensor", [RP[b], REQ], [RQT], out=qtT[:, h, lc:lc + n], in0=ps[b][:96, 0:n], scalar=96.0 ** -0.5, in1=eq[:, h, lc:lc + n], op0=ALU.mult, op1=ALU.mult)
                    s = wK(); w = v3(8, 384)(ring[s])
                    b = bank()
                    for k in range(8):
                        P.mm(ps[b][:96, 0:n], w[:, k, h * 96:(h + 1) * 96], hT[:, k, lo:lo + n], k == 0, k == 7, [RRING[s], RH], [RP[b]], k == 7)
                    P.dve("tensor_tensor", [RP[b], REQI], [RKTT_], out=ktT[:, h, lc:lc + n], in0=ps[b][:96, 0:n], in1=eqi[:, h, lc:lc + n], op=ALU.mult)
                    s = wG(); w = v3(8, 400)(ring[s])
                    b = bank()
                    for k in range(8):
                        P.mm(ps[b][:96, 0:n], w[:, k, h * 96:(h + 1) * 96], hT[:, k, lo:lo + n], k == 0, k == 7, [RRING[s], RH], [RP[b]], k == 7)
                    P.act(sgc[:, h, lc:lc + n], ps[b][:96, 0:n], AF.Silu, [RP[b]], [RSG])
            ck("g4")
            bsets = [(attb, RATT, osb, ROS, osq, ROSQ, grs, RGRS), (attb2, RATT2, osb2, ROS2, osq2, ROSQ2, grs2, RGRS2)]

            def i_s1(tile, bs_, stt):
                lo, n, ti, lc = tile
                att_, ratt = bs_[0], bs_[1]
                b = bank()
                for h in range(4):
                    P.mm(ps[b][:n, h * 128:h * 128 + n], ktT[:, h, lc:lc + n], qtT[:, h, lc:lc + n], True, True, [RKTT_, RQT], [RP[b]], h == 3)
                for h in range(4):
                    P.dve("tensor_tensor", [RP[b], RC], [ratt], out=att_[:n, h, 0:n], in0=ps[b][:n, h * 128:h * 128 + n], in1=mbd[:n, 0:n], op=ALU.mult)

            def i_s2(tile, bs_, stt):
                lo, n, ti, lc = tile
                att_, ratt, osb_, ros, osq_, rosq = bs_[0:6]
                sm = ti >= 2
                bi = ti - 2
                vc_d = vcts[:, bi, :] if sm else vct[:, ti, :]
                rvc = RVCTS if sm else RVCT
                bo = bank()
                for h in range(4):
                    P.mm(ps[bo][:96, h * 128:h * 128 + n], vc_d[:, h * 96:(h + 1) * 96], att_[:n, h, 0:n], True, False, [rvc, ratt], [RP[bo]], False)
                    if sm:
                        P.mm(ps[bo][:96, h * 128:h * 128 + n], Sbf[:, 4 + bi, h, :], qtT[:, h, lc:lc + n], False, True, [RSBF, RQT], [RP[bo]], h == 3)
                    else:
                        for h2 in range(2):
                            g = ti * 2 + h2
                            P.mm(ps[bo][:96, h * 128 + h2 * 64:h * 128 + h2 * 64 + 64], Sbf[:, g, h, :], qtT[:, h, lc + h2 * 64:lc + h2 * 64 + 64], False, h2 == 1,
                                 [RSBF, RQT], [RP[bo]], h == 3 and h2 == 1)
                ov = ps[bo][:96, :].rearrange("p (h t) -> p h t", h=4)[:, :, 0:n]
                P.act(osb_[:, :, 0:n], ov, AF.Copy, [RP[bo]], [ros])
                P.act(osq_[:, :, 0:n], osb_[:, :, 0:n], AF.Square, [ros], [rosq])

            def i_s3(tile, bs_, stt):
                lo, n, ti, lc = tile
                osq_, rosq, grs_, rgrs = bs_[4:8]
                bs = bank()
                for h in range(4):
                    P.mm(ps[bs][:96, h * 128:h * 128 + n], onesb[:96, :96], osq_[:, h, 0:n], True, True, [rosq, RC], [RP[bs]], h == 3)
                sv = ps[bs][:96, :].rearrange("p (h t) -> p h t", h=4)[:, :, 0:n]
                gv = grs_[:, :].rearrange("p (h t) -> p h t", h=4)[:, :, 0:n]
                P.act(gv, sv, AF.Ln, [RP[bs], RC], [rgrs], scale=1.0 / 96, bias=epsb[:96, 0:1])
                P.act(gv, gv, AF.Exp, [rgrs], [rgrs], scale=-0.5)

            def i_s4(tile, bs_, stt):
                lo, n, ti, lc = tile
                osb_, ros, grs_, rgrs = bs_[2], bs_[3], bs_[6], bs_[7]
                gv = grs_[:, :].rearrange("p (h t) -> p h t", h=4)[:, :, 0:n]
                P.dve("tensor_tensor", [ros, rgrs], [ros], out=osb_[:, :, 0:n], in0=osb_[:, :, 0:n], in1=gv, op=ALU.mult)
                for h in range(4):
                    P.dve("scalar_tensor_tensor", [ros, RC, RSG], [ROC], out=ocT[:, h, lo:lo + n], in0=osb_[:, h, 0:n], scalar=gln[:, l, h:h + 1],
                          in1=sgc[:, h, lc:lc + n], op0=ALU.mult, op1=ALU.mult)

            for p0 in range(0, len(tiles), 2):
                pair = tiles[p0:p0 + 2]
                for stage in (i_s1, i_s2, i_s3, i_s4):
                    for k_, tile in enumerate(pair):
                        stage(tile, bsets[k_], None)

        def xattn(l, c):
            xv = xT[:, :, 512 * c:512 * c + CW]
            norm_to_h(xv, RX[c], subs(c), gain(2, l))
            qx = big[:, 0:8, :]
            for hf in range(2):
                si = get_w(("xq", l, hf), [(v3(8, 512), rows(w_xq[l])[:, :, hf * 512:(hf + 1) * 512])])
                wv = v3(8, 512)(ring[si])
                for lo, n in subs(c):
                    for mm_ in range(4):
                        b = bank()
                        for k in range(8):
                            P.mm(ps[b][:, 0:n], wv[:, k, mm_ * 128:(mm_ + 1) * 128], hT[:, k, lo:lo + n], k == 0, k == 7, [RRING[si], RH], [RP[b]], k == 7)
                        P.act(qx[:, hf * 4 + mm_, lo:lo + n], ps[b][:, 0:n], AF.Copy, [RP[b]], [RBIG])
            ox = hT
            blocks = [(0, 512, None)]
            if c == 3:
                blocks += [(512, 16, 0), (528, 16, 1)]
            for lo, n, sb_ in blocks:
                if sb_ is None:
                    mk_, mv_, rmk, rmv = mkT, mvt, RMK, RMV
                else:
                    pooldma(mkl[:, :, :], cmk[l, sb_].rearrange("(t p) f -> p t f", p=128), [RMKL], "mkl")
                    pooldma(mvs[:, :, :], cmv[l, sb_].rearrange("(t p) f -> p t f", p=128), [RMVS], "mvs")
                    for t in range(2):
                        for hb in range(2):
                            bk = bank()
                            pv = ps[bk][:, :].bitcast(BF16)
                            for j in range(4):
                                d = hb * 4 + j
                                P.tr(pv[:, j * 128:(j + 1) * 128], mkl[:, t, d * 128:(d + 1) * 128], identb, [RMKL, RC], [RP[bk]], j == 3)
                            P.act(mkTs[:, hb * 4:(hb + 1) * 4, t * 128:(t + 1) * 128], pv[:, 0:512].rearrange("p (a b) -> p a b", a=4), AF.Copy, [RP[bk]], [RMKTS])
                    mk_, mv_, rmk, rmv = mkTs, mvs, RMKTS, RMVS
                pts = [(pTb, RPT, rden, RRD), (pTb2, RPT2, rden2, RRD2)]

                def X1(h):
                    pT_, rpt = pts[h % 2][0], pts[h % 2][1]
                    for mt_ in range(2):
                        b = bank()
                        for dd in range(2):
                            P.mm(ps[b][:, 0:n], mk_[:, 2 * h + dd, mt_ * 128:(mt_ + 1) * 128], qx[:, 2 * h + dd, lo:lo + n], dd == 0, dd == 1, [rmk, RBIG], [RP[b]], dd == 1)
                        P.act(pT_[:, mt_, 0:n], ps[b][:, 0:n], AF.Exp, [RP[b]], [rpt], scale=1.0 / 16)

                def X2(h):
                    pT_, rpt, rd_, rrd = pts[h % 2]
                    bd = bank()
                    for mt_ in range(2):
                        P.mm(ps[bd][:, 0:n], onesb, pT_[:, mt_, 0:n], mt_ == 0, mt_ == 1, [rpt, RC], [RP[bd]], mt_ == 1)
                    P.act(rd_[:, 0:n], ps[bd][:, 0:n], AF.Ln, [RP[bd]], [rrd])
                    P.act(rd_[:, 0:n], rd_[:, 0:n], AF.Exp, [rrd], [rrd], scale=-1.0)
                    for dd in range(2):
                        b = bank()
                        for mt_ in range(2):
                            P.mm(ps[b][:, 0:n], mv_[:, mt_, (2 * h + dd) * 128:(2 * h + dd + 1) * 128], pT_[:, mt_, 0:n], mt_ == 0, mt_ == 1, [rmv, rpt], [RP[b]], mt_ == 1)
                        P.dve("tensor_tensor", [RP[b], rrd], [RH], out=ox[:, 2 * h + dd, lo:lo + n], in0=ps[b][:, 0:n], in1=rd_[:, 0:n], op=ALU.mult)

                for step in range(5):
                    if step < 4:
                        X1(step)
                    if step >= 1:
                        X2(step - 1)
            proj_to_stg(c, ("xo", l), w_xo[l], ox, RH)
            post_norm_add(c, gain(3, l))

        try:
            ck("setup")
            for l in range(nl):
                layer(l)
        except StopBuild as ex:
            print("STOPPED at", ex)

        for t in range(17 if STOP[0] is None else 0):
            rn = 128 if t < 16 else 32
            c = min(t // 4, 3)
            col = t * 128
            for hb in range(2):
                b = bank()
                for j in range(4):
                    k = hb * 4 + j
                    P.tr(ps[b][:rn, j * 128:(j + 1) * 128], xT[:, k, col:col + rn], ident, [RX[c], RC], [RP[b]], j == 3)
                P.act(stg[:rn, hb, 0:512], ps[b][:rn, :], AF.Copy, [RP[b]], [RST])
            dst = yp[t * 128:(t + 1) * 128, :] if t < 16 else ys
            spdma(dst.rearrange("p (a b) -> p a b", a=2), stg[:rn, 0:2, 0:512], [RST], [], "yout", out=True)
        P.emit()
    return nc


def _consts():
    c = np.zeros((128, C_END), np.float32)
    p = np.arange(128)[:, None]
    q = np.arange(128)[None, :]
    c[:, C_ID:C_ID + 128] = (p == q)
    c[:, C_TRI:C_TRI + 128] = (p >= q)
    c[:, C_ONE:C_ONE + 128] = 1.0
    c[:, C_MASK:C_MASK + 512] = (p < np.arange(512)[None, :])
    same = (p // 64) == (q // 64)
    c[:, C_TBD:C_TBD + 128] = np.where((p <= q) & same, 1.0 / 16.0, 0.0)
    c[:, C_MBD:C_MBD + 128] = ((p <= q) & same)
    inv = np.zeros((128, 2, 15), np.float32)
    for pp in range(128):
        for j in range(2):
            w = 2 ** (2 * j + (1 if pp >= 64 else 0) + 1)
            inv[pp, j] = 1.0 / np.minimum(np.arange(15) + 1, w)
    c[:, C_INV:C_INV + 30] = inv.reshape(128, 30)
    return c


_NC_CACHE = {}


def kernel(x_prompt, x_sample, mem_prompt, cache_sb_k, cache_sb_v, state_pool, state_gla,
           cache_mem_k, cache_mem_v, w_in, w_gla_a2, b_gla_a, gla_norm, w_pool, pool_scale,
           w_branch_a, w_branch_b, w_branch_c, w_mix_out, mem_norm, w_xq, w_xk, w_xv, w_xo,
           w_ffn_in, w_ffn_out, norm_mix_pre, norm_mix_post, norm_x_pre, norm_x_post,
           norm_ffn_pre, norm_ffn_post, _nl=NL):
    f = lambda a: np.ascontiguousarray(np.asarray(a, dtype=np.float32))
    fl = lambda a: np.ascontiguousarray(np.asarray(a, dtype=np.float32)[:_nl])
    if _nl not in _NC_CACHE:
        _NC_CACHE[_nl] = build(_nl)
    nc = _NC_CACHE[_nl]
    shared = dict(w_in=fl(w_in), w_gla_a2=fl(w_gla_a2), b_gla_a=fl(b_gla_a), gla_norm=fl(gla_norm), w_pool=fl(w_pool),
                  pool_scale=fl(pool_scale), w_branch_a=fl(w_branch_a), w_branch_b=fl(w_branch_b), w_branch_c=fl(w_branch_c),
                  w_mix_out=fl(w_mix_out), mem_norm=fl(mem_norm), w_xq=fl(w_xq), w_xk=fl(w_xk), w_xv=fl(w_xv), w_xo=fl(w_xo),
                  w_ffn_in=fl(w_ffn_in), w_ffn_out=fl(w_ffn_out), norm_mix_pre=fl(norm_mix_pre), norm_mix_post=fl(norm_mix_post),
                  norm_x_pre=fl(norm_x_pre), norm_x_post=fl(norm_x_post), norm_ffn_pre=fl(norm_ffn_pre), norm_ffn_post=fl(norm_ffn_post),
                  cst=_consts())
    x_prompt = f(x_prompt); x_sample = f(x_sample); mem_prompt = f(mem_prompt)
    cache_sb_k = fl(cache_sb_k); cache_sb_v = fl(cache_sb_v); state_pool = fl(state_pool); state_gla = fl(state_gla)
    cache_mem_k = fl(cache_mem_k); cache_mem_v = fl(cache_mem_v)
    in_maps = []
    for i in range(8):
        s2 = slice(2 * i, 2 * i + 2)
        d = dict(shared)
        d.update(xp=x_prompt[i], xs=np.ascontiguousarray(x_sample[s2].reshape(32, 1024)), mem=mem_prompt[i],
                 csk=np.ascontiguousarray(cache_sb_k[:, s2].reshape(_nl, 2, 1024, 384)),
                 csv=np.ascontiguousarray(cache_sb_v[:, s2].reshape(_nl, 2, 1024, 384)),
                 spool=np.ascontiguousarray(state_pool[:, s2]), sgla=np.ascontiguousarray(state_gla[:, s2]),
                 cmk=np.ascontiguousarray(cache_mem_k[:, s2].reshape(_nl, 2, 256, 1024)),
                 cmv=np.ascontiguousarray(cache_mem_v[:, s2].reshape(_nl, 2, 256, 1024)))
        in_maps.append(d)
    res = run_bass_kernel_spmd(nc, in_maps, core_ids=list(range(8)))
    R = res.results
    def cat(k, ax=0):
        a = np.stack([r[k] for r in R], axis=ax)
        if ax == 1 and a.shape[0] < 4:
            a = np.concatenate([a, np.zeros((4 - a.shape[0],) + a.shape[1:], a.dtype)], axis=0)
        return a
    y_p = cat("yp")
    y_s = cat("ys").reshape(16, 16, 1024)
    kp = cat("okp", 1).reshape(4, 8, 2048, 6, 64)
    vp = cat("ovp", 1).reshape(4, 8, 2048, 6, 64)
    pp = cat("opp", 1)
    gp = cat("ogp", 1)
    mk = cat("omk", 1).reshape(4, 8, 256, 4, 256)
    mv = cat("omv", 1).reshape(4, 8, 256, 4, 256)
    ks = cat("oks", 1).reshape(4, 16, 16, 6, 64)
    vs = cat("ovs", 1).reshape(4, 16, 16, 6, 64)
    pls = cat("ops", 1).reshape(4, 16, 15, 256)
    gs = cat("ogs", 1).reshape(4, 16, 4, 96, 96)
    return (y_p, y_s, kp, vp, pp, gp, mk, mv, ks, vs, pls, gs)
```

```python
import numpy as np
from contextlib import ExitStack
import concourse.bass as bass
import concourse.mybir as mybir
from concourse.bass_utils import run_bass_kernel_spmd

F32 = mybir.dt.float32
BF16 = mybir.dt.bfloat16
AF = mybir.ActivationFunctionType
ALU = mybir.AluOpType

NL = 4
EPS = 1e-6
NT = 2080
CW = 544
O_Q, O_K, O_V, O_U = 0, 384, 768, 1152
O_QC, O_KC, O_VC, O_GC, O_LR, O_G = 1408, 1792, 2176, 2560, 2944, 2960
C_ID, C_TRI, C_ONE, C_MASK, C_TBD, C_MBD, C_INV, C_END = 0, 128, 256, 384, 896, 1024, 1152, 1182


class Res:
    __slots__ = ("w", "r", "name")

    def __init__(self, name=""):
        self.w = None
        self.r = {}
        self.name = name


class PRes(Res):
    __slots__ = ()


class ARes(Res):
    __slots__ = ("lo", "hi")

    def __init__(self, name, lo, hi):
        Res.__init__(self, name)
        self.lo = lo
        self.hi = hi


class DmaSem:
    def __init__(self, sem):
        self.sem = sem
        self.count = 0


class Prog:
    def __init__(self, nc, es):
        self.nc = nc
        self.es = es
        self.engs = {"sp": nc.sync, "act": nc.scalar, "pool": nc.gpsimd, "dve": nc.vector, "pe": nc.tensor}
        self.q = {e: [] for e in self.engs}
        self.sem = {e: es.enter_context(nc.semaphore("S_" + e)) for e in ("act", "pool", "dve", "pe")}
        self.cnt = {e: 0 for e in self.sem}
        self.seen = {e: {} for e in self.engs}
        self.out_sems = []
        self.nsem = 0
        self.arena = []

    def dma_sem(self, out=False):
        self.nsem += 1
        s = DmaSem(self.es.enter_context(self.nc.semaphore("D%d" % self.nsem)))
        if out:
            self.out_sems.append(s)
        return s

    def op(self, eng, fn, reads=(), writes=(), inc=True, dma=None):
        waits = {}
        own = self.sem.get(eng)

        def need(ev):
            if ev is None:
                return
            s, v = ev
            if s is own and dma is None and (eng == "pe" or v > self.cnt[eng]):
                return
            k = id(s)
            if k not in waits or waits[k][1] < v:
                waits[k] = (s, v)

        pr = [r for r in reads if isinstance(r, PRes)]
        if pr:
            reads = [r for r in reads if not isinstance(r, PRes)]
            writes = list(writes) + pr
        for r in reads:
            need(r.w)
        for w in writes:
            need(w.w)
            for ev in w.r.values():
                need(ev)
            if isinstance(w, ARes):
                for o in self.arena:
                    if o is not w and o.lo < w.hi and w.lo < o.hi:
                        need(o.w)
                        for ev in o.r.values():
                            need(ev)
        wl = []
        for k, (s, v) in waits.items():
            if self.seen[eng].get(k, 0) < v:
                self.seen[eng][k] = v
                wl.append((s, v))
        if dma is not None:
            dma.count += 16
            ev = (dma.sem, dma.count)
            incv = 16
        else:
            if inc:
                self.cnt[eng] += 1
                ev = (self.sem[eng], self.cnt[eng])
            else:
                ev = (self.sem[eng], self.cnt[eng] + 1)
            incv = 1
        for r in reads:
            k = id(ev[0])
            if k not in r.r or r.r[k][1] < ev[1]:
                r.r[k] = ev
        for w in writes:
            w.w = ev
            w.r = {}
        self.q[eng].append((wl, fn, (ev[0], incv) if (inc or dma is not None) else None))

    def mm(self, out, lhsT, rhs, start, stop, reads, writes, inc, sgc=False):
        if sgc:
            self.op("pe", lambda e: e.matmul(out, lhsT=lhsT, rhs=rhs, start=start, stop=stop, skip_group_check=True), reads, writes, inc)
        else:
            self.op("pe", lambda e: e.matmul(out, lhsT=lhsT, rhs=rhs, start=start, stop=stop), reads, writes, inc)

    def tr(self, out, in_, ident, reads, writes, inc):
        self.op("pe", lambda e: e.transpose(out, in_, ident), reads, writes, inc)

    def act(self, out, in_, func, reads, writes, **kw):
        self.op("act", lambda e: e.activation(out=out, in_=in_, func=func, **kw), reads, writes)

    def dve(self, name, reads, writes, **kw):
        self.op("dve", lambda e: getattr(e, name)(**kw), reads, writes)

    def emit(self):
        nc = self.nc
        finals = [(s.sem, s.count) for s in self.out_sems if s.count > 0]
        with nc.Block() as block:
            def run(name):
                def f(eng):
                    for wl, fn, inc in self.q[name]:
                        for s, v in wl:
                            eng.wait_ge(s, v)
                        ins = fn(eng)
                        if inc is not None:
                            ins.then_inc(inc[0], inc[1])
                    if name == "sp":
                        for s, v in finals:
                            eng.wait_ge(s, v)
                return f

            block.sync(run("sp"))
            block.scalar(run("act"))
            block.gpsimd(run("pool"))
            block.vector(run("dve"))
            block.tensor(run("pe"))


def subs(c):
    return [(0, 512)] + ([(512, 32)] if c == 3 else [])


_LASTP = [None]


class StopBuild(Exception):
    pass


STOP = [None]
MARKS = []


def ck(name):
    if _LASTP[0] is not None:
        MARKS.append((name, len(_LASTP[0].q["pe"]), len(_LASTP[0].q["act"]), len(_LASTP[0].q["dve"])))
    if STOP[0] == name:
        raise StopBuild(name)


def build(nl=NL):
    nc = bass.Bass("TRN2", target_bir_lowering=False)
    es = ExitStack()

    def din(name, shape):
        return nc.dram_tensor(name, list(shape), F32, kind="ExternalInput").ap()

    def dout(name, shape):
        return nc.dram_tensor(name, list(shape), F32, kind="ExternalOutput").ap()

    xp = din("xp", [2048, 1024]); xs = din("xs", [32, 1024]); mem = din("mem", [256, 1024])
    csk = din("csk", [nl, 2, 1024, 384]); csv = din("csv", [nl, 2, 1024, 384])
    spool = din("spool", [nl, 2, 15, 256]); sgla = din("sgla", [nl, 2, 4, 96, 96])
    cmk = din("cmk", [nl, 2, 256, 1024]); cmv = din("cmv", [nl, 2, 256, 1024])
    w_in = din("w_in", [nl, 1024, 6032]); w_a2 = din("w_gla_a2", [nl, 16, 384]); b_a = din("b_gla_a", [nl, 384])
    gla_norm = din("gla_norm", [nl, 384]); w_pool = din("w_pool", [nl, 4, 64, 64]); pool_scale = din("pool_scale", [nl, 256])
    w_ba = din("w_branch_a", [nl, 384, 1024]); w_bb = din("w_branch_b", [nl, 256, 1024]); w_bc = din("w_branch_c", [nl, 384, 1024])
    w_mix = din("w_mix_out", [nl, 1024, 1024]); mem_norm = din("mem_norm", [nl, 1024])
    w_xq = din("w_xq", [nl, 1024, 1024]); w_xk = din("w_xk", [nl, 1024, 1024]); w_xv = din("w_xv", [nl, 1024, 1024]); w_xo = din("w_xo", [nl, 1024, 1024])
    w_fi = din("w_ffn_in", [nl, 1024, 5632]); w_fo = din("w_ffn_out", [nl, 2816, 1024])
    gnames = ["norm_mix_pre", "norm_mix_post", "norm_x_pre", "norm_x_post", "norm_ffn_pre", "norm_ffn_post"]
    gains = [din(n, [nl, 1024]) for n in gnames]
    cst = din("cst", [128, C_END])

    yp = dout("yp", [2048, 1024]); ys = dout("ys", [32, 1024])
    okp = dout("okp", [nl, 2048, 384]); ovp = dout("ovp", [nl, 2048, 384])
    opp = dout("opp", [nl, 15, 256]); ogp = dout("ogp", [nl, 4, 96, 96])
    omk = dout("omk", [nl, 256, 1024]); omv = dout("omv", [nl, 256, 1024])
    oks = dout("oks", [nl, 32, 384]); ovs = dout("ovs", [nl, 32, 384])
    ops_ = dout("ops", [nl, 2, 15, 256]); ogs = dout("ogs", [nl, 2, 4, 96, 96])

    with es:
        P = Prog(nc, es)
        _LASTP[0] = P

        def sb(name, shape, dt):
            return es.enter_context(nc.sbuf_tensor(name, shape, dt))

        xT = sb("xT", [128, 8, NT], F32)
        RX = [Res("x%d" % c) for c in range(4)]
        kT = sb("kT", [128, 3, NT], BF16); RKT = Res("kT")
        vtok = sb("vtok", [128, 16, 384], BF16); RVT = Res("vtok")
        vtoks = sb("vtoks", [16, 2, 384], BF16); RVTS = Res("vtoks")
        hT = sb("hT", [128, 8, CW], BF16); RH = Res("hT")
        rstd = sb("rstd", [128, CW], F32); RRS = Res("rstd")
        big = sb("big", [128, 11, CW], BF16); RBIG = Res("big")
        oaT = sb("oaT", [128, 3, CW], BF16); ROA = Res("oaT")
        obT = sb("obT", [128, 2, CW], BF16); ROB = Res("obT")
        ocT = sb("ocT", [96, 4, CW], BF16); ROC = Res("ocT")
        cstb = sb("cstb", [128, C_INV], BF16); RC = Res("cst")
        identf = sb("identf", [128, 128], F32)
        invcf = sb("invcf", [128, 30], F32)
        gall = sb("gall", [128, 7, 4, 8], F32)
        gln = sb("gln", [96, 4, 4], F32)
        psc = sb("psc", [128, 4, 2], F32)
        epsb = sb("epsb", [128, 1], F32)
        mkT = sb("mkT", [128, 8, 256], BF16); RMK = Res("mkT")
        mvt = sb("mvt", [128, 2, 1024], BF16); RMV = Res("mvt")
        wpbd = sb("wpbd", [128, 2, 128], BF16); RWP = Res("wpbd")
        lrT = sb("lrT", [32, CW], BF16); RLR = Res("lrT")
        wa2 = sb("wa2", [32, 384], BF16); RWA2 = Res("wa2")
        ebl = sb("ebl", [96, 4, 8], F32); REBL = Res("ebl")
        Sst = sb("Sst", [96, 4, 96], F32); RS = Res("S")
        uhalo = sb("uhalo", [128, 2, 15], F32); RUH = Res("uhalo")
        NS = 3
        SLOT = 4352
        ring = [sb("ring%d" % i, [128, SLOT], BF16) for i in range(NS)]
        RRING = [Res("ring%d" % i) for i in range(NS)]
        DRING = [P.dma_sem() for _ in range(NS)]
        AR = 40960
        arena = sb("arena", [128, AR // 2], BF16)
        print("sbuf remaining", nc.sbuf_bytes_remaining)

        def av(name, off, shape, dt):
            esz = 4 if dt == F32 else 2
            nel = 1
            for d in shape[1:]:
                nel *= d
            assert off % 4 == 0 and off + nel * esz <= AR, (name, off, nel * esz)
            a = arena[:shape[0], off // 2:off // 2 + nel * esz // 2]
            if dt == F32:
                a = a.bitcast(F32)
            if len(shape) == 3:
                a = a.rearrange("p (a b) -> p a b", a=shape[1])
            elif len(shape) == 4:
                a = a.rearrange("p (a b c) -> p a b c", a=shape[1], b=shape[2])
            r = ARes(name, off, off + nel * esz)
            P.arena.append(r)
            return a, r

        stg, RST = av("stg", 0, [128, 8, CW], F32)
        sqb, RSQ = av("sqb", 17408, [128, 8, CW], BF16)
        memT, RMEM = av("memT", 26112, [128, 8, 256], F32)
        mnT, RMN = av("mnT", 34304, [128, 8, 256], BF16)
        cstf, RCF = av("cstf", 0, [128, C_END], F32)
        qT, RQ = av("qT", 0, [128, 3, CW], BF16)
        ez, REZ, spb, RSP, enb, REN, wb, RWB = [], [], [], [], [], [], [], []
        for i in range(3):
            a, r = av("ez%d" % i, 3264 + 2048 * i, [128, 512], F32); ez.append(a); REZ.append(r)
            a, r = av("sp%d" % i, 9408 + 1024 * i, [128, 512], BF16); spb.append(a); RSP.append(r)
            a, r = av("wb%d" % i, 16576 + 1024 * i, [128, 512], BF16); wb.append(a); RWB.append(r)
        for i in range(2):
            a, r = av("en%d" % i, 12480 + 2048 * i, [128, 512], F32); enb.append(a); REN.append(r)
        Rb, RR = av("Rb", 19648, [128, 512], BF16)
        kcs, RKCS = av("kcs", 20672, [128, 8, 384], BF16)
        kcsT, RKCST = av("kcsT", 26816, [128, 3, 1024], BF16)
        vcs, RVCS = av("vcs", 32960, [128, 8, 384], BF16)
        PW = 15 + 512 + 62
        uT, RU = av("uT", 0, [128, 2, PW], F32)
        pA, RPA = av("pA", 4712, [128, 2, PW], F32)
        pB, RPB = av("pB", 9424, [128, 2, PW], F32)
        pC, RPC = av("pC", 14136, [128, PW], F32)
        pD, RPD = av("pD", 16492, [128, PW], F32)
        pooled, RPL = av("pooled", 18848, [128, 2, CW], BF16)
        ptmp, RPT_ = av("ptmp", 21024, [128, 16], F32)
        HWD = 288
        qtT, RQT = av("qtT", 0, [96, 4, HWD], BF16)
        ktT, RKTT_ = av("ktT", 2304, [96, 4, HWD], BF16)
        sgc, RSG = av("sgc", 4608, [96, 4, HWD], BF16)
        eq, REQ = av("eq", 6912, [96, 4, HWD], F32)
        eqi, REQI = av("eqi", 11520, [96, 4, HWD], BF16)
        vct, RVCT = av("vct", 13824, [128, 2, 384], BF16)
        lat, RLA = av("lat", 15360, [128, 2, 384], BF16)
        ktt, RKTT = av("ktt", 16896, [128, 2, 384], BF16)
        ekt, REKT = av("ekt", 18432, [128, 384], F32)
        gt1, RGT1 = av("gt1", 19968, [128, 512], F32)
        vcts, RVCTS = av("vcts", 22016, [16, 2, 384], BF16)
        lats, RLAS = av("lats", 23552, [16, 2, 384], BF16)
        ktts, RKTTS = av("ktts", 25088, [16, 2, 384], BF16)
        Sbf, RSBF = av("Sbf", 26624, [96, 6, 4, 96], BF16)
        attb, RATT = av("attb", 31232, [128, 4, 128], BF16)
        osb, ROS = av("osb", 32256, [96, 4, 128], F32)
        osq, ROSQ = av("osq", 34304, [96, 4, 128], BF16)
        grs, RGRS = av("grs", 35328, [96, 512], F32)
        attb2, RATT2 = av("attb2", 20480, [128, 4, 128], BF16)
        osb2, ROS2 = av("osb2", 18432, [96, 4, 128], F32)
        osq2, ROSQ2 = av("osq2", 15360, [96, 4, 128], BF16)
        grs2, RGRS2 = av("grs2", 16384, [96, 512], F32)
        Stm, RSTM = av("Stm", 37376, [96, 4, 96], F32)
        Ss, RSS = av("Ss", 38912, [96, 4, 96], F32)
        sg3, RSG3 = [], []
        for i in range(3):
            a, r = av("sg%d" % i, 26112 + 2048 * i, [128, 512], F32); sg3.append(a); RSG3.append(r)
        mt1, RMT1 = av("mt1", 32256, [128, 512], F32)
        mt2, RMT2 = av("mt2", 34304, [128, 512], F32)
        pTb, RPT = av("pTb", 26112, [128, 2, 512], BF16)
        rden, RRD = av("rden", 28160, [128, 512], F32)
        pTb2, RPT2 = av("pTb2", 17408, [128, 2, 512], BF16)
        rden2, RRD2 = av("rden2", 19456, [128, 512], F32)
        mkl, RMKL = av("mkl", 0, [128, 2, 1024], BF16)
        mkTs, RMKTS = av("mkTs", 30208, [128, 8, 256], BF16)
        mvs, RMVS = av("mvs", 34304, [128, 2, 1024], BF16)
        ft1, RFT1 = av("ft1", 26112, [128, 512], F32)

        ps = [es.enter_context(nc.psum_tensor("ps%d" % i, [128, 512], F32)) for i in range(8)]
        RP = [PRes("ps%d" % i) for i in range(8)]
        bank_i = [0]

        held = set()

        def bank(hold=False):
            while True:
                b = bank_i[0] % 8
                bank_i[0] += 1
                if b not in held:
                    break
            if hold:
                held.add(b)
            return b

        ring_i = [0]
        wres = {}
        slot_key = [None] * NS

        def get_w(key, loads):
            if key in wres:
                return wres[key]
            si = ring_i[0] % NS
            ring_i[0] += 1
            if slot_key[si] is not None:
                del wres[slot_key[si]]
            slot_key[si] = key
            wres[key] = si
            for dstf, src in loads:
                dst = dstf(ring[si])
                P.op("pool", lambda e, dst=dst, src=src: e.dma_start(out=dst, in_=src), writes=[RRING[si]], dma=DRING[si])
            return si

        def v3(kc, ncols, off=0):
            return lambda t: t[:, off:off + kc * ncols].rearrange("p (k c) -> p k c", k=kc)

        def rows(w2d):
            return w2d.rearrange("(k p) c -> p k c", p=128)

        dsm = {}
        P._dsm = dsm

        def dsem(name, out=False):
            if name not in dsm:
                dsm[name] = P.dma_sem(out=out)
            return dsm[name]

        def spdma(dst, src, reads, writes, name, out=False, nc_ok=False):
            if nc_ok:
                P.op("sp", lambda e: e.dma_start(out=dst, in_=src, allow_slow_non_contiguous=True), reads, writes, dma=dsem(name, out))
            else:
                P.op("sp", lambda e: e.dma_start(out=dst, in_=src), reads, writes, dma=dsem(name, out))

        def pooldma(dst, src, writes, name):
            P.op("pool", lambda e: e.dma_start(out=dst, in_=src), writes=writes, dma=dsem(name))

        ident = identf[:, :]
        identb = cstb[:, C_ID:C_ID + 128]
        trib = cstb[:, C_TRI:C_TRI + 128]
        onesb = cstb[:, C_ONE:C_ONE + 128]
        maskb = cstb[:, C_MASK:C_MASK + 512]
        tbd = cstb[:, C_TBD:C_TBD + 128]
        mbd = cstb[:, C_MBD:C_MBD + 128]
        invc = invcf[:, :].rearrange("p (j t) -> p j t", j=2)

        spdma(cstf[:, :], cst, [], [RCF], "cstf")
        P.act(cstb[:, :], cstf[:, 0:C_INV], AF.Copy, [RCF], [RC])
        P.act(identf[:, :], cstf[:, C_ID:C_ID + 128], AF.Copy, [RCF], [RC])
        P.act(invcf[:, :], cstf[:, C_INV:C_INV + 30], AF.Copy, [RCF], [RC])
        P.op("dve", lambda e: e.memset(epsb[:], EPS), writes=[RC])
        P.op("dve", lambda e: e.memset(lrT[:], 1.0), writes=[RLR])
        P.op("dve", lambda e: e.memset(wpbd[:], 0.0), writes=[RWP])
        for i, g in enumerate(gains + [mem_norm]):
            spdma(gall[:, i, 0:nl, :], g.rearrange("l (k p) -> p l k", p=128), [], [RC], "cst", nc_ok=True)
        spdma(gln[:, 0:nl, :], gla_norm.rearrange("l (h v) -> v l h", v=96), [], [RC], "cst", nc_ok=True)
        spdma(psc[:, 0:nl, :], pool_scale.rearrange("l (j p) -> p l j", p=128), [], [RC], "cst", nc_ok=True)

        def gain(i, l):
            return gall[:, i, l, :]

        def stats(src3, rsrc, lo, n, scale):
            b = bank()
            for k in range(8):
                P.mm(ps[b][:, 0:n], onesb, src3[:, k, lo:lo + n], k == 0, k == 7, [rsrc, RC], [RP[b]], k == 7)
            P.act(rstd[:, lo:lo + n], ps[b][:, 0:n], AF.Ln, [RP[b], RC], [RRS], scale=scale, bias=epsb[:, 0:1])
            P.act(rstd[:, lo:lo + n], rstd[:, lo:lo + n], AF.Exp, [RRS], [RRS], scale=-0.5)

        def norm_to_h(xv, rx, sbl, g):
            for lo, n in sbl:
                P.act(hT[:, :, lo:lo + n], xv[:, :, lo:lo + n], AF.Square, [rx], [RH])
                stats(hT, RH, lo, n, 1.0 / 1024)
                for k in range(8):
                    P.dve("scalar_tensor_tensor", [rx, RRS, RC], [RH], out=hT[:, k, lo:lo + n], in0=xv[:, k, lo:lo + n],
                          scalar=g[:, k:k + 1], in1=rstd[:, lo:lo + n], op0=ALU.mult, op1=ALU.mult)

        def post_norm_add(c, g):
            xv = xT[:, :, 512 * c:512 * c + CW]
            for lo, n in subs(c):
                stats(sqb, RSQ, lo, n, 1.0 / 1024)
                for k in range(8):
                    P.dve("tensor_tensor", [RST, RRS], [RST], out=stg[:, k, lo:lo + n], in0=stg[:, k, lo:lo + n], in1=rstd[:, lo:lo + n], op=ALU.mult)
                    P.dve("scalar_tensor_tensor", [RST, RC, RX[c]], [RX[c]], out=xv[:, k, lo:lo + n], in0=stg[:, k, lo:lo + n],
                          scalar=g[:, k:k + 1], in1=xv[:, k, lo:lo + n], op0=ALU.mult, op1=ALU.add)

        def proj_to_stg(c, key, wsrc2d, src3, rsrc):
            for hf in range(2):
                si = get_w((key, hf), [(v3(8, 512), rows(wsrc2d)[:, :, hf * 512:(hf + 1) * 512])])
                wv = v3(8, 512)(ring[si])
                for lo, n in subs(c):
                    for mm_ in range(4):
                        m = hf * 4 + mm_
                        b = bank()
                        for k in range(8):
                            P.mm(ps[b][:, 0:n], wv[:, k, mm_ * 128:(mm_ + 1) * 128], src3[:, k, lo:lo + n], k == 0, k == 7, [RRING[si], rsrc], [RP[b]], k == 7)
                        P.act(stg[:, m, lo:lo + n], ps[b][:, 0:n], AF.Copy, [RP[b]], [RST])
                        P.act(sqb[:, m, lo:lo + n], stg[:, m, lo:lo + n], AF.Square, [RST], [RSQ])

        for t in range(17):
            rn = 128 if t < 16 else 32
            src = xp[t * 128:(t + 1) * 128, :] if t < 16 else xs
            spdma(stg[:rn, 0:2, 0:512], src.rearrange("p (a b) -> p a b", a=2), [], [RST], "xin")
            c = min(t // 4, 3)
            col = t * 128
            for hb in range(2):
                b = bank()
                for j in range(4):
                    P.tr(ps[b][:, j * rn:(j + 1) * rn], stg[:rn, hb, j * 128:(j + 1) * 128], ident[:rn, :rn], [RST, RC], [RP[b]], j == 3)
                P.act(xT[:, hb * 4:(hb + 1) * 4, col:col + rn], ps[b][:, 0:4 * rn].rearrange("p (a b) -> p a b", a=4), AF.Copy, [RP[b]], [RX[c]])

        def layer(l):
            wr = rows(w_in[l])
            for t in range(2):
                spdma(stg[:, 0:2, 0:512], mem[t * 128:(t + 1) * 128, :].rearrange("p (a b) -> p a b", a=2), [], [RST], "xin")
                for hb in range(2):
                    b = bank()
                    for j in range(4):
                        P.tr(ps[b][:, j * 128:(j + 1) * 128], stg[:, hb, j * 128:(j + 1) * 128], ident, [RST, RC], [RP[b]], j == 3)
                    P.act(memT[:, hb * 4:(hb + 1) * 4, t * 128:(t + 1) * 128], ps[b][:, 0:512].rearrange("p (a b) -> p a b", a=4), AF.Copy, [RP[b]], [RMEM])
            ck("mem1")
            P.act(mnT[:, :, :], memT[:, :, :], AF.Square, [RMEM], [RMN])
            stats(mnT, RMN, 0, 256, 1.0 / 1024)
            for k in range(8):
                P.dve("scalar_tensor_tensor", [RMEM, RC, RRS], [RMN], out=mnT[:, k, :], in0=memT[:, k, :], scalar=gain(6, l)[:, k:k + 1],
                      in1=rstd[:, 0:256], op0=ALU.mult, op1=ALU.mult)
            ck("mem2")
            for which, wsrc, odram in ((0, w_xk, omk), (1, w_xv, omv)):
                if which == 1:
                    ck("mem3")
                for hf in range(2):
                    si = get_w(("xkv", which, hf), [(v3(8, 512), rows(wsrc[l])[:, :, hf * 512:(hf + 1) * 512])])
                    wv = v3(8, 512)(ring[si])
                    if which == 0:
                        for jj in range(4):
                            b = bank()
                            for k in range(8):
                                P.mm(ps[b][:, 0:256], wv[:, k, jj * 128:(jj + 1) * 128], mnT[:, k, :], k == 0, k == 7, [RRING[si], RMN], [RP[b]], k == 7)
                            P.act(mkT[:, hf * 4 + jj, :], ps[b][:, 0:256], AF.Copy, [RP[b]], [RMK])
                    for t in range(2):
                        b = bank()
                        for k in range(8):
                            P.mm(ps[b][:, :], mnT[:, k, t * 128:(t + 1) * 128], wv[:, k, :], k == 0, k == 7, [RRING[si], RMN], [RP[b]], k == 7)
                        P.act(stg[:, t, 0:512], ps[b][:, :], AF.Copy, [RP[b]], [RST])
                        if which == 1:
                            P.dve("tensor_copy", [RP[b]], [RMV], out=mvt[:, t, hf * 512:(hf + 1) * 512], in_=ps[b][:, :])
                    spdma(odram[l, :, hf * 512:(hf + 1) * 512].rearrange("(t p) f -> p t f", p=128), stg[:, 0:2, 0:512], [RST], [], "omem", out=True)
            ck("memkv")
            for c in range(4):
                xv = xT[:, :, 512 * c:512 * c + CW]
                norm_to_h(xv, RX[c], subs(c), gain(0, l))
                sk = get_w(("wk", l), [(v3(8, 384), wr[:, :, O_K:O_K + 384])])
                wk = v3(8, 384)(ring[sk])
                for lo, n in subs(c):
                    for j in range(3):
                        b = bank()
                        for k in range(8):
                            P.mm(ps[b][:, 0:n], wk[:, k, j * 128:(j + 1) * 128], hT[:, k, lo:lo + n], k == 0, k == 7, [RRING[sk], RH], [RP[b]], k == 7)
                        P.act(kT[:, j, 512 * c + lo:512 * c + lo + n], ps[b][:, 0:n], AF.Copy, [RP[b]], [RKT])
                sv_ = get_w(("wv", l), [(v3(8, 384), wr[:, :, O_V:O_V + 384])])
                wvv = v3(8, 384)(ring[sv_])
                tiles = [(t * 128, 128, 4 * c + t) for t in range(4)]
                if c == 3:
                    tiles += [(512, 16, 16), (528, 16, 17)]
                for lo, n, gt in tiles:
                    for which, wsl, rsl in ((0, wk, sk), (1, wvv, sv_)):
                        b = bank()
                        for k in range(8):
                            P.mm(ps[b][:n, 0:384], hT[:, k, lo:lo + n], wsl[:, k, :], k == 0, k == 7, [RRING[rsl], RH], [RP[b]], k == 7)
                        P.act(stg[:n, which, 0:384], ps[b][:n, 0:384], AF.Copy, [RP[b]], [RST])
                        if which == 1:
                            if gt < 16:
                                P.dve("tensor_copy", [RP[b]], [RVT], out=vtok[:, gt, :], in_=ps[b][:, 0:384])
                            else:
                                P.dve("tensor_copy", [RP[b]], [RVTS], out=vtoks[:, gt - 16, :], in_=ps[b][:16, 0:384])
                    if gt < 16:
                        spdma(okp[l, gt * 128:(gt + 1) * 128, :], stg[:, 0, 0:384], [RST], [], "okv", out=True)
                        spdma(ovp[l, gt * 128:(gt + 1) * 128, :], stg[:, 1, 0:384], [RST], [], "okv", out=True)
                    else:
                        bb = gt - 16
                        spdma(oks[l, bb * 16:(bb + 1) * 16, :], stg[:16, 0, 0:384], [RST], [], "okv", out=True)
                        spdma(ovs[l, bb * 16:(bb + 1) * 16, :], stg[:16, 1, 0:384], [RST], [], "okv", out=True)
            ck("prekv")
            for g4 in range(4):
                j, hh = g4 // 2, g4 % 2
                pooldma(wpbd[hh * 64:(hh + 1) * 64, j, hh * 64:(hh + 1) * 64], w_pool[l, g4], [RWP], "wp")
            pooldma(wa2[0:16, :], w_a2[l], [RWA2], "wa2")
            pooldma(wa2[16:17, :], b_a[l:l + 1, :], [RWA2], "wa2")
            P.op("dve", lambda e: e.memset(Sst[:], 0.0), writes=[RS])
            for c in range(4):
                chunk(l, c, wr)

        def chunk(l, c, wr):
            xv = xT[:, :, 512 * c:512 * c + CW]
            sbl = subs(c)
            norm_to_h(xv, RX[c], sbl, gain(0, l))
            si = get_w(("wq", l), [(v3(8, 384), wr[:, :, O_Q:O_Q + 384])])
            wq = v3(8, 384)(ring[si])
            for lo, n in sbl:
                for j in range(3):
                    b = bank()
                    for k in range(8):
                        P.mm(ps[b][:, 0:n], wq[:, k, j * 128:(j + 1) * 128], hT[:, k, lo:lo + n], k == 0, k == 7, [RRING[si], RH], [RP[b]], k == 7)
                    P.act(qT[:, j, lo:lo + n], ps[b][:, 0:n], AF.Copy, [RP[b]], [RQ])
            ck("q")
            nkt = 4 * c + 4
            kts = list(range(nkt - 1, -1, -1))
            for h in range(6):
                j, pb = h // 2, 64 * (h % 2)
                bo = bank(hold=True)
                st = {}

                def S1(ti):
                    kt = kts[ti]
                    i = kt - 4 * c
                    c0 = 128 * i if i > 0 else 0
                    ncl = 512 - c0
                    r = ti % 3
                    bz = bank()
                    P.mm(ps[bz][:, 0:ncl], kT[pb:pb + 64, j, kt * 128:(kt + 1) * 128], qT[pb:pb + 64, j, c0:512], True, True, [RKT, RQ], [RP[bz]], True)
                    P.act(ez[r][:, 0:ncl], ps[bz][:, 0:ncl], AF.Exp, [RP[bz]], [REZ[r]], scale=0.125)
                    if i >= 0:
                        P.dve("tensor_tensor", [REZ[r], RC], [REZ[r]], out=ez[r][:, 0:ncl], in0=ez[r][:, 0:ncl], in1=maskb[:, 0:ncl], op=ALU.mult)
                    P.act(spb[r][:, 0:ncl], ez[r][:, 0:ncl], AF.Ln, [REZ[r]], [RSP[r]], bias=1.0)
                    st[ti] = (kt, c0, ncl, r)

                def S2(ti):
                    kt, c0, ncl, r = st[ti]
                    r2 = ti % 2
                    bc = bank()
                    lastk = (kt == nkt - 1)
                    P.mm(ps[bc][:, 0:ncl], trib, spb[r][:, 0:ncl], True, lastk, [RSP[r], RC], [RP[bc]], lastk)
                    if not lastk:
                        P.mm(ps[bc][:, 0:ncl], onesb, Rb[:, c0:512], False, True, [RR, RC], [RP[bc]], True)
                    P.act(enb[r2][:, 0:ncl], ps[bc][:, 0:ncl], AF.Exp, [RP[bc]], [REN[r2]], scale=-1.0)
                    P.dve("tensor_tensor", [REZ[r], REN[r2]], [RWB[r]], out=wb[r][:, 0:ncl], in0=ez[r][:, 0:ncl], in1=enb[r2][:, 0:ncl], op=ALU.mult)
                    if kt > 0:
                        if lastk:
                            P.op("dve", lambda e: e.memset(Rb[:, 0:384], 0.0), writes=[RR])
                            P.dve("tensor_copy", [RSP[r]], [RR], out=Rb[:, c0:512], in_=spb[r][:, 0:ncl])
                        else:
                            P.dve("tensor_tensor", [RSP[r], RR], [RR], out=Rb[:, c0:512], in0=Rb[:, c0:512], in1=spb[r][:, 0:ncl], op=ALU.add)

                def S3(ti):
                    kt, c0, ncl, r = st[ti]
                    P.mm(ps[bo][:, c0:512], vtok[:, kt, j * 128:(j + 1) * 128], wb[r][:, 0:ncl], kt == nkt - 1, kt == 0, [RVT, RWB[r]], [RP[bo]], kt == 0, sgc=True)

                nt_ = len(kts)
                for step in range(nt_ + 2):
                    if step < nt_:
                        S1(step)
                    if 0 <= step - 1 < nt_:
                        S2(step - 1)
                    if 0 <= step - 2 < nt_:
                        S3(step - 2)
                P.act(oaT[pb:pb + 64, j, 0:512], ps[bo][pb:pb + 64, 0:512], AF.Copy, [RP[bo]], [ROA])
                held.discard(bo)
            ck("sb%d" % c)
            if c == 3:
                for b_ in range(2):
                    sb_sample(l, b_)
            ck("sbs%d" % c)
            pool_branch(l, c, wr)
            ck("pool%d" % c)
            for hf in range(2):
                gla_half(l, c, hf, wr)
            ck("gla%d" % c)
            merged = big[:, 0:8, :]
            if c == 3:
                for m in range(8):
                    loads = []
                    for br in range(3):
                        loads.append((lambda t, br=br: t[:, 0:3072].rearrange("p (k r c) -> p k r c", k=8, r=3)[:, :, br, :],
                                      wr[:, :, O_G + br * 1024 + m * 128:O_G + br * 1024 + (m + 1) * 128]))
                    loads.append((lambda t: t[:, 3072:3456].rearrange("p (k c) -> p k c", k=3), rows(w_ba[l])[:, :, m * 128:(m + 1) * 128]))
                    loads.append((lambda t: t[:, 3456:3712].rearrange("p (k c) -> p k c", k=2), rows(w_bb[l])[:, :, m * 128:(m + 1) * 128]))
                    loads.append((lambda t: t[:96, 3712:4224].rearrange("p (k c) -> p k c", k=4), w_bc[l].rearrange("(h v) c -> v h c", v=96)[:, :, m * 128:(m + 1) * 128]))
                    si = get_w(("mrg", l, m), loads)
                    gv = ring[si][:, 0:3072].rearrange("p (k r c) -> p k r c", k=8, r=3)
                    a_v = ring[si][:, 3072:3456].rearrange("p (k c) -> p k c", k=3)
                    b_v = ring[si][:, 3456:3712].rearrange("p (k c) -> p k c", k=2)
                    c_v = ring[si][:96, 3712:4224].rearrange("p (k c) -> p k c", k=4)
                    for lo, n in sbl:
                        bg = [bank() for _ in range(3)]
                        bb = [bank() for _ in range(3)]
                        for br in range(3):
                            for k in range(8):
                                P.mm(ps[bg[br]][:, 0:n], gv[:, k, br, :], hT[:, k, lo:lo + n], k == 0, k == 7, [RRING[si], RH], [RP[bg[br]]], k == 7)
                        for k in range(3):
                            P.mm(ps[bb[0]][:, 0:n], a_v[:, k, :], oaT[:, k, lo:lo + n], k == 0, k == 2, [RRING[si], ROA], [RP[bb[0]]], k == 2)
                        for k in range(2):
                            P.mm(ps[bb[1]][:, 0:n], b_v[:, k, :], obT[:, k, lo:lo + n], k == 0, k == 1, [RRING[si], ROB], [RP[bb[1]]], k == 1)
                        for k in range(4):
                            P.mm(ps[bb[2]][:, 0:n], c_v[:, k, :], ocT[:, k, lo:lo + n], k == 0, k == 3, [RRING[si], ROC], [RP[bb[2]]], k == 3)
                        for br in range(3):
                            P.act(sg3[br][:, 0:n], ps[bg[br]][:, 0:n], AF.Sigmoid, [RP[bg[br]]], [RSG3[br]])
                        P.dve("tensor_tensor", [RSG3[0], RP[bb[0]]], [RMT1], out=mt1[:, 0:n], in0=sg3[0][:, 0:n], in1=ps[bb[0]][:, 0:n], op=ALU.mult)
                        P.dve("tensor_tensor", [RSG3[1], RP[bb[1]]], [RMT2], out=mt2[:, 0:n], in0=sg3[1][:, 0:n], in1=ps[bb[1]][:, 0:n], op=ALU.mult)
                        P.dve("tensor_tensor", [RMT1, RMT2], [RMT1], out=mt1[:, 0:n], in0=mt1[:, 0:n], in1=mt2[:, 0:n], op=ALU.add)
                        P.dve("tensor_tensor", [RSG3[2], RP[bb[2]]], [RMT2], out=mt2[:, 0:n], in0=sg3[2][:, 0:n], in1=ps[bb[2]][:, 0:n], op=ALU.mult)
                        P.dve("tensor_tensor", [RMT1, RMT2], [RBIG], out=merged[:, m, lo:lo + n], in0=mt1[:, 0:n], in1=mt2[:, 0:n], op=ALU.add)
            else:
                brs = [(w_ba, 3, 128, oaT, ROA, lambda t: t[:, 0:3072].rearrange("p (k c) -> p k c", k=3), lambda w: rows(w)),
                       (w_bb, 2, 128, obT, ROB, lambda t: t[:, 0:2048].rearrange("p (k c) -> p k c", k=2), lambda w: rows(w)),
                       (w_bc, 4, 96, ocT, ROC, lambda t: t[:96, 0:4096].rearrange("p (k c) -> p k c", k=4), lambda w: w.rearrange("(h v) c -> v h c", v=96))]
                for br, (wb_, kc_, kp_, src_, rsrc_, vf_, rf_) in enumerate(brs):
                    for hf in range(2):
                        sb_i = get_w(("mb", l, br), [(vf_, rf_(wb_[l]))])
                        wbv = vf_(ring[sb_i])
                        sg_i = get_w(("mg", l, br, hf), [(v3(8, 512), wr[:, :, O_G + br * 1024 + hf * 512:O_G + br * 1024 + (hf + 1) * 512])])
                        gvw = v3(8, 512)(ring[sg_i])
                        for mm_ in range(4):
                            m = hf * 4 + mm_
                            for lo, n in sbl:
                                bg, bb = bank(), bank()
                                for k in range(8):
                                    P.mm(ps[bg][:, 0:n], gvw[:, k, mm_ * 128:(mm_ + 1) * 128], hT[:, k, lo:lo + n], k == 0, k == 7, [RRING[sg_i], RH], [RP[bg]], k == 7)
                                for k in range(kc_):
                                    P.mm(ps[bb][:, 0:n], wbv[:kp_, k, m * 128:(m + 1) * 128], src_[:kp_, k, lo:lo + n], k == 0, k == kc_ - 1, [RRING[sb_i], rsrc_], [RP[bb]], k == kc_ - 1)
                                sgb = sg3[(m + br) % 3]
                                rsg = RSG3[(m + br) % 3]
                                P.act(sgb[:, 0:n], ps[bg][:, 0:n], AF.Sigmoid, [RP[bg]], [rsg])
                                if br == 0:
                                    P.dve("tensor_tensor", [rsg, RP[bb]], [RST], out=stg[:, m, lo:lo + n], in0=sgb[:, 0:n], in1=ps[bb][:, 0:n], op=ALU.mult)
                                else:
                                    mt_, rmt_ = (mt1, RMT1) if (m % 2 == 0) else (mt2, RMT2)
                                    P.dve("tensor_tensor", [rsg, RP[bb]], [rmt_], out=mt_[:, 0:n], in0=sgb[:, 0:n], in1=ps[bb][:, 0:n], op=ALU.mult)
                                    if br == 1:
                                        P.dve("tensor_tensor", [rmt_, RST], [RST], out=stg[:, m, lo:lo + n], in0=stg[:, m, lo:lo + n], in1=mt_[:, 0:n], op=ALU.add)
                                    else:
                                        P.dve("tensor_tensor", [rmt_, RST], [RBIG], out=merged[:, m, lo:lo + n], in0=stg[:, m, lo:lo + n], in1=mt_[:, 0:n], op=ALU.add)
            proj_to_stg(c, ("mix", l), w_mix[l], merged, RBIG)
            post_norm_add(c, gain(1, l))
            ck("mix%d" % c)
            xattn(l, c)
            ck("xattn%d" % c)
            norm_to_h(xv, RX[c], sbl, gain(4, l))
            hid = big
            wfr = rows(w_fi[l])
            wor = rows(w_fo[l])
            for jh in range(2):
                for j0, nj in ((0, 2), (2, 2), (4, 2), (6, 2), (8, 2), (10, 1)):
                    jg = jh * 11 + j0
                    si = get_w(("fi", l, jg), [
                        (lambda t, nj=nj: t[:, 0:16 * nj * 128].rearrange("p (k r c) -> p k r c", k=8, r=2)[:, :, 0, :], wfr[:, :, jg * 128:(jg + nj) * 128]),
                        (lambda t, nj=nj: t[:, 0:16 * nj * 128].rearrange("p (k r c) -> p k r c", k=8, r=2)[:, :, 1, :], wfr[:, :, 2816 + jg * 128:2816 + (jg + nj) * 128])])
                    wv = ring[si][:, 0:16 * nj * 128].rearrange("p (k r c) -> p k r c", k=8, r=2)
                    for jj in range(nj):
                        for lo, n in sbl:
                            b1, b2 = bank(), bank()
                            for k in range(8):
                                P.mm(ps[b1][:, 0:n], wv[:, k, 0, jj * 128:(jj + 1) * 128], hT[:, k, lo:lo + n], k == 0, k == 7, [RRING[si], RH], [RP[b1]], k == 7)
                            for k in range(8):
                                P.mm(ps[b2][:, 0:n], wv[:, k, 1, jj * 128:(jj + 1) * 128], hT[:, k, lo:lo + n], k == 0, k == 7, [RRING[si], RH], [RP[b2]], k == 7)
                            P.act(ft1[:, 0:n], ps[b1][:, 0:n], AF.Silu, [RP[b1]], [RFT1])
                            P.dve("tensor_tensor", [RFT1, RP[b2]], [RBIG], out=hid[:, j0 + jj, lo:lo + n], in0=ft1[:, 0:n], in1=ps[b2][:, 0:n], op=ALU.mult)
                for mb in range(4):
                    si = get_w(("fo", l, jh, mb), [(v3(11, 256), wor[:, jh * 11:(jh + 1) * 11, mb * 256:(mb + 1) * 256])])
                    wv = v3(11, 256)(ring[si])
                    for lo, n in sbl:
                        for mm_ in range(2):
                            m = mb * 2 + mm_
                            b = bank()
                            for k in range(11):
                                P.mm(ps[b][:, 0:n], wv[:, k, mm_ * 128:(mm_ + 1) * 128], hid[:, k, lo:lo + n], k == 0, k == 10, [RRING[si], RBIG], [RP[b]], k == 10)
                            if jh == 0:
                                P.act(stg[:, m, lo:lo + n], ps[b][:, 0:n], AF.Copy, [RP[b]], [RST])
                            else:
                                P.dve("tensor_tensor", [RST, RP[b]], [RST], out=stg[:, m, lo:lo + n], in0=stg[:, m, lo:lo + n], in1=ps[b][:, 0:n], op=ALU.add)
                                P.act(sqb[:, m, lo:lo + n], stg[:, m, lo:lo + n], AF.Square, [RST], [RSQ])
            post_norm_add(c, gain(5, l))
            ck("ffn%d" % c)

        def sb_sample(l, b_):
            pooldma(kcs[:, :, :], csk[l, b_].rearrange("(t p) f -> p t f", p=128), [RKCS], "kcs")
            pooldma(vcs[:, :, :], csv[l, b_].rearrange("(t p) f -> p t f", p=128), [RVCS], "vcs")
            for t in range(8):
                bk = bank()
                pv = ps[bk][:, :].bitcast(BF16)
                for j in range(3):
                    P.tr(pv[:, j * 128:(j + 1) * 128], kcs[:, t, j * 128:(j + 1) * 128], identb, [RKCS, RC], [RP[bk]], j == 2)
                P.act(kcsT[:, :, t * 128:(t + 1) * 128], pv[:, 0:384].rearrange("p (a b) -> p a b", a=3), AF.Copy, [RP[bk]], [RKCST])
            ck("ss1")
            bo = bank(hold=True)
            qc0 = 512 + 16 * b_
            NCOL = 96
            Rn = Rb[:16, 256:256 + NCOL]
            for kt in range(8, -1, -1):
                np_ = 16 if kt == 8 else 128
                bzs = [bank(), bank()]
                for par in range(2):
                    for hh in range(3):
                        h = 2 * hh + par
                        j, pb = h // 2, 64 * (h % 2)
                        qv = qT[pb:pb + 64, j, qc0:qc0 + 16]
                        if kt == 8:
                            kv = kT[pb:pb + 64, j, 2048 + 16 * b_:2048 + 16 * b_ + 16]
                        else:
                            kv = kcsT[pb:pb + 64, j, kt * 128:(kt + 1) * 128]
                        P.mm(ps[bzs[par]][:np_, hh * 16:hh * 16 + 16], kv, qv, True, True, [RKT, RKCST, RQ], [RP[bzs[par]]], hh == 2)
                if kt == 7:
                    ck("ss2")
                if kt == 8:
                    ck("ss3")
                r = 0
                for par in range(2):
                    P.act(ez[r][:np_, par * 48:par * 48 + 48], ps[bzs[par]][:np_, 0:48], AF.Exp, [RP[bzs[par]]], [REZ[r]], scale=0.125)
                if kt == 8:
                    for h in range(6):
                        P.dve("tensor_tensor", [REZ[r], RC], [REZ[r]], out=ez[r][:16, h * 16:h * 16 + 16], in0=ez[r][:16, h * 16:h * 16 + 16],
                              in1=maskb[:16, 0:16], op=ALU.mult)
                P.act(spb[r][:np_, 0:NCOL], ez[r][:np_, 0:NCOL], AF.Ln, [REZ[r]], [RSP[r]], bias=1.0)
                bc = bank()
                P.mm(ps[bc][:np_, 0:NCOL], trib[:np_, :np_], spb[r][:np_, 0:NCOL], True, kt == 8, [RSP[r], RC], [RP[bc]], kt == 8)
                if kt <= 7:
                    P.mm(ps[bc][:, 0:NCOL], onesb[:16, :], Rn, False, kt == 7, [RR, RC], [RP[bc]], kt == 7)
                if kt < 7:
                    P.mm(ps[bc][:, 0:NCOL], onesb, Rb[:, 0:NCOL], False, True, [RR, RC], [RP[bc]], True)
                ck("ss4")
                P.act(enb[r][:np_, 0:NCOL], ps[bc][:np_, 0:NCOL], AF.Exp, [RP[bc]], [REN[r]], scale=-1.0)
                P.dve("tensor_tensor", [REZ[r], REN[r]], [RWB[r]], out=wb[r][:np_, 0:NCOL], in0=ez[r][:np_, 0:NCOL], in1=enb[r][:np_, 0:NCOL], op=ALU.mult)
                if kt == 8:
                    P.dve("tensor_copy", [RSP[r]], [RR], out=Rn, in_=spb[r][:16, 0:NCOL])
                elif kt == 7:
                    P.dve("tensor_copy", [RSP[r]], [RR], out=Rb[:, 0:NCOL], in_=spb[r][:, 0:NCOL])
                elif kt > 0:
                    P.dve("tensor_tensor", [RSP[r], RR], [RR], out=Rb[:, 0:NCOL], in0=Rb[:, 0:NCOL], in1=spb[r][:, 0:NCOL], op=ALU.add)
                ck("ss5")
                for h in range(6):
                    j = h // 2
                    if kt == 8:
                        lv = vtoks[:16, b_, j * 128:(j + 1) * 128]
                    else:
                        lv = vcs[:, kt, j * 128:(j + 1) * 128]
                    hc = (h % 2) * 48 + (h // 2) * 16
                    P.mm(ps[bo][:, hc:hc + 16], lv, wb[r][:np_, hc:hc + 16], (kt == 8 and h == 0), kt == 0, [RVTS, RVCS, RWB[r]], [RP[bo]], (h == 5 and kt == 0), sgc=True)
            for h in range(6):
                j, pb = h // 2, 64 * (h % 2)
                hc = (h % 2) * 48 + (h // 2) * 16
                P.act(oaT[pb:pb + 64, j, qc0:qc0 + 16], ps[bo][pb:pb + 64, hc:hc + 16], AF.Copy, [RP[bo]], [ROA])
            held.discard(bo)

        def pool_branch(l, c, wr):
            si = get_w(("wu", l), [(v3(8, 256), wr[:, :, O_U:O_U + 256])])
            wu = v3(8, 256)(ring[si])
            W = 527 if c < 3 else PW
            if c > 0:
                P.act(uT[:, :, 0:15], uhalo[:, :, :], AF.Copy, [RUH], [RU])
            else:
                P.op("dve", lambda e: e.memset(uT[:, :, 0:15], 0.0), writes=[RU])
            segs = [(0, 512, 15)]
            if c == 3:
                segs += [(512, 16, 527 + 15), (528, 16, 527 + 31 + 15)]
                for b_ in range(2):
                    off = 527 + 31 * b_
                    for j in range(2):
                        spdma(uT[:, j, off:off + 15], spool[l, b_, :, j * 128:(j + 1) * 128].rearrange("t p -> p t"), [], [RU], "hist", nc_ok=True)
            for lo, n, dst in segs:
                for j in range(2):
                    b = bank()
                    for k in range(8):
                        P.mm(ps[b][:, 0:n], wu[:, k, j * 128:(j + 1) * 128], hT[:, k, lo:lo + n], k == 0, k == 7, [RRING[si], RH], [RP[b]], k == 7)
                    P.act(uT[:, j, dst:dst + n], ps[b][:, 0:n], AF.Copy, [RP[b]], [RU])
            P.act(uhalo[:, :, :], uT[:, :, 512:527], AF.Copy, [RU], [RUH])
            if c == 3:
                outs = [(512 - 15, opp[l]), (513, ops_[l, 0]), (529, ops_[l, 1])]
                for lo, od in outs:
                    b = bank()
                    for k in range(8):
                        P.mm(ps[b][:15, 0:256], hT[:, k, lo:lo + 15], wu[:, k, :], k == 0, k == 7, [RRING[si], RH], [RP[b]], k == 7)
                    P.act(pD[:15, 0:256], ps[b][:15, 0:256], AF.Copy, [RP[b]], [RPD])
                    spdma(od, pD[:15, 0:256], [RPD], [], "opool", out=True)
            P.dve("tensor_tensor", [RU], [RPA], out=pA[:, :, 1:W], in0=uT[:, :, 1:W], in1=uT[:, :, 0:W - 1], op=ALU.add)
            P.dve("tensor_tensor", [RPA], [RPB], out=pB[:, :, 3:W], in0=pA[:, :, 3:W], in1=pA[:, :, 1:W - 2], op=ALU.add)
            P.dve("tensor_tensor", [RPB], [RPC], out=pC[:, 7:W], in0=pB[:, 1, 7:W], in1=pB[:, 1, 3:W - 4], op=ALU.add)
            P.dve("tensor_tensor", [RPC], [RPD], out=pD[:, 15:W], in0=pC[:, 15:W], in1=pC[:, 7:W - 8], op=ALU.add)
            sel = [(0, 64, 0, pA[0:64, 0, :], 0.5, RPA), (64, 128, 0, pB[64:128, 0, :], 0.25, RPB), (0, 64, 1, pC[0:64, :], 0.125, RPC), (64, 128, 1, pD[64:128, :], 0.0625, RPD)]
            for p0, p1, j, sv, iw, rsv in sel:
                for lo, n, src in segs:
                    P.dve("scalar_tensor_tensor", [rsv, RU], [RPL], out=pooled[p0:p1, j, lo:lo + n], in0=sv[:, src:src + n], scalar=iw,
                          in1=uT[p0:p1, j, src:src + n], op0=ALU.mult, op1=ALU.subtract)
                if c == 0:
                    P.dve("tensor_tensor", [rsv, RC], [RPT_], out=ptmp[p0:p1, 0:15], in0=sv[:, 15:30], in1=invc[p0:p1, j, :], op=ALU.mult)
                    P.dve("tensor_tensor", [RPT_, RU], [RPL], out=pooled[p0:p1, j, 0:15], in0=ptmp[p0:p1, 0:15], in1=uT[p0:p1, j, 15:30], op=ALU.subtract)
            for lo, n in subs(c):
                for j in range(2):
                    b = bank()
                    P.mm(ps[b][:, 0:n], wpbd[:, j, :], pooled[:, j, lo:lo + n], True, True, [RWP, RPL], [RP[b]], True)
                    P.act(obT[:, j, lo:lo + n], ps[b][:, 0:n], AF.Copy, [RP[b], RC], [ROB], scale=psc[:, l, j:j + 1])

        def gla_half(l, c, hf, wr):
            base = 256 * hf
            sbl = [(base, 256, 0)]
            tiles = [(base, 128, 0, 0), (base + 128, 128, 1, 128)]
            samp = (c == 3 and hf == 1)
            if samp:
                sbl += [(512, 32, 256)]
                tiles += [(512, 16, 2, 256), (528, 16, 3, 272)]
            ncl = 288 if samp else 256

            def wG():
                return get_w(("gG", l), [(v3(8, 400), wr[:, :, O_GC:O_GC + 400])])

            def wK():
                return get_w(("gK", l), [(v3(8, 384), wr[:, :, O_KC:O_KC + 384])])

            def wV():
                return get_w(("gV", l), [(v3(8, 384), wr[:, :, O_VC:O_VC + 384])])

            def wQ():
                return get_w(("gQ", l), [(v3(8, 384), wr[:, :, O_QC:O_QC + 384])])

            s = wG(); w = v3(8, 400)(ring[s])
            for lo, n, lc in sbl:
                b = bank()
                for k in range(8):
                    P.mm(ps[b][:16, 0:n], w[:, k, 384:400], hT[:, k, lo:lo + n], k == 0, k == 7, [RRING[s], RH], [RP[b]], k == 7)
                P.act(lrT[:16, lo:lo + n], ps[b][:16, 0:n], AF.Copy, [RP[b]], [RLR])
            ck("g1")
            for lo, n, ti, lc in tiles:
                sm = ti >= 2
                bi = ti - 2
                vc_d = vcts[:, bi, :] if sm else vct[:, ti, :]
                la_d = lats[:, bi, :] if sm else lat[:, ti, :]
                kt_d = ktts[:, bi, :] if sm else ktt[:, ti, :]
                b = bank()
                P.mm(ps[b][:n, 0:384], lrT[0:17, lo:lo + n], wa2[0:17, :], True, True, [RLR, RWA2], [RP[b]], True)
                P.act(gt1[:n, 0:384], ps[b][:n, 0:384], AF.Exp, [RP[b]], [RGT1], scale=-1.0)
                P.act(la_d, gt1[:n, 0:384], AF.Ln, [RGT1], [RLAS if sm else RLA], bias=1.0)
                rla = RLAS if sm else RLA
                b = bank()
                P.mm(ps[b][:n, 0:384], tbd[:n, :n], la_d, True, True, [rla, RC], [RP[b]], True)
                P.act(ekt[:n, :], ps[b][:n, 0:384], AF.Exp, [RP[b]], [REKT])
                s = wK(); w = v3(8, 384)(ring[s])
                b = bank()
                for k in range(8):
                    P.mm(ps[b][:n, 0:384], hT[:, k, lo:lo + n], w[:, k, :], k == 0, k == 7, [RRING[s], RH], [RP[b]], k == 7)
                P.dve("tensor_tensor", [REKT, RP[b]], [RKTTS if sm else RKTT], out=kt_d, in0=ps[b][:n, 0:384], in1=ekt[:n, :], op=ALU.mult)
                s = wV(); w = v3(8, 384)(ring[s])
                b = bank()
                for k in range(8):
                    P.mm(ps[b][:n, 0:384], hT[:, k, lo:lo + n], w[:, k, :], k == 0, k == 7, [RRING[s], RH], [RP[b]], k == 7)
                P.act(vc_d, ps[b][:n, 0:384], AF.Copy, [RP[b]], [RVCTS if sm else RVCT])
                b = bank()
                for h in range(4):
                    P.mm(ps[b][:96, h * 128:h * 128 + n], la_d[:, h * 96:(h + 1) * 96], tbd[:n, :n], True, True, [rla, RC], [RP[b]], h == 3)
                pvw = ps[b][:96, :].rearrange("p (h t) -> p h t", h=4)[:, :, 0:n]
                P.act(eq[:, :, lc:lc + n], pvw, AF.Exp, [RP[b]], [REQ], scale=-1.0)
                P.act(eqi[:, :, lc:lc + n], pvw, AF.Exp, [RP[b]], [REQI])
            ck("g3")
            P.act(ebl[:, :, 0:4], eq[:, :, 63:256:64], AF.Copy, [REQ], [REBL])
            if samp:
                P.act(ebl[:, :, 4:6], eq[:, :, 271:288:16], AF.Copy, [REQ], [REBL])
            ub = []
            for g in range(4):
                ti, h2 = g // 2, g % 2
                b = bank(hold=True)
                ub.append(b)
                for h in range(4):
                    P.mm(ps[b][:96, h * 96:(h + 1) * 96], ktt[h2 * 64:(h2 + 1) * 64, ti, h * 96:(h + 1) * 96], vct[h2 * 64:(h2 + 1) * 64, ti, h * 96:(h + 1) * 96],
                         True, True, [RKTT, RVCT], [RP[b]], h == 3)
            for g in range(4):
                b = ub[g]
                P.act(Sbf[:, g], Sst[:], AF.Copy, [RS], [RSBF])
                P.dve("tensor_tensor", [RS, RP[b]], [RSTM], out=Stm[:, :, :], in0=Sst[:], in1=ps[b][:96, 0:384].rearrange("p (h v) -> p h v", h=4), op=ALU.add)
                held.discard(b)
                for h in range(4):
                    P.dve("tensor_scalar", [RSTM, REBL], [RS], out=Sst[:, h, :], in0=Stm[:, h, :], scalar1=ebl[:, h, g:g + 1], scalar2=None, op0=ALU.mult)
            if samp:
                spdma(ogp[l].rearrange("h k v -> k h v"), Sst[:], [RS], [], "ogp", out=True)
                for b_ in range(2):
                    spdma(Ss[:, :, :], sgla[l, b_].rearrange("h k v -> k h v"), [], [RSS], "sgla")
                    P.act(Sbf[:, 4 + b_], Ss[:, :, :], AF.Copy, [RSS], [RSBF])
                    b = bank()
                    for h in range(4):
                        P.mm(ps[b][:96, h * 96:(h + 1) * 96], ktts[:, b_, h * 96:(h + 1) * 96], vcts[:, b_, h * 96:(h + 1) * 96], True, True, [RKTTS, RVCTS], [RP[b]], h == 3)
                    P.dve("tensor_tensor", [RSS, RP[b]], [RSTM], out=Stm[:, :, :], in0=Ss[:, :, :], in1=ps[b][:96, 0:384].rearrange("p (h v) -> p h v", h=4), op=ALU.add)
                    for h in range(4):
                        P.dve("tensor_scalar", [RSTM, REBL], [RSS], out=Ss[:, h, :], in0=Stm[:, h, :], scalar1=ebl[:, h, 4 + b_:5 + b_], scalar2=None, op0=ALU.mult)
                    spdma(ogs[l, b_].rearrange("h k v -> k h v"), Ss[:, :, :], [RSS], [], "ogs", out=True)
            ck("g2")
            for lo, n, lc in sbl:
                for h in range(4):
                    s = wQ(); w = v3(8, 384)(ring[s])
                    b = bank()
                    for k in range(8):
                        P.mm(ps[b][:96, 0:n], w[:, k, h * 96:(h + 1) * 96], hT[:, k, lo:lo + n], k == 0, k == 7, [RRING[s], RH], [RP[b]], k == 7)
                    P.dve("scalar_tensor_tensor", [RP[b], REQ], [RQT], out=qtT[:, h, lc:lc + n], in0=ps[b][:96, 0:n], scalar=96.0 ** -0.5, in1=eq[:, h, lc:lc + n], op0=ALU.mult, op1=ALU.mult)
                    s = wK(); w = v3(8, 384)(ring[s])
                    b = bank()
                    for k in range(8):
                        P.mm(ps[b][:96, 0:n], w[:, k, h * 96:(h + 1) * 96], hT[:, k, lo:lo + n], k == 0, k == 7, [RRING[s], RH], [RP[b]], k == 7)
                    P.dve("tensor_tensor", [RP[b], REQI], [RKTT_], out=ktT[:, h, lc:lc + n], in0=ps[b][:96, 0:n], in1=eqi[:, h, lc:lc + n], op=ALU.mult)
                    s = wG(); w = v3(8, 400)(ring[s])
                    b = bank()
                    for k in range(8):
                        P.mm(ps[b][:96, 0:n], w[:, k, h * 96:(h + 1) * 96], hT[:, k, lo:lo + n], k == 0, k == 7, [RRING[s], RH], [RP[b]], k == 7)
                    P.act(sgc[:, h, lc:lc + n], ps[b][:96, 0:n], AF.Silu, [RP[b]], [RSG])
            ck("g4")
            bsets = [(attb, RATT, osb, ROS, osq, ROSQ, grs, RGRS), (attb2, RATT2, osb2, ROS2, osq2, ROSQ2, grs2, RGRS2)]

            def i_s1(tile, bs_, stt):
                lo, n, ti, lc = tile
                att_, ratt = bs_[0], bs_[1]
                b = bank()
                for h in range(4):
                    P.mm(ps[b][:n, h * 128:h * 128 + n], ktT[:, h, lc:lc + n], qtT[:, h, lc:lc + n], True, True, [RKTT_, RQT], [RP[b]], h == 3)
                for h in range(4):
                    P.dve("tensor_tensor", [RP[b], RC], [ratt], out=att_[:n, h, 0:n], in0=ps[b][:n, h * 128:h * 128 + n], in1=mbd[:n, 0:n], op=ALU.mult)

            def i_s2(tile, bs_, stt):
                lo, n, ti, lc = tile
                att_, ratt, osb_, ros, osq_, rosq = bs_[0:6]
                sm = ti >= 2
                bi = ti - 2
                vc_d = vcts[:, bi, :] if sm else vct[:, ti, :]
                rvc = RVCTS if sm else RVCT
                bo = bank()
                for h in range(4):
                    P.mm(ps[bo][:96, h * 128:h * 128 + n], vc_d[:, h * 96:(h + 1) * 96], att_[:n, h, 0:n], True, False, [rvc, ratt], [RP[bo]], False)
                    if sm:
                        P.mm(ps[bo][:96, h * 128:h * 128 + n], Sbf[:, 4 + bi, h, :], qtT[:, h, lc:lc + n], False, True, [RSBF, RQT], [RP[bo]], h == 3)
                    else:
                        for h2 in range(2):
                            g = ti * 2 + h2
                            P.mm(ps[bo][:96, h * 128 + h2 * 64:h * 128 + h2 * 64 + 64], Sbf[:, g, h, :], qtT[:, h, lc + h2 * 64:lc + h2 * 64 + 64], False, h2 == 1,
                                 [RSBF, RQT], [RP[bo]], h == 3 and h2 == 1)
                ov = ps[bo][:96, :].rearrange("p (h t) -> p h t", h=4)[:, :, 0:n]
                P.act(osb_[:, :, 0:n], ov, AF.Copy, [RP[bo]], [ros])
                P.act(osq_[:, :, 0:n], osb_[:, :, 0:n], AF.Square, [ros], [rosq])

            def i_s3(tile, bs_, stt):
                lo, n, ti, lc = tile
                osq_, rosq, grs_, rgrs = bs_[4:8]
                bs = bank()
                for h in range(4):
                    P.mm(ps[bs][:96, h * 128:h * 128 + n], onesb[:96, :96], osq_[:, h, 0:n], True, True, [rosq, RC], [RP[bs]], h == 3)
                sv = ps[bs][:96, :].rearrange("p (h t) -> p h t", h=4)[:, :, 0:n]
                gv = grs_[:, :].rearrange("p (h t) -> p h t", h=4)[:, :, 0:n]
                P.act(gv, sv, AF.Ln, [RP[bs], RC], [rgrs], scale=1.0 / 96, bias=epsb[:96, 0:1])
                P.act(gv, gv, AF.Exp, [rgrs], [rgrs], scale=-0.5)

            def i_s4(tile, bs_, stt):
                lo, n, ti, lc = tile
                osb_, ros, grs_, rgrs = bs_[2], bs_[3], bs_[6], bs_[7]
                gv = grs_[:, :].rearrange("p (h t) -> p h t", h=4)[:, :, 0:n]
                P.dve("tensor_tensor", [ros, rgrs], [ros], out=osb_[:, :, 0:n], in0=osb_[:, :, 0:n], in1=gv, op=ALU.mult)
                for h in range(4):
                    P.dve("scalar_tensor_tensor", [ros, RC, RSG], [ROC], out=ocT[:, h, lo:lo + n], in0=osb_[:, h, 0:n], scalar=gln[:, l, h:h + 1],
                          in1=sgc[:, h, lc:lc + n], op0=ALU.mult, op1=ALU.mult)

            for p0 in range(0, len(tiles), 2):
                pair = tiles[p0:p0 + 2]
                for stage in (i_s1, i_s2, i_s3, i_s4):
                    for k_, tile in enumerate(pair):
                        stage(tile, bsets[k_], None)

        def xattn(l, c):
            xv = xT[:, :, 512 * c:512 * c + CW]
            norm_to_h(xv, RX[c], subs(c), gain(2, l))
            qx = big[:, 0:8, :]
            for hf in range(2):
                si = get_w(("xq", l, hf), [(v3(8, 512), rows(w_xq[l])[:, :, hf * 512:(hf + 1) * 512])])
                wv = v3(8, 512)(ring[si])
                for lo, n in subs(c):
                    for mm_ in range(4):
                        b = bank()
                        for k in range(8):
                            P.mm(ps[b][:, 0:n], wv[:, k, mm_ * 128:(mm_ + 1) * 128], hT[:, k, lo:lo + n], k == 0, k == 7, [RRING[si], RH], [RP[b]], k == 7)
                        P.act(qx[:, hf * 4 + mm_, lo:lo + n], ps[b][:, 0:n], AF.Copy, [RP[b]], [RBIG])
            ox = hT
            blocks = [(0, 512, None)]
            if c == 3:
                blocks += [(512, 16, 0), (528, 16, 1)]
            for lo, n, sb_ in blocks:
                if sb_ is None:
                    mk_, mv_, rmk, rmv = mkT, mvt, RMK, RMV
                else:
                    pooldma(mkl[:, :, :], cmk[l, sb_].rearrange("(t p) f -> p t f", p=128), [RMKL], "mkl")
                    pooldma(mvs[:, :, :], cmv[l, sb_].rearrange("(t p) f -> p t f", p=128), [RMVS], "mvs")
                    for t in range(2):
                        for hb in range(2):
                            bk = bank()
                            pv = ps[bk][:, :].bitcast(BF16)
                            for j in range(4):
                                d = hb * 4 + j
                                P.tr(pv[:, j * 128:(j + 1) * 128], mkl[:, t, d * 128:(d + 1) * 128], identb, [RMKL, RC], [RP[bk]], j == 3)
                            P.act(mkTs[:, hb * 4:(hb + 1) * 4, t * 128:(t + 1) * 128], pv[:, 0:512].rearrange("p (a b) -> p a b", a=4), AF.Copy, [RP[bk]], [RMKTS])
                    mk_, mv_, rmk, rmv = mkTs, mvs, RMKTS, RMVS
                pts = [(pTb, RPT, rden, RRD), (pTb2, RPT2, rden2, RRD2)]

                def X1(h):
                    pT_, rpt = pts[h % 2][0], pts[h % 2][1]
                    for mt_ in range(2):
                        b = bank()
                        for dd in range(2):
                            P.mm(ps[b][:, 0:n], mk_[:, 2 * h + dd, mt_ * 128:(mt_ + 1) * 128], qx[:, 2 * h + dd, lo:lo + n], dd == 0, dd == 1, [rmk, RBIG], [RP[b]], dd == 1)
                        P.act(pT_[:, mt_, 0:n], ps[b][:, 0:n], AF.Exp, [RP[b]], [rpt], scale=1.0 / 16)

                def X2(h):
                    pT_, rpt, rd_, rrd = pts[h % 2]
                    bd = bank()
                    for mt_ in range(2):
                        P.mm(ps[bd][:, 0:n], onesb, pT_[:, mt_, 0:n], mt_ == 0, mt_ == 1, [rpt, RC], [RP[bd]], mt_ == 1)
                    P.act(rd_[:, 0:n], ps[bd][:, 0:n], AF.Ln, [RP[bd]], [rrd])
                    P.act(rd_[:, 0:n], rd_[:, 0:n], AF.Exp, [rrd], [rrd], scale=-1.0)
                    for dd in range(2):
                        b = bank()
                        for mt_ in range(2):
                            P.mm(ps[b][:, 0:n], mv_[:, mt_, (2 * h + dd) * 128:(2 * h + dd + 1) * 128], pT_[:, mt_, 0:n], mt_ == 0, mt_ == 1, [rmv, rpt], [RP[b]], mt_ == 1)
                        P.dve("tensor_tensor", [RP[b], rrd], [RH], out=ox[:, 2 * h + dd, lo:lo + n], in0=ps[b][:, 0:n], in1=rd_[:, 0:n], op=ALU.mult)

                for step in range(5):
                    if step < 4:
                        X1(step)
                    if step >= 1:
                        X2(step - 1)
            proj_to_stg(c, ("xo", l), w_xo[l], ox, RH)
            post_norm_add(c, gain(3, l))

        try:
            ck("setup")
            for l in range(nl):
                layer(l)
        except StopBuild as ex:
            print("STOPPED at", ex)

        for t in range(17 if STOP[0] is None else 0):
            rn = 128 if t < 16 else 32
            c = min(t // 4, 3)
            col = t * 128
            for hb in range(2):
                b = bank()
                for j in range(4):
                    k = hb * 4 + j
                    P.tr(ps[b][:rn, j * 128:(j + 1) * 128], xT[:, k, col:col + rn], ident, [RX[c], RC], [RP[b]], j == 3)
                P.act(stg[:rn, hb, 0:512], ps[b][:rn, :], AF.Copy, [RP[b]], [RST])
            dst = yp[t * 128:(t + 1) * 128, :] if t < 16 else ys
            spdma(dst.rearrange("p (a b) -> p a b", a=2), stg[:rn, 0:2, 0:512], [RST], [], "yout", out=True)
        P.emit()
    return nc


def _consts():
    c = np.zeros((128, C_END), np.float32)
    p = np.arange(128)[:, None]
    q = np.arange(128)[None, :]
    c[:, C_ID:C_ID + 128] = (p == q)
    c[:, C_TRI:C_TRI + 128] = (p >= q)
    c[:, C_ONE:C_ONE + 128] = 1.0
    c[:, C_MASK:C_MASK + 512] = (p < np.arange(512)[None, :])
    same = (p // 64) == (q // 64)
    c[:, C_TBD:C_TBD + 128] = np.where((p <= q) & same, 1.0 / 16.0, 0.0)
    c[:, C_MBD:C_MBD + 128] = ((p <= q) & same)
    inv = np.zeros((128, 2, 15), np.float32)
    for pp in range(128):
        for j in range(2):
            w = 2 ** (2 * j + (1 if pp >= 64 else 0) + 1)
            inv[pp, j] = 1.0 / np.minimum(np.arange(15) + 1, w)
    c[:, C_INV:C_INV + 30] = inv.reshape(128, 30)
    return c


_NC_CACHE = {}


def kernel(x_prompt, x_sample, mem_prompt, cache_sb_k, cache_sb_v, state_pool, state_gla,
           cache_mem_k, cache_mem_v, w_in, w_gla_a2, b_gla_a, gla_norm, w_pool, pool_scale,
           w_branch_a, w_branch_b, w_branch_c, w_mix_out, mem_norm, w_xq, w_xk, w_xv, w_xo,
           w_ffn_in, w_ffn_out, norm_mix_pre, norm_mix_post, norm_x_pre, norm_x_post,
           norm_ffn_pre, norm_ffn_post, _nl=NL):
    f = lambda a: np.ascontiguousarray(np.asarray(a, dtype=np.float32))
    fl = lambda a: np.ascontiguousarray(np.asarray(a, dtype=np.float32)[:_nl])
    if _nl not in _NC_CACHE:
        _NC_CACHE[_nl] = build(_nl)
    nc = _NC_CACHE[_nl]
    shared = dict(w_in=fl(w_in), w_gla_a2=fl(w_gla_a2), b_gla_a=fl(b_gla_a), gla_norm=fl(gla_norm), w_pool=fl(w_pool),
                  pool_scale=fl(pool_scale), w_branch_a=fl(w_branch_a), w_branch_b=fl(w_branch_b), w_branch_c=fl(w_branch_c),
                  w_mix_out=fl(w_mix_out), mem_norm=fl(mem_norm), w_xq=fl(w_xq), w_xk=fl(w_xk), w_xv=fl(w_xv), w_xo=fl(w_xo),
                  w_ffn_in=fl(w_ffn_in), w_ffn_out=fl(w_ffn_out), norm_mix_pre=fl(norm_mix_pre), norm_mix_post=fl(norm_mix_post),
                  norm_x_pre=fl(norm_x_pre), norm_x_post=fl(norm_x_post), norm_ffn_pre=fl(norm_ffn_pre), norm_ffn_post=fl(norm_ffn_post),
                  cst=_consts())
    x_prompt = f(x_prompt); x_sample = f(x_sample); mem_prompt = f(mem_prompt)
    cache_sb_k = fl(cache_sb_k); cache_sb_v = fl(cache_sb_v); state_pool = fl(state_pool); state_gla = fl(state_gla)
    cache_mem_k = fl(cache_mem_k); cache_mem_v = fl(cache_mem_v)
    in_maps = []
    for i in range(8):
        s2 = slice(2 * i, 2 * i + 2)
        d = dict(shared)
        d.update(xp=x_prompt[i], xs=np.ascontiguousarray(x_sample[s2].reshape(32, 1024)), mem=mem_prompt[i],
                 csk=np.ascontiguousarray(cache_sb_k[:, s2].reshape(_nl, 2, 1024, 384)),
                 csv=np.ascontiguousarray(cache_sb_v[:, s2].reshape(_nl, 2, 1024, 384)),
                 spool=np.ascontiguousarray(state_pool[:, s2]), sgla=np.ascontiguousarray(state_gla[:, s2]),
                 cmk=np.ascontiguousarray(cache_mem_k[:, s2].reshape(_nl, 2, 256, 1024)),
                 cmv=np.ascontiguousarray(cache_mem_v[:, s2].reshape(_nl, 2, 256, 1024)))
        in_maps.append(d)
    res = run_bass_kernel_spmd(nc, in_maps, core_ids=list(range(8)))
    R = res.results
    def cat(k, ax=0):
        a = np.stack([r[k] for r in R], axis=ax)
        if ax == 1 and a.shape[0] < 4:
            a = np.concatenate([a, np.zeros((4 - a.shape[0],) + a.shape[1:], a.dtype)], axis=0)
        return a
    y_p = cat("yp")
    y_s = cat("ys").reshape(16, 16, 1024)
    kp = cat("okp", 1).reshape(4, 8, 2048, 6, 64)
    vp = cat("ovp", 1).reshape(4, 8, 2048, 6, 64)
    pp = cat("opp", 1)
    gp = cat("ogp", 1)
    mk = cat("omk", 1).reshape(4, 8, 256, 4, 256)
    mv = cat("omv", 1).reshape(4, 8, 256, 4, 256)
    ks = cat("oks", 1).reshape(4, 16, 16, 6, 64)
    vs = cat("ovs", 1).reshape(4, 16, 16, 6, 64)
    pls = cat("ops", 1).reshape(4, 16, 15, 256)
    gs = cat("ogs", 1).reshape(4, 16, 4, 96, 96)
    return (y_p, y_s, kp, vp, pp, gp, mk, mv, ks, vs, pls, gs)
```

```python
import numpy as np
from contextlib import ExitStack
import concourse.bass as bass
import concourse.mybir as mybir
from concourse.bass_utils import run_bass_kernel_spmd

F32 = mybir.dt.float32
BF16 = mybir.dt.bfloat16
AF = mybir.ActivationFunctionType
ALU = mybir.AluOpType

NL = 4
EPS = 1e-6
NT = 2080
CW = 544
O_Q, O_K, O_V, O_U = 0, 384, 768, 1152
O_QC, O_KC, O_VC, O_GC, O_LR, O_G = 1408, 1792, 2176, 2560, 2944, 2960
C_ID, C_TRI, C_ONE, C_MASK, C_TBD, C_MBD, C_INV, C_END = 0, 128, 256, 384, 896, 1024, 1152, 1182


class Res:
    __slots__ = ("w", "r", "name")

    def __init__(self, name=""):
        self.w = None
        self.r = {}
        self.name = name


class PRes(Res):
    __slots__ = ()


class ARes(Res):
    __slots__ = ("lo", "hi")

    def __init__(self, name, lo, hi):
        Res.__init__(self, name)
        self.lo = lo
        self.hi = hi


class DmaSem:
    def __init__(self, sem):
        self.sem = sem
        self.count = 0


class Prog:
    def __init__(self, nc, es):
        self.nc = nc
        self.es = es
        self.engs = {"sp": nc.sync, "act": nc.scalar, "pool": nc.gpsimd, "dve": nc.vector, "pe": nc.tensor}
        self.q = {e: [] for e in self.engs}
        self.sem = {e: es.enter_context(nc.semaphore("S_" + e)) for e in ("act", "pool", "dve", "pe")}
        self.cnt = {e: 0 for e in self.sem}
        self.seen = {e: {} for e in self.engs}
        self.out_sems = []
        self.nsem = 0
        self.arena = []

    def dma_sem(self, out=False):
        self.nsem += 1
        s = DmaSem(self.es.enter_context(self.nc.semaphore("D%d" % self.nsem)))
        if out:
            self.out_sems.append(s)
        return s

    def op(self, eng, fn, reads=(), writes=(), inc=True, dma=None):
        waits = {}
        own = self.sem.get(eng)

        def need(ev):
            if ev is None:
                return
            s, v = ev
            if s is own and dma is None and (eng == "pe" or v > self.cnt[eng]):
                return
            k = id(s)
            if k not in waits or waits[k][1] < v:
                waits[k] = (s, v)

        pr = [r for r in reads if isinstance(r, PRes)]
        if pr:
            reads = [r for r in reads if not isinstance(r, PRes)]
            writes = list(writes) + pr
        for r in reads:
            need(r.w)
        for w in writes:
            need(w.w)
            for ev in w.r.values():
                need(ev)
            if isinstance(w, ARes):
                for o in self.arena:
                    if o is not w and o.lo < w.hi and w.lo < o.hi:
                        need(o.w)
                        for ev in o.r.values():
                            need(ev)
        wl = []
        for k, (s, v) in waits.items():
            if self.seen[eng].get(k, 0) < v:
                self.seen[eng][k] = v
                wl.append((s, v))
        if dma is not None:
            dma.count += 16
            ev = (dma.sem, dma.count)
            incv = 16
        else:
            if inc:
                self.cnt[eng] += 1
                ev = (self.sem[eng], self.cnt[eng])
            else:
                ev = (self.sem[eng], self.cnt[eng] + 1)
            incv = 1
        for r in reads:
            k = id(ev[0])
            if k not in r.r or r.r[k][1] < ev[1]:
                r.r[k] = ev
        for w in writes:
            w.w = ev
            w.r = {}
        self.q[eng].append((wl, fn, (ev[0], incv) if (inc or dma is not None) else None))

    def mm(self, out, lhsT, rhs, start, stop, reads, writes, inc, sgc=False):
        if sgc:
            self.op("pe", lambda e: e.matmul(out, lhsT=lhsT, rhs=rhs, start=start, stop=stop, skip_group_check=True), reads, writes, inc)
        else:
            self.op("pe", lambda e: e.matmul(out, lhsT=lhsT, rhs=rhs, start=start, stop=stop), reads, writes, inc)

    def tr(self, out, in_, ident, reads, writes, inc):
        self.op("pe", lambda e: e.transpose(out, in_, ident), reads, writes, inc)

    def act(self, out, in_, func, reads, writes, **kw):
        self.op("act", lambda e: e.activation(out=out, in_=in_, func=func, **kw), reads, writes)

    def dve(self, name, reads, writes, **kw):
        self.op("dve", lambda e: getattr(e, name)(**kw), reads, writes)

    def emit(self):
        nc = self.nc
        finals = [(s.sem, s.count) for s in self.out_sems if s.count > 0]
        with nc.Block() as block:
            def run(name):
                def f(eng):
                    for wl, fn, inc in self.q[name]:
                        for s, v in wl:
                            eng.wait_ge(s, v)
                        ins = fn(eng)
                        if inc is not None:
                            ins.then_inc(inc[0], inc[1])
                    if name == "sp":
                        for s, v in finals:
                            eng.wait_ge(s, v)
                return f

            block.sync(run("sp"))
            block.scalar(run("act"))
            block.gpsimd(run("pool"))
            block.vector(run("dve"))
            block.tensor(run("pe"))


def subs(c):
    return [(0, 512)] + ([(512, 32)] if c == 3 else [])


_LASTP = [None]


class StopBuild(Exception):
    pass


STOP = [None]
MARKS = []


def ck(name):
    if _LASTP[0] is not None:
        MARKS.append((name, len(_LASTP[0].q["pe"]), len(_LASTP[0].q["act"]), len(_LASTP[0].q["dve"])))
    if STOP[0] == name:
        raise StopBuild(name)


def build(nl=NL):
    nc = bass.Bass("TRN2", target_bir_lowering=False)
    es = ExitStack()

    def din(name, shape):
        return nc.dram_tensor(name, list(shape), F32, kind="ExternalInput").ap()

    def dout(name, shape):
        return nc.dram_tensor(name, list(shape), F32, kind="ExternalOutput").ap()

    xp = din("xp", [2048, 1024]); xs = din("xs", [32, 1024]); mem = din("mem", [256, 1024])
    csk = din("csk", [nl, 2, 1024, 384]); csv = din("csv", [nl, 2, 1024, 384])
    spool = din("spool", [nl, 2, 15, 256]); sgla = din("sgla", [nl, 2, 4, 96, 96])
    cmk = din("cmk", [nl, 2, 256, 1024]); cmv = din("cmv", [nl, 2, 256, 1024])
    w_in = din("w_in", [nl, 1024, 6032]); w_a2 = din("w_gla_a2", [nl, 16, 384]); b_a = din("b_gla_a", [nl, 384])
    gla_norm = din("gla_norm", [nl, 384]); w_pool = din("w_pool", [nl, 4, 64, 64]); pool_scale = din("pool_scale", [nl, 256])
    w_ba = din("w_branch_a", [nl, 384, 1024]); w_bb = din("w_branch_b", [nl, 256, 1024]); w_bc = din("w_branch_c", [nl, 384, 1024])
    w_mix = din("w_mix_out", [nl, 1024, 1024]); mem_norm = din("mem_norm", [nl, 1024])
    w_xq = din("w_xq", [nl, 1024, 1024]); w_xk = din("w_xk", [nl, 1024, 1024]); w_xv = din("w_xv", [nl, 1024, 1024]); w_xo = din("w_xo", [nl, 1024, 1024])
    w_fi = din("w_ffn_in", [nl, 1024, 5632]); w_fo = din("w_ffn_out", [nl, 2816, 1024])
    gnames = ["norm_mix_pre", "norm_mix_post", "norm_x_pre", "norm_x_post", "norm_ffn_pre", "norm_ffn_post"]
    gains = [din(n, [nl, 1024]) for n in gnames]
    cst = din("cst", [128, C_END])

    yp = dout("yp", [2048, 1024]); ys = dout("ys", [32, 1024])
    okp = dout("okp", [nl, 2048, 384]); ovp = dout("ovp", [nl, 2048, 384])
    opp = dout("opp", [nl, 15, 256]); ogp = dout("ogp", [nl, 4, 96, 96])
    omk = dout("omk", [nl, 256, 1024]); omv = dout("omv", [nl, 256, 1024])
    oks = dout("oks", [nl, 32, 384]); ovs = dout("ovs", [nl, 32, 384])
    ops_ = dout("ops", [nl, 2, 15, 256]); ogs = dout("ogs", [nl, 2, 4, 96, 96])

    with es:
        P = Prog(nc, es)
        _LASTP[0] = P

        def sb(name, shape, dt):
            return es.enter_context(nc.sbuf_tensor(name, shape, dt))

        xT = sb("xT", [128, 8, NT], F32)
        RX = [Res("x%d" % c) for c in range(4)]
        kT = sb("kT", [128, 3, NT], BF16); RKT = Res("kT")
        vtok = sb("vtok", [128, 16, 384], BF16); RVT = Res("vtok")
        vtoks = sb("vtoks", [16, 2, 384], BF16); RVTS = Res("vtoks")
        hT = sb("hT", [128, 8, CW], BF16); RH = Res("hT")
        rstd = sb("rstd", [128, CW], F32); RRS = Res("rstd")
        big = sb("big", [128, 11, CW], BF16); RBIG = Res("big")
        oaT = sb("oaT", [128, 3, CW], BF16); ROA = Res("oaT")
        obT = sb("obT", [128, 2, CW], BF16); ROB = Res("obT")
        ocT = sb("ocT", [96, 4, CW], BF16); ROC = Res("ocT")
        cstb = sb("cstb", [128, C_INV], BF16); RC = Res("cst")
        identf = sb("identf", [128, 128], F32)
        invcf = sb("invcf", [128, 30], F32)
        gall = sb("gall", [128, 7, 4, 8], F32)
        gln = sb("gln", [96, 4, 4], F32)
        psc = sb("psc", [128, 4, 2], F32)
        epsb = sb("epsb", [128, 1], F32)
        mkT = sb("mkT", [128, 8, 256], BF16); RMK = Res("mkT")
        mvt = sb("mvt", [128, 2, 1024], BF16); RMV = Res("mvt")
        wpbd = sb("wpbd", [128, 2, 128], BF16); RWP = Res("wpbd")
        lrT = sb("lrT", [32, CW], BF16); RLR = Res("lrT")
        wa2 = sb("wa2", [32, 384], BF16); RWA2 = Res("wa2")
        ebl = sb("ebl", [96, 4, 8], F32); REBL = Res("ebl")
        Sst = sb("Sst", [96, 4, 96], F32); RS = Res("S")
        uhalo = sb("uhalo", [128, 2, 15], F32); RUH = Res("uhalo")
        NS = 3
        SLOT = 4352
        ring = [sb("ring%d" % i, [128, SLOT], BF16) for i in range(NS)]
        RRING = [Res("ring%d" % i) for i in range(NS)]
        DRING = [P.dma_sem() for _ in range(NS)]
        AR = 40960
        arena = sb("arena", [128, AR // 2], BF16)
        print("sbuf remaining", nc.sbuf_bytes_remaining)

        def av(name, off, shape, dt):
            esz = 4 if dt == F32 else 2
            nel = 1
            for d in shape[1:]:
                nel *= d
            assert off % 4 == 0 and off + nel * esz <= AR, (name, off, nel * esz)
            a = arena[:shape[0], off // 2:off // 2 + nel * esz // 2]
            if dt == F32:
                a = a.bitcast(F32)
            if len(shape) == 3:
                a = a.rearrange("p (a b) -> p a b", a=shape[1])
            elif len(shape) == 4:
                a = a.rearrange("p (a b c) -> p a b c", a=shape[1], b=shape[2])
            r = ARes(name, off, off + nel * esz)
            P.arena.append(r)
            return a, r

        stg, RST = av("stg", 0, [128, 8, CW], F32)
        sqb, RSQ = av("sqb", 17408, [128, 8, CW], BF16)
        memT, RMEM = av("memT", 26112, [128, 8, 256], F32)
        mnT, RMN = av("mnT", 34304, [128, 8, 256], BF16)
        cstf, RCF = av("cstf", 0, [128, C_END], F32)
        qT, RQ = av("qT", 0, [128, 3, CW], BF16)
        ez, REZ, spb, RSP, enb, REN, wb, RWB = [], [], [], [], [], [], [], []
        for i in range(3):
            a, r = av("ez%d" % i, 3264 + 2048 * i, [128, 512], F32); ez.append(a); REZ.append(r)
            a, r = av("sp%d" % i, 9408 + 1024 * i, [128, 512], BF16); spb.append(a); RSP.append(r)
            a, r = av("wb%d" % i, 16576 + 1024 * i, [128, 512], BF16); wb.append(a); RWB.append(r)
        for i in range(2):
            a, r = av("en%d" % i, 12480 + 2048 * i, [128, 512], F32); enb.append(a); REN.append(r)
        Rb, RR = av("Rb", 19648, [128, 512], BF16)
        kcs, RKCS = av("kcs", 20672, [128, 8, 384], BF16)
        kcsT, RKCST = av("kcsT", 26816, [128, 3, 1024], BF16)
        vcs, RVCS = av("vcs", 32960, [128, 8, 384], BF16)
        PW = 15 + 512 + 62
        uT, RU = av("uT", 0, [128, 2, PW], F32)
        pA, RPA = av("pA", 4712, [128, 2, PW], F32)
        pB, RPB = av("pB", 9424, [128, 2, PW], F32)
        pC, RPC = av("pC", 14136, [128, PW], F32)
        pD, RPD = av("pD", 16492, [128, PW], F32)
        pooled, RPL = av("pooled", 18848, [128, 2, CW], BF16)
        ptmp, RPT_ = av("ptmp", 21024, [128, 16], F32)
        HWD = 288
        qtT, RQT = av("qtT", 0, [96, 4, HWD], BF16)
        ktT, RKTT_ = av("ktT", 2304, [96, 4, HWD], BF16)
        sgc, RSG = av("sgc", 4608, [96, 4, HWD], BF16)
        eq, REQ = av("eq", 6912, [96, 4, HWD], F32)
        eqi, REQI = av("eqi", 11520, [96, 4, HWD], BF16)
        vct, RVCT = av("vct", 13824, [128, 2, 384], BF16)
        lat, RLA = av("lat", 15360, [128, 2, 384], BF16)
        ktt, RKTT = av("ktt", 16896, [128, 2, 384], BF16)
        ekt, REKT = av("ekt", 18432, [128, 384], F32)
        gt1, RGT1 = av("gt1", 19968, [128, 512], F32)
        vcts, RVCTS = av("vcts", 22016, [16, 2, 384], BF16)
        lats, RLAS = av("lats", 23552, [16, 2, 384], BF16)
        ktts, RKTTS = av("ktts", 25088, [16, 2, 384], BF16)
        Sbf, RSBF = av("Sbf", 26624, [96, 6, 4, 96], BF16)
        attb, RATT = av("attb", 31232, [128, 4, 128], BF16)
        osb, ROS = av("osb", 32256, [96, 4, 128], F32)
        osq, ROSQ = av("osq", 34304, [96, 4, 128], BF16)
        grs, RGRS = av("grs", 35328, [96, 512], F32)
        attb2, RATT2 = av("attb2", 20480, [128, 4, 128], BF16)
        osb2, ROS2 = av("osb2", 18432, [96, 4, 128], F32)
        osq2, ROSQ2 = av("osq2", 15360, [96, 4, 128], BF16)
        grs2, RGRS2 = av("grs2", 16384, [96, 512], F32)
        Stm, RSTM = av("Stm", 37376, [96, 4, 96], F32)
        Ss, RSS = av("Ss", 38912, [96, 4, 96], F32)
        sg3, RSG3 = [], []
        for i in range(3):
            a, r = av("sg%d" % i, 26112 + 2048 * i, [128, 512], F32); sg3.append(a); RSG3.append(r)
        mt1, RMT1 = av("mt1", 32256, [128, 512], F32)
        mt2, RMT2 = av("mt2", 34304, [128, 512], F32)
        pTb, RPT = av("pTb", 26112, [128, 2, 512], BF16)
        rden, RRD = av("rden", 28160, [128, 512], F32)
        pTb2, RPT2 = av("pTb2", 17408, [128, 2, 512], BF16)
        rden2, RRD2 = av("rden2", 19456, [128, 512], F32)
        mkl, RMKL = av("mkl", 0, [128, 2, 1024], BF16)
        mkTs, RMKTS = av("mkTs", 30208, [128, 8, 256], BF16)
        mvs, RMVS = av("mvs", 34304, [128, 2, 1024], BF16)
        ft1, RFT1 = av("ft1", 26112, [128, 512], F32)

        ps = [es.enter_context(nc.psum_tensor("ps%d" % i, [128, 512], F32)) for i in range(8)]
        RP = [PRes("ps%d" % i) for i in range(8)]
        bank_i = [0]

        held = set()

        def bank(hold=False):
            while True:
                b = bank_i[0] % 8
                bank_i[0] += 1
                if b not in held:
                    break
            if hold:
                held.add(b)
            return b

        ring_i = [0]
        wres = {}
        slot_key = [None] * NS

        def get_w(key, loads):
            if key in wres:
                return wres[key]
            si = ring_i[0] % NS
            ring_i[0] += 1
            if slot_key[si] is not None:
                del wres[slot_key[si]]
            slot_key[si] = key
            wres[key] = si
            for dstf, src in loads:
                dst = dstf(ring[si])
                P.op("pool", lambda e, dst=dst, src=src: e.dma_start(out=dst, in_=src), writes=[RRING[si]], dma=DRING[si])
            return si

        def v3(kc, ncols, off=0):
            return lambda t: t[:, off:off + kc * ncols].rearrange("p (k c) -> p k c", k=kc)

        def rows(w2d):
            return w2d.rearrange("(k p) c -> p k c", p=128)

        dsm = {}
        P._dsm = dsm

        def dsem(name, out=False):
            if name not in dsm:
                dsm[name] = P.dma_sem(out=out)
            return dsm[name]

        def spdma(dst, src, reads, writes, name, out=False, nc_ok=False):
            if nc_ok:
                P.op("sp", lambda e: e.dma_start(out=dst, in_=src, allow_slow_non_contiguous=True), reads, writes, dma=dsem(name, out))
            else:
                P.op("sp", lambda e: e.dma_start(out=dst, in_=src), reads, writes, dma=dsem(name, out))

        def pooldma(dst, src, writes, name):
            P.op("pool", lambda e: e.dma_start(out=dst, in_=src), writes=writes, dma=dsem(name))

        ident = identf[:, :]
        identb = cstb[:, C_ID:C_ID + 128]
        trib = cstb[:, C_TRI:C_TRI + 128]
        onesb = cstb[:, C_ONE:C_ONE + 128]
        maskb = cstb[:, C_MASK:C_MASK + 512]
        tbd = cstb[:, C_TBD:C_TBD + 128]
        mbd = cstb[:, C_MBD:C_MBD + 128]
        invc = invcf[:, :].rearrange("p (j t) -> p j t", j=2)

        spdma(cstf[:, :], cst, [], [RCF], "cstf")
        P.act(cstb[:, :], cstf[:, 0:C_INV], AF.Copy, [RCF], [RC])
        P.act(identf[:, :], cstf[:, C_ID:C_ID + 128], AF.Copy, [RCF], [RC])
        P.act(invcf[:, :], cstf[:, C_INV:C_INV + 30], AF.Copy, [RCF], [RC])
        P.op("dve", lambda e: e.memset(epsb[:], EPS), writes=[RC])
        P.op("dve", lambda e: e.memset(lrT[:], 1.0), writes=[RLR])
        P.op("dve", lambda e: e.memset(wpbd[:], 0.0), writes=[RWP])
        for i, g in enumerate(gains + [mem_norm]):
            spdma(gall[:, i, 0:nl, :], g.rearrange("l (k p) -> p l k", p=128), [], [RC], "cst", nc_ok=True)
        spdma(gln[:, 0:nl, :], gla_norm.rearrange("l (h v) -> v l h", v=96), [], [RC], "cst", nc_ok=True)
        spdma(psc[:, 0:nl, :], pool_scale.rearrange("l (j p) -> p l j", p=128), [], [RC], "cst", nc_ok=True)

        def gain(i, l):
            return gall[:, i, l, :]

        def stats(src3, rsrc, lo, n, scale):
            b = bank()
            for k in range(8):
                P.mm(ps[b][:, 0:n], onesb, src3[:, k, lo:lo + n], k == 0, k == 7, [rsrc, RC], [RP[b]], k == 7)
            P.act(rstd[:, lo:lo + n], ps[b][:, 0:n], AF.Ln, [RP[b], RC], [RRS], scale=scale, bias=epsb[:, 0:1])
            P.act(rstd[:, lo:lo + n], rstd[:, lo:lo + n], AF.Exp, [RRS], [RRS], scale=-0.5)

        def norm_to_h(xv, rx, sbl, g):
            for lo, n in sbl:
                P.act(hT[:, :, lo:lo + n], xv[:, :, lo:lo + n], AF.Square, [rx], [RH])
                stats(hT, RH, lo, n, 1.0 / 1024)
                for k in range(8):
                    P.dve("scalar_tensor_tensor", [rx, RRS, RC], [RH], out=hT[:, k, lo:lo + n], in0=xv[:, k, lo:lo + n],
                          scalar=g[:, k:k + 1], in1=rstd[:, lo:lo + n], op0=ALU.mult, op1=ALU.mult)

        def post_norm_add(c, g):
            xv = xT[:, :, 512 * c:512 * c + CW]
            for lo, n in subs(c):
                stats(sqb, RSQ, lo, n, 1.0 / 1024)
                for k in range(8):
                    P.dve("tensor_tensor", [RST, RRS], [RST], out=stg[:, k, lo:lo + n], in0=stg[:, k, lo:lo + n], in1=rstd[:, lo:lo + n], op=ALU.mult)
                    P.dve("scalar_tensor_tensor", [RST, RC, RX[c]], [RX[c]], out=xv[:, k, lo:lo + n], in0=stg[:, k, lo:lo + n],
                          scalar=g[:, k:k + 1], in1=xv[:, k, lo:lo + n], op0=ALU.mult, op1=ALU.add)

        def proj_to_stg(c, key, wsrc2d, src3, rsrc):
            for hf in range(2):
                si = get_w((key, hf), [(v3(8, 512), rows(wsrc2d)[:, :, hf * 512:(hf + 1) * 512])])
                wv = v3(8, 512)(ring[si])
                for lo, n in subs(c):
                    for mm_ in range(4):
                        m = hf * 4 + mm_
                        b = bank()
                        for k in range(8):
                            P.mm(ps[b][:, 0:n], wv[:, k, mm_ * 128:(mm_ + 1) * 128], src3[:, k, lo:lo + n], k == 0, k == 7, [RRING[si], rsrc], [RP[b]], k == 7)
                        P.act(stg[:, m, lo:lo + n], ps[b][:, 0:n], AF.Copy, [RP[b]], [RST])
                        P.act(sqb[:, m, lo:lo + n], stg[:, m, lo:lo + n], AF.Square, [RST], [RSQ])

        for t in range(17):
            rn = 128 if t < 16 else 32
            src = xp[t * 128:(t + 1) * 128, :] if t < 16 else xs
            spdma(stg[:rn, 0:2, 0:512], src.rearrange("p (a b) -> p a b", a=2), [], [RST], "xin")
            c = min(t // 4, 3)
            col = t * 128
            for hb in range(2):
                b = bank()
                for j in range(4):
                    P.tr(ps[b][:, j * rn:(j + 1) * rn], stg[:rn, hb, j * 128:(j + 1) * 128], ident[:rn, :rn], [RST, RC], [RP[b]], j == 3)
                P.act(xT[:, hb * 4:(hb + 1) * 4, col:col + rn], ps[b][:, 0:4 * rn].rearrange("p (a b) -> p a b", a=4), AF.Copy, [RP[b]], [RX[c]])

        def layer(l):
            wr = rows(w_in[l])
            for t in range(2):
                spdma(stg[:, 0:2, 0:512], mem[t * 128:(t + 1) * 128, :].rearrange("p (a b) -> p a b", a=2), [], [RST], "xin")
                for hb in range(2):
                    b = bank()
                    for j in range(4):
                        P.tr(ps[b][:, j * 128:(j + 1) * 128], stg[:, hb, j * 128:(j + 1) * 128], ident, [RST, RC], [RP[b]], j == 3)
                    P.act(memT[:, hb * 4:(hb + 1) * 4, t * 128:(t + 1) * 128], ps[b][:, 0:512].rearrange("p (a b) -> p a b", a=4), AF.Copy, [RP[b]], [RMEM])
            ck("mem1")
            P.act(mnT[:, :, :], memT[:, :, :], AF.Square, [RMEM], [RMN])
            stats(mnT, RMN, 0, 256, 1.0 / 1024)
            for k in range(8):
                P.dve("scalar_tensor_tensor", [RMEM, RC, RRS], [RMN], out=mnT[:, k, :], in0=memT[:, k, :], scalar=gain(6, l)[:, k:k + 1],
                      in1=rstd[:, 0:256], op0=ALU.mult, op1=ALU.mult)
            ck("mem2")
            for which, wsrc, odram in ((0, w_xk, omk), (1, w_xv, omv)):
                if which == 1:
                    ck("mem3")
                for hf in range(2):
                    si = get_w(("xkv", which, hf), [(v3(8, 512), rows(wsrc[l])[:, :, hf * 512:(hf + 1) * 512])])
                    wv = v3(8, 512)(ring[si])
                    if which == 0:
                        for jj in range(4):
                            b = bank()
                            for k in range(8):
                                P.mm(ps[b][:, 0:256], wv[:, k, jj * 128:(jj + 1) * 128], mnT[:, k, :], k == 0, k == 7, [RRING[si], RMN], [RP[b]], k == 7)
                            P.act(mkT[:, hf * 4 + jj, :], ps[b][:, 0:256], AF.Copy, [RP[b]], [RMK])
                    for t in range(2):
                        b = bank()
                        for k in range(8):
                            P.mm(ps[b][:, :], mnT[:, k, t * 128:(t + 1) * 128], wv[:, k, :], k == 0, k == 7, [RRING[si], RMN], [RP[b]], k == 7)
                        P.act(stg[:, t, 0:512], ps[b][:, :], AF.Copy, [RP[b]], [RST])
                        if which == 1:
                            P.dve("tensor_copy", [RP[b]], [RMV], out=mvt[:, t, hf * 512:(hf + 1) * 512], in_=ps[b][:, :])
                    spdma(odram[l, :, hf * 512:(hf + 1) * 512].rearrange("(t p) f -> p t f", p=128), stg[:, 0:2, 0:512], [RST], [], "omem", out=True)
            ck("memkv")
            for c in range(4):
                xv = xT[:, :, 512 * c:512 * c + CW]
                norm_to_h(xv, RX[c], subs(c), gain(0, l))
                sk = get_w(("wk", l), [(v3(8, 384), wr[:, :, O_K:O_K + 384])])
                wk = v3(8, 384)(ring[sk])
                for lo, n in subs(c):
                    for j in range(3):
                        b = bank()
                        for k in range(8):
                            P.mm(ps[b][:, 0:n], wk[:, k, j * 128:(j + 1) * 128], hT[:, k, lo:lo + n], k == 0, k == 7, [RRING[sk], RH], [RP[b]], k == 7)
                        P.act(kT[:, j, 512 * c + lo:512 * c + lo + n], ps[b][:, 0:n], AF.Copy, [RP[b]], [RKT])
                sv_ = get_w(("wv", l), [(v3(8, 384), wr[:, :, O_V:O_V + 384])])
                wvv = v3(8, 384)(ring[sv_])
                tiles = [(t * 128, 128, 4 * c + t) for t in range(4)]
                if c == 3:
                    tiles += [(512, 16, 16), (528, 16, 17)]
                for lo, n, gt in tiles:
                    for which, wsl, rsl in ((0, wk, sk), (1, wvv, sv_)):
                        b = bank()
                        for k in range(8):
                            P.mm(ps[b][:n, 0:384], hT[:, k, lo:lo + n], wsl[:, k, :], k == 0, k == 7, [RRING[rsl], RH], [RP[b]], k == 7)
                        P.act(stg[:n, which, 0:384], ps[b][:n, 0:384], AF.Copy, [RP[b]], [RST])
                        if which == 1:
                            if gt < 16:
                                P.dve("tensor_copy", [RP[b]], [RVT], out=vtok[:, gt, :], in_=ps[b][:, 0:384])
                            else:
                                P.dve("tensor_copy", [RP[b]], [RVTS], out=vtoks[:, gt - 16, :], in_=ps[b][:16, 0:384])
                    if gt < 16:
                        spdma(okp[l, gt * 128:(gt + 1) * 128, :], stg[:, 0, 0:384], [RST], [], "okv", out=True)
                        spdma(ovp[l, gt * 128:(gt + 1) * 128, :], stg[:, 1, 0:384], [RST], [], "okv", out=True)
                    else:
                        bb = gt - 16
                        spdma(oks[l, bb * 16:(bb + 1) * 16, :], stg[:16, 0, 0:384], [RST], [], "okv", out=True)
                        spdma(ovs[l, bb * 16:(bb + 1) * 16, :], stg[:16, 1, 0:384], [RST], [], "okv", out=True)
            ck("prekv")
            for g4 in range(4):
                j, hh = g4 // 2, g4 % 2
                pooldma(wpbd[hh * 64:(hh + 1) * 64, j, hh * 64:(hh + 1) * 64], w_pool[l, g4], [RWP], "wp")
            pooldma(wa2[0:16, :], w_a2[l], [RWA2], "wa2")
            pooldma(wa2[16:17, :], b_a[l:l + 1, :], [RWA2], "wa2")
            P.op("dve", lambda e: e.memset(Sst[:], 0.0), writes=[RS])
            for c in range(4):
                chunk(l, c, wr)

        def chunk(l, c, wr):
            xv = xT[:, :, 512 * c:512 * c + CW]
            sbl = subs(c)
            norm_to_h(xv, RX[c], sbl, gain(0, l))
            si = get_w(("wq", l), [(v3(8, 384), wr[:, :, O_Q:O_Q + 384])])
            wq = v3(8, 384)(ring[si])
            for lo, n in sbl:
                for j in range(3):
                    b = bank()
                    for k in range(8):
                        P.mm(ps[b][:, 0:n], wq[:, k, j * 128:(j + 1) * 128], hT[:, k, lo:lo + n], k == 0, k == 7, [RRING[si], RH], [RP[b]], k == 7)
                    P.act(qT[:, j, lo:lo + n], ps[b][:, 0:n], AF.Copy, [RP[b]], [RQ])
            ck("q")
            nkt = 4 * c + 4
            kts = list(range(nkt - 1, -1, -1))
            for h in range(6):
                j, pb = h // 2, 64 * (h % 2)
                bo = bank(hold=True)
                st = {}

                def S1(ti):
                    kt = kts[ti]
                    i = kt - 4 * c
                    c0 = 128 * i if i > 0 else 0
                    ncl = 512 - c0
                    r = ti % 3
                    bz = bank()
                    P.mm(ps[bz][:, 0:ncl], kT[pb:pb + 64, j, kt * 128:(kt + 1) * 128], qT[pb:pb + 64, j, c0:512], True, True, [RKT, RQ], [RP[bz]], True)
                    P.act(ez[r][:, 0:ncl], ps[bz][:, 0:ncl], AF.Exp, [RP[bz]], [REZ[r]], scale=0.125)
                    if i >= 0:
                        P.dve("tensor_tensor", [REZ[r], RC], [REZ[r]], out=ez[r][:, 0:ncl], in0=ez[r][:, 0:ncl], in1=maskb[:, 0:ncl], op=ALU.mult)
                    P.act(spb[r][:, 0:ncl], ez[r][:, 0:ncl], AF.Ln, [REZ[r]], [RSP[r]], bias=1.0)
                    st[ti] = (kt, c0, ncl, r)

                def S2(ti):
                    kt, c0, ncl, r = st[ti]
                    r2 = ti % 2
                    bc = bank()
                    lastk = (kt == nkt - 1)
                    P.mm(ps[bc][:, 0:ncl], trib, spb[r][:, 0:ncl], True, lastk, [RSP[r], RC], [RP[bc]], lastk)
                    if not lastk:
                        P.mm(ps[bc][:, 0:ncl], onesb, Rb[:, c0:512], False, True, [RR, RC], [RP[bc]], True)
                    P.act(enb[r2][:, 0:ncl], ps[bc][:, 0:ncl], AF.Exp, [RP[bc]], [REN[r2]], scale=-1.0)
                    P.dve("tensor_tensor", [REZ[r], REN[r2]], [RWB[r]], out=wb[r][:, 0:ncl], in0=ez[r][:, 0:ncl], in1=enb[r2][:, 0:ncl], op=ALU.mult)
                    if kt > 0:
                        if lastk:
                            P.op("dve", lambda e: e.memset(Rb[:, 0:384], 0.0), writes=[RR])
                            P.dve("tensor_copy", [RSP[r]], [RR], out=Rb[:, c0:512], in_=spb[r][:, 0:ncl])
                        else:
                            P.dve("tensor_tensor", [RSP[r], RR], [RR], out=Rb[:, c0:512], in0=Rb[:, c0:512], in1=spb[r][:, 0:ncl], op=ALU.add)

                def S3(ti):
                    kt, c0, ncl, r = st[ti]
                    P.mm(ps[bo][:, c0:512], vtok[:, kt, j * 128:(j + 1) * 128], wb[r][:, 0:ncl], kt == nkt - 1, kt == 0, [RVT, RWB[r]], [RP[bo]], kt == 0, sgc=True)

                nt_ = len(kts)
                for step in range(nt_ + 2):
                    if step < nt_:
                        S1(step)
                    if 0 <= step - 1 < nt_:
                        S2(step - 1)
                    if 0 <= step - 2 < nt_:
                        S3(step - 2)
                P.act(oaT[pb:pb + 64, j, 0:512], ps[bo][pb:pb + 64, 0:512], AF.Copy, [RP[bo]], [ROA])
                held.discard(bo)
            ck("sb%d" % c)
            if c == 3:
                for b_ in range(2):
                    sb_sample(l, b_)
            ck("sbs%d" % c)
            pool_branch(l, c, wr)
            ck("pool%d" % c)
            for hf in range(2):
                gla_half(l, c, hf, wr)
            ck("gla%d" % c)
            merged = big[:, 0:8, :]
            brs = [(w_ba, 3, 128, oaT, ROA, lambda t: t[:, 0:1536].rearrange("p (k c) -> p k c", k=3), lambda w: rows(w)),
                   (w_bb, 2, 128, obT, ROB, lambda t: t[:, 0:1024].rearrange("p (k c) -> p k c", k=2), lambda w: rows(w)),
                   (w_bc, 4, 96, ocT, ROC, lambda t: t[:96, 0:2048].rearrange("p (k c) -> p k c", k=4), lambda w: w.rearrange("(h v) c -> v h c", v=96))]
            for br, (wb_, kc_, kp_, src_, rsrc_, vf_, rf_) in enumerate(brs):
                for hf in range(2):
                    sb_i = get_w(("mb", l, br, hf), [(vf_, rf_(wb_[l])[:, :, hf * 512:(hf + 1) * 512])])
                    wbv = vf_(ring[sb_i])
                    sg_i = get_w(("mg", l, br, hf), [(v3(8, 512), wr[:, :, O_G + br * 1024 + hf * 512:O_G + br * 1024 + (hf + 1) * 512])])
                    gvw = v3(8, 512)(ring[sg_i])
                    for mm_ in range(4):
                        m = hf * 4 + mm_
                        for lo, n in sbl:
                            bg, bb = bank(), bank()
                            for k in range(8):
                                P.mm(ps[bg][:, 0:n], gvw[:, k, mm_ * 128:(mm_ + 1) * 128], hT[:, k, lo:lo + n], k == 0, k == 7, [RRING[sg_i], RH], [RP[bg]], k == 7)
                            for k in range(kc_):
                                P.mm(ps[bb][:, 0:n], wbv[:kp_, k, mm_ * 128:(mm_ + 1) * 128], src_[:kp_, k, lo:lo + n], k == 0, k == kc_ - 1, [RRING[sb_i], rsrc_], [RP[bb]], k == kc_ - 1)
                            sgb = sg3[(m + br) % 3]
                            rsg = RSG3[(m + br) % 3]
                            P.act(sgb[:, 0:n], ps[bg][:, 0:n], AF.Sigmoid, [RP[bg]], [rsg])
                            if br == 0:
                                P.dve("tensor_tensor", [rsg, RP[bb]], [RST], out=stg[:, m, lo:lo + n], in0=sgb[:, 0:n], in1=ps[bb][:, 0:n], op=ALU.mult)
                            else:
                                mt_, rmt_ = (mt1, RMT1) if (m % 2 == 0) else (mt2, RMT2)
                                P.dve("tensor_tensor", [rsg, RP[bb]], [rmt_], out=mt_[:, 0:n], in0=sgb[:, 0:n], in1=ps[bb][:, 0:n], op=ALU.mult)
                                if br == 1:
                                    P.dve("tensor_tensor", [rmt_, RST], [RST], out=stg[:, m, lo:lo + n], in0=stg[:, m, lo:lo + n], in1=mt_[:, 0:n], op=ALU.add)
                                else:
                                    P.dve("tensor_tensor", [rmt_, RST], [RBIG], out=merged[:, m, lo:lo + n], in0=stg[:, m, lo:lo + n], in1=mt_[:, 0:n], op=ALU.add)
            proj_to_stg(c, ("mix", l), w_mix[l], merged, RBIG)
            post_norm_add(c, gain(1, l))
            ck("mix%d" % c)
            xattn(l, c)
            ck("xattn%d" % c)
            norm_to_h(xv, RX[c], sbl, gain(4, l))
            hid = big
            wfr = rows(w_fi[l])
            wor = rows(w_fo[l])
            for jh in range(2):
                for j0, nj in ((0, 2), (2, 2), (4, 2), (6, 2), (8, 2), (10, 1)):
                    jg = jh * 11 + j0
                    si = get_w(("fi", l, jg), [
                        (lambda t, nj=nj: t[:, 0:16 * nj * 128].rearrange("p (k r c) -> p k r c", k=8, r=2)[:, :, 0, :], wfr[:, :, jg * 128:(jg + nj) * 128]),
                        (lambda t, nj=nj: t[:, 0:16 * nj * 128].rearrange("p (k r c) -> p k r c", k=8, r=2)[:, :, 1, :], wfr[:, :, 2816 + jg * 128:2816 + (jg + nj) * 128])])
                    wv = ring[si][:, 0:16 * nj * 128].rearrange("p (k r c) -> p k r c", k=8, r=2)
                    for jj in range(nj):
                        for lo, n in sbl:
                            b1, b2 = bank(), bank()
                            for k in range(8):
                                P.mm(ps[b1][:, 0:n], wv[:, k, 0, jj * 128:(jj + 1) * 128], hT[:, k, lo:lo + n], k == 0, k == 7, [RRING[si], RH], [RP[b1]], k == 7)
                            for k in range(8):
                                P.mm(ps[b2][:, 0:n], wv[:, k, 1, jj * 128:(jj + 1) * 128], hT[:, k, lo:lo + n], k == 0, k == 7, [RRING[si], RH], [RP[b2]], k == 7)
                            P.act(ft1[:, 0:n], ps[b1][:, 0:n], AF.Silu, [RP[b1]], [RFT1])
                            P.dve("tensor_tensor", [RFT1, RP[b2]], [RBIG], out=hid[:, j0 + jj, lo:lo + n], in0=ft1[:, 0:n], in1=ps[b2][:, 0:n], op=ALU.mult)
                for mb in range(4):
                    si = get_w(("fo", l, jh, mb), [(v3(11, 256), wor[:, jh * 11:(jh + 1) * 11, mb * 256:(mb + 1) * 256])])
                    wv = v3(11, 256)(ring[si])
                    for lo, n in sbl:
                        for mm_ in range(2):
                            m = mb * 2 + mm_
                            b = bank()
                            for k in range(11):
                                P.mm(ps[b][:, 0:n], wv[:, k, mm_ * 128:(mm_ + 1) * 128], hid[:, k, lo:lo + n], k == 0, k == 10, [RRING[si], RBIG], [RP[b]], k == 10)
                            if jh == 0:
                                P.act(stg[:, m, lo:lo + n], ps[b][:, 0:n], AF.Copy, [RP[b]], [RST])
                            else:
                                P.dve("tensor_tensor", [RST, RP[b]], [RST], out=stg[:, m, lo:lo + n], in0=stg[:, m, lo:lo + n], in1=ps[b][:, 0:n], op=ALU.add)
                                P.act(sqb[:, m, lo:lo + n], stg[:, m, lo:lo + n], AF.Square, [RST], [RSQ])
            post_norm_add(c, gain(5, l))
            ck("ffn%d" % c)

        def sb_sample(l, b_):
            pooldma(kcs[:, :, :], csk[l, b_].rearrange("(t p) f -> p t f", p=128), [RKCS], "kcs")
            pooldma(vcs[:, :, :], csv[l, b_].rearrange("(t p) f -> p t f", p=128), [RVCS], "vcs")
            for t in range(8):
                bk = bank()
                pv = ps[bk][:, :].bitcast(BF16)
                for j in range(3):
                    P.tr(pv[:, j * 128:(j + 1) * 128], kcs[:, t, j * 128:(j + 1) * 128], identb, [RKCS, RC], [RP[bk]], j == 2)
                P.act(kcsT[:, :, t * 128:(t + 1) * 128], pv[:, 0:384].rearrange("p (a b) -> p a b", a=3), AF.Copy, [RP[bk]], [RKCST])
            ck("ss1")
            bo = bank(hold=True)
            qc0 = 512 + 16 * b_
            NCOL = 96
            Rn = Rb[:16, 256:256 + NCOL]
            for kt in range(8, -1, -1):
                np_ = 16 if kt == 8 else 128
                bzs = [bank(), bank()]
                for par in range(2):
                    for hh in range(3):
                        h = 2 * hh + par
                        j, pb = h // 2, 64 * (h % 2)
                        qv = qT[pb:pb + 64, j, qc0:qc0 + 16]
                        if kt == 8:
                            kv = kT[pb:pb + 64, j, 2048 + 16 * b_:2048 + 16 * b_ + 16]
                        else:
                            kv = kcsT[pb:pb + 64, j, kt * 128:(kt + 1) * 128]
                        P.mm(ps[bzs[par]][:np_, hh * 16:hh * 16 + 16], kv, qv, True, True, [RKT, RKCST, RQ], [RP[bzs[par]]], hh == 2)
                if kt == 7:
                    ck("ss2")
                if kt == 8:
                    ck("ss3")
                r = 0
                for par in range(2):
                    P.act(ez[r][:np_, par * 48:par * 48 + 48], ps[bzs[par]][:np_, 0:48], AF.Exp, [RP[bzs[par]]], [REZ[r]], scale=0.125)
                if kt == 8:
                    for h in range(6):
                        P.dve("tensor_tensor", [REZ[r], RC], [REZ[r]], out=ez[r][:16, h * 16:h * 16 + 16], in0=ez[r][:16, h * 16:h * 16 + 16],
                              in1=maskb[:16, 0:16], op=ALU.mult)
                P.act(spb[r][:np_, 0:NCOL], ez[r][:np_, 0:NCOL], AF.Ln, [REZ[r]], [RSP[r]], bias=1.0)
                bc = bank()
                P.mm(ps[bc][:np_, 0:NCOL], trib[:np_, :np_], spb[r][:np_, 0:NCOL], True, kt == 8, [RSP[r], RC], [RP[bc]], kt == 8)
                if kt <= 7:
                    P.mm(ps[bc][:, 0:NCOL], onesb[:16, :], Rn, False, kt == 7, [RR, RC], [RP[bc]], kt == 7)
                if kt < 7:
                    P.mm(ps[bc][:, 0:NCOL], onesb, Rb[:, 0:NCOL], False, True, [RR, RC], [RP[bc]], True)
                ck("ss4")
                P.act(enb[r][:np_, 0:NCOL], ps[bc][:np_, 0:NCOL], AF.Exp, [RP[bc]], [REN[r]], scale=-1.0)
                P.dve("tensor_tensor", [REZ[r], REN[r]], [RWB[r]], out=wb[r][:np_, 0:NCOL], in0=ez[r][:np_, 0:NCOL], in1=enb[r][:np_, 0:NCOL], op=ALU.mult)
                if kt == 8:
                    P.dve("tensor_copy", [RSP[r]], [RR], out=Rn, in_=spb[r][:16, 0:NCOL])
                elif kt == 7:
                    P.dve("tensor_copy", [RSP[r]], [RR], out=Rb[:, 0:NCOL], in_=spb[r][:, 0:NCOL])
                elif kt > 0:
                    P.dve("tensor_tensor", [RSP[r], RR], [RR], out=Rb[:, 0:NCOL], in0=Rb[:, 0:NCOL], in1=spb[r][:, 0:NCOL], op=ALU.add)
                ck("ss5")
                for h in range(6):
                    j = h // 2
                    if kt == 8:
                        lv = vtoks[:16, b_, j * 128:(j + 1) * 128]
                    else:
                        lv = vcs[:, kt, j * 128:(j + 1) * 128]
                    hc = (h % 2) * 48 + (h // 2) * 16
                    P.mm(ps[bo][:, hc:hc + 16], lv, wb[r][:np_, hc:hc + 16], (kt == 8 and h == 0), kt == 0, [RVTS, RVCS, RWB[r]], [RP[bo]], (h == 5 and kt == 0), sgc=True)
            for h in range(6):
                j, pb = h // 2, 64 * (h % 2)
                hc = (h % 2) * 48 + (h // 2) * 16
                P.act(oaT[pb:pb + 64, j, qc0:qc0 + 16], ps[bo][pb:pb + 64, hc:hc + 16], AF.Copy, [RP[bo]], [ROA])
            held.discard(bo)

        def pool_branch(l, c, wr):
            si = get_w(("wu", l), [(v3(8, 256), wr[:, :, O_U:O_U + 256])])
            wu = v3(8, 256)(ring[si])
            W = 527 if c < 3 else PW
            if c > 0:
                P.act(uT[:, :, 0:15], uhalo[:, :, :], AF.Copy, [RUH], [RU])
            else:
                P.op("dve", lambda e: e.memset(uT[:, :, 0:15], 0.0), writes=[RU])
            segs = [(0, 512, 15)]
            if c == 3:
                segs += [(512, 16, 527 + 15), (528, 16, 527 + 31 + 15)]
                for b_ in range(2):
                    off = 527 + 31 * b_
                    for j in range(2):
                        spdma(uT[:, j, off:off + 15], spool[l, b_, :, j * 128:(j + 1) * 128].rearrange("t p -> p t"), [], [RU], "hist", nc_ok=True)
            for lo, n, dst in segs:
                for j in range(2):
                    b = bank()
                    for k in range(8):
                        P.mm(ps[b][:, 0:n], wu[:, k, j * 128:(j + 1) * 128], hT[:, k, lo:lo + n], k == 0, k == 7, [RRING[si], RH], [RP[b]], k == 7)
                    P.act(uT[:, j, dst:dst + n], ps[b][:, 0:n], AF.Copy, [RP[b]], [RU])
            P.act(uhalo[:, :, :], uT[:, :, 512:527], AF.Copy, [RU], [RUH])
            if c == 3:
                outs = [(512 - 15, opp[l]), (513, ops_[l, 0]), (529, ops_[l, 1])]
                for lo, od in outs:
                    b = bank()
                    for k in range(8):
                        P.mm(ps[b][:15, 0:256], hT[:, k, lo:lo + 15], wu[:, k, :], k == 0, k == 7, [RRING[si], RH], [RP[b]], k == 7)
                    P.act(pD[:15, 0:256], ps[b][:15, 0:256], AF.Copy, [RP[b]], [RPD])
                    spdma(od, pD[:15, 0:256], [RPD], [], "opool", out=True)
            P.dve("tensor_tensor", [RU], [RPA], out=pA[:, :, 1:W], in0=uT[:, :, 1:W], in1=uT[:, :, 0:W - 1], op=ALU.add)
            P.dve("tensor_tensor", [RPA], [RPB], out=pB[:, :, 3:W], in0=pA[:, :, 3:W], in1=pA[:, :, 1:W - 2], op=ALU.add)
            P.dve("tensor_tensor", [RPB], [RPC], out=pC[:, 7:W], in0=pB[:, 1, 7:W], in1=pB[:, 1, 3:W - 4], op=ALU.add)
            P.dve("tensor_tensor", [RPC], [RPD], out=pD[:, 15:W], in0=pC[:, 15:W], in1=pC[:, 7:W - 8], op=ALU.add)
            sel = [(0, 64, 0, pA[0:64, 0, :], 0.5, RPA), (64, 128, 0, pB[64:128, 0, :], 0.25, RPB), (0, 64, 1, pC[0:64, :], 0.125, RPC), (64, 128, 1, pD[64:128, :], 0.0625, RPD)]
            for p0, p1, j, sv, iw, rsv in sel:
                for lo, n, src in segs:
                    P.dve("scalar_tensor_tensor", [rsv, RU], [RPL], out=pooled[p0:p1, j, lo:lo + n], in0=sv[:, src:src + n], scalar=iw,
                          in1=uT[p0:p1, j, src:src + n], op0=ALU.mult, op1=ALU.subtract)
                if c == 0:
                    P.dve("tensor_tensor", [rsv, RC], [RPT_], out=ptmp[p0:p1, 0:15], in0=sv[:, 15:30], in1=invc[p0:p1, j, :], op=ALU.mult)
                    P.dve("tensor_tensor", [RPT_, RU], [RPL], out=pooled[p0:p1, j, 0:15], in0=ptmp[p0:p1, 0:15], in1=uT[p0:p1, j, 15:30], op=ALU.subtract)
            for lo, n in subs(c):
                for j in range(2):
                    b = bank()
                    P.mm(ps[b][:, 0:n], wpbd[:, j, :], pooled[:, j, lo:lo + n], True, True, [RWP, RPL], [RP[b]], True)
                    P.act(obT[:, j, lo:lo + n], ps[b][:, 0:n], AF.Copy, [RP[b], RC], [ROB], scale=psc[:, l, j:j + 1])

        def gla_half(l, c, hf, wr):
            base = 256 * hf
            sbl = [(base, 256, 0)]
            tiles = [(base, 128, 0, 0), (base + 128, 128, 1, 128)]
            samp = (c == 3 and hf == 1)
            if samp:
                sbl += [(512, 32, 256)]
                tiles += [(512, 16, 2, 256), (528, 16, 3, 272)]
            ncl = 288 if samp else 256

            def wG():
                return get_w(("gG", l), [(v3(8, 400), wr[:, :, O_GC:O_GC + 400])])

            def wK():
                return get_w(("gK", l), [(v3(8, 384), wr[:, :, O_KC:O_KC + 384])])

            def wV():
                return get_w(("gV", l), [(v3(8, 384), wr[:, :, O_VC:O_VC + 384])])

            def wQ():
                return get_w(("gQ", l), [(v3(8, 384), wr[:, :, O_QC:O_QC + 384])])

            s = wG(); w = v3(8, 400)(ring[s])
            for lo, n, lc in sbl:
                b = bank()
                for k in range(8):
                    P.mm(ps[b][:16, 0:n], w[:, k, 384:400], hT[:, k, lo:lo + n], k == 0, k == 7, [RRING[s], RH], [RP[b]], k == 7)
                P.act(lrT[:16, lo:lo + n], ps[b][:16, 0:n], AF.Copy, [RP[b]], [RLR])
            ck("g1")
            for lo, n, ti, lc in tiles:
                sm = ti >= 2
                bi = ti - 2
                vc_d = vcts[:, bi, :] if sm else vct[:, ti, :]
                la_d = lats[:, bi, :] if sm else lat[:, ti, :]
                kt_d = ktts[:, bi, :] if sm else ktt[:, ti, :]
                b = bank()
                P.mm(ps[b][:n, 0:384], lrT[0:17, lo:lo + n], wa2[0:17, :], True, True, [RLR, RWA2], [RP[b]], True)
                P.act(gt1[:n, 0:384], ps[b][:n, 0:384], AF.Exp, [RP[b]], [RGT1], scale=-1.0)
                P.act(la_d, gt1[:n, 0:384], AF.Ln, [RGT1], [RLAS if sm else RLA], bias=1.0)
                rla = RLAS if sm else RLA
                b = bank()
                P.mm(ps[b][:n, 0:384], tbd[:n, :n], la_d, True, True, [rla, RC], [RP[b]], True)
                P.act(ekt[:n, :], ps[b][:n, 0:384], AF.Exp, [RP[b]], [REKT])
                s = wK(); w = v3(8, 384)(ring[s])
                b = bank()
                for k in range(8):
                    P.mm(ps[b][:n, 0:384], hT[:, k, lo:lo + n], w[:, k, :], k == 0, k == 7, [RRING[s], RH], [RP[b]], k == 7)
                P.dve("tensor_tensor", [REKT, RP[b]], [RKTTS if sm else RKTT], out=kt_d, in0=ps[b][:n, 0:384], in1=ekt[:n, :], op=ALU.mult)
                s = wV(); w = v3(8, 384)(ring[s])
                b = bank()
                for k in range(8):
                    P.mm(ps[b][:n, 0:384], hT[:, k, lo:lo + n], w[:, k, :], k == 0, k == 7, [RRING[s], RH], [RP[b]], k == 7)
                P.act(vc_d, ps[b][:n, 0:384], AF.Copy, [RP[b]], [RVCTS if sm else RVCT])
                b = bank()
                for h in range(4):
                    P.mm(ps[b][:96, h * 128:h * 128 + n], la_d[:, h * 96:(h + 1) * 96], tbd[:n, :n], True, True, [rla, RC], [RP[b]], h == 3)
                pvw = ps[b][:96, :].rearrange("p (h t) -> p h t", h=4)[:, :, 0:n]
                P.act(eq[:, :, lc:lc + n], pvw, AF.Exp, [RP[b]], [REQ], scale=-1.0)
                P.act(eqi[:, :, lc:lc + n], pvw, AF.Exp, [RP[b]], [REQI])
            ck("g3")
            P.act(ebl[:, :, 0:4], eq[:, :, 63:256:64], AF.Copy, [REQ], [REBL])
            if samp:
                P.act(ebl[:, :, 4:6], eq[:, :, 271:288:16], AF.Copy, [REQ], [REBL])
            ub = []
            for g in range(4):
                ti, h2 = g // 2, g % 2
                b = bank(hold=True)
                ub.append(b)
                for h in range(4):
                    P.mm(ps[b][:96, h * 96:(h + 1) * 96], ktt[h2 * 64:(h2 + 1) * 64, ti, h * 96:(h + 1) * 96], vct[h2 * 64:(h2 + 1) * 64, ti, h * 96:(h + 1) * 96],
                         True, True, [RKTT, RVCT], [RP[b]], h == 3)
            for g in range(4):
                b = ub[g]
                P.act(Sbf[:, g], Sst[:], AF.Copy, [RS], [RSBF])
                P.dve("tensor_tensor", [RS, RP[b]], [RSTM], out=Stm[:, :, :], in0=Sst[:], in1=ps[b][:96, 0:384].rearrange("p (h v) -> p h v", h=4), op=ALU.add)
                held.discard(b)
                for h in range(4):
                    P.dve("tensor_scalar", [RSTM, REBL], [RS], out=Sst[:, h, :], in0=Stm[:, h, :], scalar1=ebl[:, h, g:g + 1], scalar2=None, op0=ALU.mult)
            if samp:
                spdma(ogp[l].rearrange("h k v -> k h v"), Sst[:], [RS], [], "ogp", out=True)
                for b_ in range(2):
                    spdma(Ss[:, :, :], sgla[l, b_].rearrange("h k v -> k h v"), [], [RSS], "sgla")
                    P.act(Sbf[:, 4 + b_], Ss[:, :, :], AF.Copy, [RSS], [RSBF])
                    b = bank()
                    for h in range(4):
                        P.mm(ps[b][:96, h * 96:(h + 1) * 96], ktts[:, b_, h * 96:(h + 1) * 96], vcts[:, b_, h * 96:(h + 1) * 96], True, True, [RKTTS, RVCTS], [RP[b]], h == 3)
                    P.dve("tensor_tensor", [RSS, RP[b]], [RSTM], out=Stm[:, :, :], in0=Ss[:, :, :], in1=ps[b][:96, 0:384].rearrange("p (h v) -> p h v", h=4), op=ALU.add)
                    for h in range(4):
                        P.dve("tensor_scalar", [RSTM, REBL], [RSS], out=Ss[:, h, :], in0=Stm[:, h, :], scalar1=ebl[:, h, 4 + b_:5 + b_], scalar2=None, op0=ALU.mult)
                    spdma(ogs[l, b_].rearrange("h k v -> k h v"), Ss[:, :, :], [RSS], [], "ogs", out=True)
            ck("g2")
            for lo, n, lc in sbl:
                for h in range(4):
                    s = wQ(); w = v3(8, 384)(ring[s])
                    b = bank()
                    for k in range(8):
                        P.mm(ps[b][:96, 0:n], w[:, k, h * 96:(h + 1) * 96], hT[:, k, lo:lo + n], k == 0, k == 7, [RRING[s], RH], [RP[b]], k == 7)
                    P.dve("scalar_tensor_tensor", [RP[b], REQ], [RQT], out=qtT[:, h, lc:lc + n], in0=ps[b][:96, 0:n], scalar=96.0 ** -0.5, in1=eq[:, h, lc:lc + n], op0=ALU.mult, op1=ALU.mult)
                    s = wK(); w = v3(8, 384)(ring[s])
                    b = bank()
                    for k in range(8):
                        P.mm(ps[b][:96, 0:n], w[:, k, h * 96:(h + 1) * 96], hT[:, k, lo:lo + n], k == 0, k == 7, [RRING[s], RH], [RP[b]], k == 7)
                    P.dve("tensor_tensor", [RP[b], REQI], [RKTT_], out=ktT[:, h, lc:lc + n], in0=ps[b][:96, 0:n], in1=eqi[:, h, lc:lc + n], op=ALU.mult)
                    s = wG(); w = v3(8, 400)(ring[s])
                    b = bank()
                    for k in range(8):
                        P.mm(ps[b][:96, 0:n], w[:, k, h * 96:(h + 1) * 96], hT[:, k, lo:lo + n], k == 0, k == 7, [RRING[s], RH], [RP[b]], k == 7)
                    P.act(sgc[:, h, lc:lc + n], ps[b][:96, 0:n], AF.Silu, [RP[b]], [RSG])
            ck("g4")
            bsets = [(attb, RATT, osb, ROS, osq, ROSQ, grs, RGRS), (attb2, RATT2, osb2, ROS2, osq2, ROSQ2, grs2, RGRS2)]

            def i_s1(tile, bs_, stt):
                lo, n, ti, lc = tile
                att_, ratt = bs_[0], bs_[1]
                b = bank()
                for h in range(4):
                    P.mm(ps[b][:n, h * 128:h * 128 + n], ktT[:, h, lc:lc + n], qtT[:, h, lc:lc + n], True, True, [RKTT_, RQT], [RP[b]], h == 3)
                for h in range(4):
                    P.dve("tensor_tensor", [RP[b], RC], [ratt], out=att_[:n, h, 0:n], in0=ps[b][:n, h * 128:h * 128 + n], in1=mbd[:n, 0:n], op=ALU.mult)

            def i_s2(tile, bs_, stt):
                lo, n, ti, lc = tile
                att_, ratt, osb_, ros, osq_, rosq = bs_[0:6]
                sm = ti >= 2
                bi = ti - 2
                vc_d = vcts[:, bi, :] if sm else vct[:, ti, :]
                rvc = RVCTS if sm else RVCT
                bo = bank()
                for h in range(4):
                    P.mm(ps[bo][:96, h * 128:h * 128 + n], vc_d[:, h * 96:(h + 1) * 96], att_[:n, h, 0:n], True, False, [rvc, ratt], [RP[bo]], False)
                    if sm:
                        P.mm(ps[bo][:96, h * 128:h * 128 + n], Sbf[:, 4 + bi, h, :], qtT[:, h, lc:lc + n], False, True, [RSBF, RQT], [RP[bo]], h == 3)
                    else:
                        for h2 in range(2):
                            g = ti * 2 + h2
                            P.mm(ps[bo][:96, h * 128 + h2 * 64:h * 128 + h2 * 64 + 64], Sbf[:, g, h, :], qtT[:, h, lc + h2 * 64:lc + h2 * 64 + 64], False, h2 == 1,
                                 [RSBF, RQT], [RP[bo]], h == 3 and h2 == 1)
                ov = ps[bo][:96, :].rearrange("p (h t) -> p h t", h=4)[:, :, 0:n]
                P.act(osb_[:, :, 0:n], ov, AF.Copy, [RP[bo]], [ros])
                P.act(osq_[:, :, 0:n], osb_[:, :, 0:n], AF.Square, [ros], [rosq])

            def i_s3(tile, bs_, stt):
                lo, n, ti, lc = tile
                osq_, rosq, grs_, rgrs = bs_[4:8]
                bs = bank()
                for h in range(4):
                    P.mm(ps[bs][:96, h * 128:h * 128 + n], onesb[:96, :96], osq_[:, h, 0:n], True, True, [rosq, RC], [RP[bs]], h == 3)
                sv = ps[bs][:96, :].rearrange("p (h t) -> p h t", h=4)[:, :, 0:n]
                gv = grs_[:, :].rearrange("p (h t) -> p h t", h=4)[:, :, 0:n]
                P.act(gv, sv, AF.Ln, [RP[bs], RC], [rgrs], scale=1.0 / 96, bias=epsb[:96, 0:1])
                P.act(gv, gv, AF.Exp, [rgrs], [rgrs], scale=-0.5)

            def i_s4(tile, bs_, stt):
                lo, n, ti, lc = tile
                osb_, ros, grs_, rgrs = bs_[2], bs_[3], bs_[6], bs_[7]
                gv = grs_[:, :].rearrange("p (h t) -> p h t", h=4)[:, :, 0:n]
                P.dve("tensor_tensor", [ros, rgrs], [ros], out=osb_[:, :, 0:n], in0=osb_[:, :, 0:n], in1=gv, op=ALU.mult)
                for h in range(4):
                    P.dve("scalar_tensor_tensor", [ros, RC, RSG], [ROC], out=ocT[:, h, lo:lo + n], in0=osb_[:, h, 0:n], scalar=gln[:, l, h:h + 1],
                          in1=sgc[:, h, lc:lc + n], op0=ALU.mult, op1=ALU.mult)

            for p0 in range(0, len(tiles), 2):
                pair = tiles[p0:p0 + 2]
                for stage in (i_s1, i_s2, i_s3, i_s4):
                    for k_, tile in enumerate(pair):
                        stage(tile, bsets[k_], None)

        def xattn(l, c):
            xv = xT[:, :, 512 * c:512 * c + CW]
            norm_to_h(xv, RX[c], subs(c), gain(2, l))
            qx = big[:, 0:8, :]
            for hf in range(2):
                si = get_w(("xq", l, hf), [(v3(8, 512), rows(w_xq[l])[:, :, hf * 512:(hf + 1) * 512])])
                wv = v3(8, 512)(ring[si])
                for lo, n in subs(c):
                    for mm_ in range(4):
                        b = bank()
                        for k in range(8):
                            P.mm(ps[b][:, 0:n], wv[:, k, mm_ * 128:(mm_ + 1) * 128], hT[:, k, lo:lo + n], k == 0, k == 7, [RRING[si], RH], [RP[b]], k == 7)
                        P.act(qx[:, hf * 4 + mm_, lo:lo + n], ps[b][:, 0:n], AF.Copy, [RP[b]], [RBIG])
            ox = hT
            blocks = [(0, 512, None)]
            if c == 3:
                blocks += [(512, 16, 0), (528, 16, 1)]
            for lo, n, sb_ in blocks:
                if sb_ is None:
                    mk_, mv_, rmk, rmv = mkT, mvt, RMK, RMV
                else:
                    pooldma(mkl[:, :, :], cmk[l, sb_].rearrange("(t p) f -> p t f", p=128), [RMKL], "mkl")
                    pooldma(mvs[:, :, :], cmv[l, sb_].rearrange("(t p) f -> p t f", p=128), [RMVS], "mvs")
                    for t in range(2):
                        for hb in range(2):
                            bk = bank()
                            pv = ps[bk][:, :].bitcast(BF16)
                            for j in range(4):
                                d = hb * 4 + j
                                P.tr(pv[:, j * 128:(j + 1) * 128], mkl[:, t, d * 128:(d + 1) * 128], identb, [RMKL, RC], [RP[bk]], j == 3)
                            P.act(mkTs[:, hb * 4:(hb + 1) * 4, t * 128:(t + 1) * 128], pv[:, 0:512].rearrange("p (a b) -> p a b", a=4), AF.Copy, [RP[bk]], [RMKTS])
                    mk_, mv_, rmk, rmv = mkTs, mvs, RMKTS, RMVS
                pts = [(pTb, RPT, rden, RRD), (pTb2, RPT2, rden2, RRD2)]

                def X1(h):
                    pT_, rpt = pts[h % 2][0], pts[h % 2][1]
                    for mt_ in range(2):
                        b = bank()
                        for dd in range(2):
                            P.mm(ps[b][:, 0:n], mk_[:, 2 * h + dd, mt_ * 128:(mt_ + 1) * 128], qx[:, 2 * h + dd, lo:lo + n], dd == 0, dd == 1, [rmk, RBIG], [RP[b]], dd == 1)
                        P.act(pT_[:, mt_, 0:n], ps[b][:, 0:n], AF.Exp, [RP[b]], [rpt], scale=1.0 / 16)

                def X2(h):
                    pT_, rpt, rd_, rrd = pts[h % 2]
                    bd = bank()
                    for mt_ in range(2):
                        P.mm(ps[bd][:, 0:n], onesb, pT_[:, mt_, 0:n], mt_ == 0, mt_ == 1, [rpt, RC], [RP[bd]], mt_ == 1)
                    P.act(rd_[:, 0:n], ps[bd][:, 0:n], AF.Ln, [RP[bd]], [rrd])
                    P.act(rd_[:, 0:n], rd_[:, 0:n], AF.Exp, [rrd], [rrd], scale=-1.0)
                    for dd in range(2):
                        b = bank()
                        for mt_ in range(2):
                            P.mm(ps[b][:, 0:n], mv_[:, mt_, (2 * h + dd) * 128:(2 * h + dd + 1) * 128], pT_[:, mt_, 0:n], mt_ == 0, mt_ == 1, [rmv, rpt], [RP[b]], mt_ == 1)
                        P.dve("tensor_tensor", [RP[b], rrd], [RH], out=ox[:, 2 * h + dd, lo:lo + n], in0=ps[b][:, 0:n], in1=rd_[:, 0:n], op=ALU.mult)

                for step in range(5):
                    if step < 4:
                        X1(step)
                    if step >= 1:
                        X2(step - 1)
            proj_to_stg(c, ("xo", l), w_xo[l], ox, RH)
            post_norm_add(c, gain(3, l))

        try:
            ck("setup")
            for l in range(nl):
                layer(l)
        except StopBuild as ex:
            print("STOPPED at", ex)

        for t in range(17 if STOP[0] is None else 0):
            rn = 128 if t < 16 else 32
            c = min(t // 4, 3)
            col = t * 128
            for hb in range(2):
                b = bank()
                for j in range(4):
                    k = hb * 4 + j
                    P.tr(ps[b][:rn, j * 128:(j + 1) * 128], xT[:, k, col:col + rn], ident, [RX[c], RC], [RP[b]], j == 3)
                P.act(stg[:rn, hb, 0:512], ps[b][:rn, :], AF.Copy, [RP[b]], [RST])
            dst = yp[t * 128:(t + 1) * 128, :] if t < 16 else ys
            spdma(dst.rearrange("p (a b) -> p a b", a=2), stg[:rn, 0:2, 0:512], [RST], [], "yout", out=True)
        P.emit()
    return nc


def _consts():
    c = np.zeros((128, C_END), np.float32)
    p = np.arange(128)[:, None]
    q = np.arange(128)[None, :]
    c[:, C_ID:C_ID + 128] = (p == q)
    c[:, C_TRI:C_TRI + 128] = (p >= q)
    c[:, C_ONE:C_ONE + 128] = 1.0
    c[:, C_MASK:C_MASK + 512] = (p < np.arange(512)[None, :])
    same = (p // 64) == (q // 64)
    c[:, C_TBD:C_TBD + 128] = np.where((p <= q) & same, 1.0 / 16.0, 0.0)
    c[:, C_MBD:C_MBD + 128] = ((p <= q) & same)
    inv = np.zeros((128, 2, 15), np.float32)
    for pp in range(128):
        for j in range(2):
            w = 2 ** (2 * j + (1 if pp >= 64 else 0) + 1)
            inv[pp, j] = 1.0 / np.minimum(np.arange(15) + 1, w)
    c[:, C_INV:C_INV + 30] = inv.reshape(128, 30)
    return c


_NC_CACHE = {}


def kernel(x_prompt, x_sample, mem_prompt, cache_sb_k, cache_sb_v, state_pool, state_gla,
           cache_mem_k, cache_mem_v, w_in, w_gla_a2, b_gla_a, gla_norm, w_pool, pool_scale,
           w_branch_a, w_branch_b, w_branch_c, w_mix_out, mem_norm, w_xq, w_xk, w_xv, w_xo,
           w_ffn_in, w_ffn_out, norm_mix_pre, norm_mix_post, norm_x_pre, norm_x_post,
           norm_ffn_pre, norm_ffn_post, _nl=NL):
    f = lambda a: np.ascontiguousarray(np.asarray(a, dtype=np.float32))
    fl = lambda a: np.ascontiguousarray(np.asarray(a, dtype=np.float32)[:_nl])
    if _nl not in _NC_CACHE:
        _NC_CACHE[_nl] = build(_nl)
    nc = _NC_CACHE[_nl]
    shared = dict(w_in=fl(w_in), w_gla_a2=fl(w_gla_a2), b_gla_a=fl(b_gla_a), gla_norm=fl(gla_norm), w_pool=fl(w_pool),
                  pool_scale=fl(pool_scale), w_branch_a=fl(w_branch_a), w_branch_b=fl(w_branch_b), w_branch_c=fl(w_branch_c),
                  w_mix_out=fl(w_mix_out), mem_norm=fl(mem_norm), w_xq=fl(w_xq), w_xk=fl(w_xk), w_xv=fl(w_xv), w_xo=fl(w_xo),
                  w_ffn_in=fl(w_ffn_in), w_ffn_out=fl(w_ffn_out), norm_mix_pre=fl(norm_mix_pre), norm_mix_post=fl(norm_mix_post),
                  norm_x_pre=fl(norm_x_pre), norm_x_post=fl(norm_x_post), norm_ffn_pre=fl(norm_ffn_pre), norm_ffn_post=fl(norm_ffn_post),
                  cst=_consts())
    x_prompt = f(x_prompt); x_sample = f(x_sample); mem_prompt = f(mem_prompt)
    cache_sb_k = fl(cache_sb_k); cache_sb_v = fl(cache_sb_v); state_pool = fl(state_pool); state_gla = fl(state_gla)
    cache_mem_k = fl(cache_mem_k); cache_mem_v = fl(cache_mem_v)
    in_maps = []
    for i in range(8):
        s2 = slice(2 * i, 2 * i + 2)
        d = dict(shared)
        d.update(xp=x_prompt[i], xs=np.ascontiguousarray(x_sample[s2].reshape(32, 1024)), mem=mem_prompt[i],
                 csk=np.ascontiguousarray(cache_sb_k[:, s2].reshape(_nl, 2, 1024, 384)),
                 csv=np.ascontiguousarray(cache_sb_v[:, s2].reshape(_nl, 2, 1024, 384)),
                 spool=np.ascontiguousarray(state_pool[:, s2]), sgla=np.ascontiguousarray(state_gla[:, s2]),
                 cmk=np.ascontiguousarray(cache_mem_k[:, s2].reshape(_nl, 2, 256, 1024)),
                 cmv=np.ascontiguousarray(cache_mem_v[:, s2].reshape(_nl, 2, 256, 1024)))
        in_maps.append(d)
    res = run_bass_kernel_spmd(nc, in_maps, core_ids=list(range(8)))
    R = res.results
    def cat(k, ax=0):
        a = np.stack([r[k] for r in R], axis=ax)
        if ax == 1 and a.shape[0] < 4:
            a = np.concatenate([a, np.zeros((4 - a.shape[0],) + a.shape[1:], a.dtype)], axis=0)
        return a
    y_p = cat("yp")
    y_s = cat("ys").reshape(16, 16, 1024)
    kp = cat("okp", 1).reshape(4, 8, 2048, 6, 64)
    vp = cat("ovp", 1).reshape(4, 8, 2048, 6, 64)
    pp = cat("opp", 1)
    gp = cat("ogp", 1)
    mk = cat("omk", 1).reshape(4, 8, 256, 4, 256)
    mv = cat("omv", 1).reshape(4, 8, 256, 4, 256)
    ks = cat("oks", 1).reshape(4, 16, 16, 6, 64)
    vs = cat("ovs", 1).reshape(4, 16, 16, 6, 64)
    pls = cat("ops", 1).reshape(4, 16, 15, 256)
    gs = cat("ogs", 1).reshape(4, 16, 4, 96, 96)
    return (y_p, y_s, kp, vp, pp, gp, mk, mv, ks, vs, pls, gs)
```

```python
import numpy as np
from contextlib import ExitStack
import concourse.bass as bass
import concourse.mybir as mybir
from concourse.bass_utils import run_bass_kernel_spmd

F32 = mybir.dt.float32
BF16 = mybir.dt.bfloat16
AF = mybir.ActivationFunctionType
ALU = mybir.AluOpType

NL = 4
EPS = 1e-6
NT = 2080
CW = 544
O_Q, O_K, O_V, O_U = 0, 384, 768, 1152
O_QC, O_KC, O_VC, O_GC, O_LR, O_G = 1408, 1792, 2176, 2560, 2944, 2960
C_ID, C_TRI, C_ONE, C_MASK, C_TBD, C_MBD, C_INV, C_END = 0, 128, 256, 384, 896, 1024, 1152, 1182


class Res:
    __slots__ = ("w", "r", "name")

    def __init__(self, name=""):
        self.w = None
        self.r = {}
        self.name = name


class PRes(Res):
    __slots__ = ()


class ARes(Res):
    __slots__ = ("lo", "hi")

    def __init__(self, name, lo, hi):
        Res.__init__(self, name)
        self.lo = lo
        self.hi = hi


class DmaSem:
    def __init__(self, sem):
        self.sem = sem
        self.count = 0


class Prog:
    def __init__(self, nc, es):
        self.nc = nc
        self.es = es
        self.engs = {"sp": nc.sync, "act": nc.scalar, "pool": nc.gpsimd, "dve": nc.vector, "pe": nc.tensor}
        self.q = {e: [] for e in self.engs}
        self.sem = {e: es.enter_context(nc.semaphore("S_" + e)) for e in ("act", "pool", "dve", "pe")}
        self.cnt = {e: 0 for e in self.sem}
        self.seen = {e: {} for e in self.engs}
        self.out_sems = []
        self.nsem = 0
        self.arena = []

    def dma_sem(self, out=False):
        self.nsem += 1
        s = DmaSem(self.es.enter_context(self.nc.semaphore("D%d" % self.nsem)))
        if out:
            self.out_sems.append(s)
        return s

    def op(self, eng, fn, reads=(), writes=(), inc=True, dma=None):
        waits = {}
        own = self.sem.get(eng)

        def need(ev):
            if ev is None:
                return
            s, v = ev
            if s is own and dma is None and (eng == "pe" or v > self.cnt[eng]):
                return
            k = id(s)
            if k not in waits or waits[k][1] < v:
                waits[k] = (s, v)

        pr = [r for r in reads if isinstance(r, PRes)]
        if pr:
            reads = [r for r in reads if not isinstance(r, PRes)]
            writes = list(writes) + pr
        for r in reads:
            need(r.w)
        for w in writes:
            need(w.w)
            for ev in w.r.values():
                need(ev)
            if isinstance(w, ARes):
                for o in self.arena:
                    if o is not w and o.lo < w.hi and w.lo < o.hi:
                        need(o.w)
                        for ev in o.r.values():
                            need(ev)
        wl = []
        for k, (s, v) in waits.items():
            if self.seen[eng].get(k, 0) < v:
                self.seen[eng][k] = v
                wl.append((s, v))
        if dma is not None:
            dma.count += 16
            ev = (dma.sem, dma.count)
            incv = 16
        else:
            if inc:
                self.cnt[eng] += 1
                ev = (self.sem[eng], self.cnt[eng])
            else:
                ev = (self.sem[eng], self.cnt[eng] + 1)
            incv = 1
        for r in reads:
            k = id(ev[0])
            if k not in r.r or r.r[k][1] < ev[1]:
                r.r[k] = ev
        for w in writes:
            w.w = ev
            w.r = {}
        self.q[eng].append((wl, fn, (ev[0], incv) if (inc or dma is not None) else None))

    def mm(self, out, lhsT, rhs, start, stop, reads, writes, inc, sgc=False):
        if sgc:
            self.op("pe", lambda e: e.matmul(out, lhsT=lhsT, rhs=rhs, start=start, stop=stop, skip_group_check=True), reads, writes, inc)
        else:
            self.op("pe", lambda e: e.matmul(out, lhsT=lhsT, rhs=rhs, start=start, stop=stop), reads, writes, inc)

    def tr(self, out, in_, ident, reads, writes, inc):
        self.op("pe", lambda e: e.transpose(out, in_, ident), reads, writes, inc)

    def act(self, out, in_, func, reads, writes, **kw):
        self.op("act", lambda e: e.activation(out=out, in_=in_, func=func, **kw), reads, writes)

    def dve(self, name, reads, writes, **kw):
        self.op("dve", lambda e: getattr(e, name)(**kw), reads, writes)

    def emit(self):
        nc = self.nc
        finals = [(s.sem, s.count) for s in self.out_sems if s.count > 0]
        with nc.Block() as block:
            def run(name):
                def f(eng):
                    for wl, fn, inc in self.q[name]:
                        for s, v in wl:
                            eng.wait_ge(s, v)
                        ins = fn(eng)
                        if inc is not None:
                            ins.then_inc(inc[0], inc[1])
                    if name == "sp":
                        for s, v in finals:
                            eng.wait_ge(s, v)
                return f

            block.sync(run("sp"))
            block.scalar(run("act"))
            block.gpsimd(run("pool"))
            block.vector(run("dve"))
            block.tensor(run("pe"))


def subs(c):
    return [(0, 512)] + ([(512, 32)] if c == 3 else [])


_LASTP = [None]


class StopBuild(Exception):
    pass


STOP = [None]
MARKS = []


def ck(name):
    if _LASTP[0] is not None:
        MARKS.append((name, len(_LASTP[0].q["pe"]), len(_LASTP[0].q["act"]), len(_LASTP[0].q["dve"])))
    if STOP[0] == name:
        raise StopBuild(name)


def build(nl=NL):
    nc = bass.Bass("TRN2", target_bir_lowering=False)
    es = ExitStack()

    def din(name, shape):
        return nc.dram_tensor(name, list(shape), F32, kind="ExternalInput").ap()

    def dout(name, shape):
        return nc.dram_tensor(name, list(shape), F32, kind="ExternalOutput").ap()

    xp = din("xp", [2048, 1024]); xs = din("xs", [32, 1024]); mem = din("mem", [256, 1024])
    csk = din("csk", [nl, 2, 1024, 384]); csv = din("csv", [nl, 2, 1024, 384])
    spool = din("spool", [nl, 2, 15, 256]); sgla = din("sgla", [nl, 2, 4, 96, 96])
    cmk = din("cmk", [nl, 2, 256, 1024]); cmv = din("cmv", [nl, 2, 256, 1024])
    w_in = din("w_in", [nl, 1024, 6032]); w_a2 = din("w_gla_a2", [nl, 16, 384]); b_a = din("b_gla_a", [nl, 384])
    gla_norm = din("gla_norm", [nl, 384]); w_pool = din("w_pool", [nl, 4, 64, 64]); pool_scale = din("pool_scale", [nl, 256])
    w_ba = din("w_branch_a", [nl, 384, 1024]); w_bb = din("w_branch_b", [nl, 256, 1024]); w_bc = din("w_branch_c", [nl, 384, 1024])
    w_mix = din("w_mix_out", [nl, 1024, 1024]); mem_norm = din("mem_norm", [nl, 1024])
    w_xq = din("w_xq", [nl, 1024, 1024]); w_xk = din("w_xk", [nl, 1024, 1024]); w_xv = din("w_xv", [nl, 1024, 1024]); w_xo = din("w_xo", [nl, 1024, 1024])
    w_fi = din("w_ffn_in", [nl, 1024, 5632]); w_fo = din("w_ffn_out", [nl, 2816, 1024])
    gnames = ["norm_mix_pre", "norm_mix_post", "norm_x_pre", "norm_x_post", "norm_ffn_pre", "norm_ffn_post"]
    gains = [din(n, [nl, 1024]) for n in gnames]
    cst = din("cst", [128, C_END])

    yp = dout("yp", [2048, 1024]); ys = dout("ys", [32, 1024])
    okp = dout("okp", [nl, 2048, 384]); ovp = dout("ovp", [nl, 2048, 384])
    opp = dout("opp", [nl, 15, 256]); ogp = dout("ogp", [nl, 4, 96, 96])
    omk = dout("omk", [nl, 256, 1024]); omv = dout("omv", [nl, 256, 1024])
    oks = dout("oks", [nl, 32, 384]); ovs = dout("ovs", [nl, 32, 384])
    ops_ = dout("ops", [nl, 2, 15, 256]); ogs = dout("ogs", [nl, 2, 4, 96, 96])

    with es:
        P = Prog(nc, es)
        _LASTP[0] = P

        def sb(name, shape, dt):
            return es.enter_context(nc.sbuf_tensor(name, shape, dt))

        xT = sb("xT", [128, 8, NT], F32)
        RX = [Res("x%d" % c) for c in range(4)]
        kT = sb("kT", [128, 3, NT], BF16); RKT = Res("kT")
        vtok = sb("vtok", [128, 16, 384], BF16); RVT = Res("vtok")
        vtoks = sb("vtoks", [16, 2, 384], BF16); RVTS = Res("vtoks")
        hT = sb("hT", [128, 8, CW], BF16); RH = Res("hT")
        rstd = sb("rstd", [128, CW], F32); RRS = Res("rstd")
        big = sb("big", [128, 11, CW], BF16); RBIG = Res("big")
        oaT = sb("oaT", [128, 3, CW], BF16); ROA = Res("oaT")
        obT = sb("obT", [128, 2, CW], BF16); ROB = Res("obT")
        ocT = sb("ocT", [96, 4, CW], BF16); ROC = Res("ocT")
        cstb = sb("cstb", [128, C_INV], BF16); RC = Res("cst")
        identf = sb("identf", [128, 128], F32)
        invcf = sb("invcf", [128, 30], F32)
        gall = sb("gall", [128, 7, 4, 8], F32)
        gln = sb("gln", [96, 4, 4], F32)
        psc = sb("psc", [128, 4, 2], F32)
        epsb = sb("epsb", [128, 1], F32)
        mkT = sb("mkT", [128, 8, 256], BF16); RMK = Res("mkT")
        mvt = sb("mvt", [128, 2, 1024], BF16); RMV = Res("mvt")
        wpbd = sb("wpbd", [128, 2, 128], BF16); RWP = Res("wpbd")
        lrT = sb("lrT", [32, CW], BF16); RLR = Res("lrT")
        wa2 = sb("wa2", [32, 384], BF16); RWA2 = Res("wa2")
        ebl = sb("ebl", [96, 4, 8], F32); REBL = Res("ebl")
        Sst = sb("Sst", [96, 4, 96], F32); RS = Res("S")
        uhalo = sb("uhalo", [128, 2, 15], F32); RUH = Res("uhalo")
        NS = 3
        SLOT = 4352
        ring = [sb("ring%d" % i, [128, SLOT], BF16) for i in range(NS)]
        RRING = [Res("ring%d" % i) for i in range(NS)]
        DRING = [P.dma_sem() for _ in range(NS)]
        AR = 40960
        arena = sb("arena", [128, AR // 2], BF16)
        print("sbuf remaining", nc.sbuf_bytes_remaining)

        def av(name, off, shape, dt):
            esz = 4 if dt == F32 else 2
            nel = 1
            for d in shape[1:]:
                nel *= d
            assert off % 4 == 0 and off + nel * esz <= AR, (name, off, nel * esz)
            a = arena[:shape[0], off // 2:off // 2 + nel * esz // 2]
            if dt == F32:
                a = a.bitcast(F32)
            if len(shape) == 3:
                a = a.rearrange("p (a b) -> p a b", a=shape[1])
            elif len(shape) == 4:
                a = a.rearrange("p (a b c) -> p a b c", a=shape[1], b=shape[2])
            r = ARes(name, off, off + nel * esz)
            P.arena.append(r)
            return a, r

        stg, RST = av("stg", 0, [128, 8, CW], F32)
        sqb, RSQ = av("sqb", 17408, [128, 8, CW], BF16)
        memT, RMEM = av("memT", 26112, [128, 8, 256], F32)
        mnT, RMN = av("mnT", 34304, [128, 8, 256], BF16)
        cstf, RCF = av("cstf", 0, [128, C_END], F32)
        qT, RQ = av("qT", 0, [128, 3, CW], BF16)
        ez, REZ, spb, RSP, enb, REN, wb, RWB = [], [], [], [], [], [], [], []
        for i in range(3):
            a, r = av("ez%d" % i, 3264 + 2048 * i, [128, 512], F32); ez.append(a); REZ.append(r)
            a, r = av("sp%d" % i, 9408 + 1024 * i, [128, 512], BF16); spb.append(a); RSP.append(r)
            a, r = av("wb%d" % i, 16576 + 1024 * i, [128, 512], BF16); wb.append(a); RWB.append(r)
        for i in range(2):
            a, r = av("en%d" % i, 12480 + 2048 * i, [128, 512], F32); enb.append(a); REN.append(r)
        Rb, RR = av("Rb", 19648, [128, 512], BF16)
        kcs, RKCS = av("kcs", 20672, [128, 8, 384], BF16)
        kcsT, RKCST = av("kcsT", 26816, [128, 3, 1024], BF16)
        vcs, RVCS = av("vcs", 32960, [128, 8, 384], BF16)
        PW = 15 + 512 + 62
        uT, RU = av("uT", 0, [128, 2, PW], F32)
        pA, RPA = av("pA", 4712, [128, 2, PW], F32)
        pB, RPB = av("pB", 9424, [128, 2, PW], F32)
        pC, RPC = av("pC", 14136, [128, PW], F32)
        pD, RPD = av("pD", 16492, [128, PW], F32)
        pooled, RPL = av("pooled", 18848, [128, 2, CW], BF16)
        ptmp, RPT_ = av("ptmp", 21024, [128, 16], F32)
        HWD = 288
        qtT, RQT = av("qtT", 0, [96, 4, HWD], BF16)
        ktT, RKTT_ = av("ktT", 2304, [96, 4, HWD], BF16)
        sgc, RSG = av("sgc", 4608, [96, 4, HWD], BF16)
        eq, REQ = av("eq", 6912, [96, 4, HWD], F32)
        eqi, REQI = av("eqi", 11520, [96, 4, HWD], BF16)
        vct, RVCT = av("vct", 13824, [128, 2, 384], BF16)
        lat, RLA = av("lat", 15360, [128, 2, 384], BF16)
        ktt, RKTT = av("ktt", 16896, [128, 2, 384], BF16)
        ekt, REKT = av("ekt", 18432, [128, 384], F32)
        gt1, RGT1 = av("gt1", 19968, [128, 512], F32)
        vcts, RVCTS = av("vcts", 22016, [16, 2, 384], BF16)
        lats, RLAS = av("lats", 23552, [16, 2, 384], BF16)
        ktts, RKTTS = av("ktts", 25088, [16, 2, 384], BF16)
        Sbf, RSBF = av("Sbf", 26624, [96, 6, 4, 96], BF16)
        attb, RATT = av("attb", 31232, [128, 4, 128], BF16)
        osb, ROS = av("osb", 32256, [96, 4, 128], F32)
        osq, ROSQ = av("osq", 34304, [96, 4, 128], BF16)
        grs, RGRS = av("grs", 35328, [96, 512], F32)
        attb2, RATT2 = av("attb2", 20480, [128, 4, 128], BF16)
        osb2, ROS2 = av("osb2", 18432, [96, 4, 128], F32)
        osq2, ROSQ2 = av("osq2", 15360, [96, 4, 128], BF16)
        grs2, RGRS2 = av("grs2", 16384, [96, 512], F32)
        Stm, RSTM = av("Stm", 37376, [96, 4, 96], F32)
        Ss, RSS = av("Ss", 38912, [96, 4, 96], F32)
        sg3, RSG3 = [], []
        for i in range(3):
            a, r = av("sg%d" % i, 26112 + 2048 * i, [128, 512], F32); sg3.append(a); RSG3.append(r)
        mt1, RMT1 = av("mt1", 32256, [128, 512], F32)
        mt2, RMT2 = av("mt2", 34304, [128, 512], F32)
        pTb, RPT = av("pTb", 26112, [128, 2, 512], BF16)
        rden, RRD = av("rden", 28160, [128, 512], F32)
        pTb2, RPT2 = av("pTb2", 17408, [128, 2, 512], BF16)
        rden2, RRD2 = av("rden2", 19456, [128, 512], F32)
        mkl, RMKL = av("mkl", 0, [128, 2, 1024], BF16)
        mkTs, RMKTS = av("mkTs", 30208, [128, 8, 256], BF16)
        mvs, RMVS = av("mvs", 34304, [128, 2, 1024], BF16)
        ft1, RFT1 = av("ft1", 26112, [128, 512], F32)

        ps = [es.enter_context(nc.psum_tensor("ps%d" % i, [128, 512], F32)) for i in range(8)]
        RP = [PRes("ps%d" % i) for i in range(8)]
        bank_i = [0]

        held = set()

        def bank(hold=False):
            while True:
                b = bank_i[0] % 8
                bank_i[0] += 1
                if b not in held:
                    break
            if hold:
                held.add(b)
            return b

        ring_i = [0]
        wres = {}
        slot_key = [None] * NS

        def get_w(key, loads):
            if key in wres:
                return wres[key]
            si = ring_i[0] % NS
            ring_i[0] += 1
            if slot_key[si] is not None:
                del wres[slot_key[si]]
            slot_key[si] = key
            wres[key] = si
            for dstf, src in loads:
                dst = dstf(ring[si])
                P.op("pool", lambda e, dst=dst, src=src: e.dma_start(out=dst, in_=src), writes=[RRING[si]], dma=DRING[si])
            return si

        def v3(kc, ncols, off=0):
            return lambda t: t[:, off:off + kc * ncols].rearrange("p (k c) -> p k c", k=kc)

        def rows(w2d):
            return w2d.rearrange("(k p) c -> p k c", p=128)

        dsm = {}
        P._dsm = dsm

        def dsem(name, out=False):
            if name not in dsm:
                dsm[name] = P.dma_sem(out=out)
            return dsm[name]

        def spdma(dst, src, reads, writes, name, out=False, nc_ok=False):
            if nc_ok:
                P.op("sp", lambda e: e.dma_start(out=dst, in_=src, allow_slow_non_contiguous=True), reads, writes, dma=dsem(name, out))
            else:
                P.op("sp", lambda e: e.dma_start(out=dst, in_=src), reads, writes, dma=dsem(name, out))

        def pooldma(dst, src, writes, name):
            P.op("pool", lambda e: e.dma_start(out=dst, in_=src), writes=writes, dma=dsem(name))

        ident = identf[:, :]
        identb = cstb[:, C_ID:C_ID + 128]
        trib = cstb[:, C_TRI:C_TRI + 128]
        onesb = cstb[:, C_ONE:C_ONE + 128]
        maskb = cstb[:, C_MASK:C_MASK + 512]
        tbd = cstb[:, C_TBD:C_TBD + 128]
        mbd = cstb[:, C_MBD:C_MBD + 128]
        invc = invcf[:, :].rearrange("p (j t) -> p j t", j=2)

        spdma(cstf[:, :], cst, [], [RCF], "cstf")
        P.act(cstb[:, :], cstf[:, 0:C_INV], AF.Copy, [RCF], [RC])
        P.act(identf[:, :], cstf[:, C_ID:C_ID + 128], AF.Copy, [RCF], [RC])
        P.act(invcf[:, :], cstf[:, C_INV:C_INV + 30], AF.Copy, [RCF], [RC])
        P.op("dve", lambda e: e.memset(epsb[:], EPS), writes=[RC])
        P.op("dve", lambda e: e.memset(lrT[:], 1.0), writes=[RLR])
        P.op("dve", lambda e: e.memset(wpbd[:], 0.0), writes=[RWP])
        for i, g in enumerate(gains + [mem_norm]):
            spdma(gall[:, i, 0:nl, :], g.rearrange("l (k p) -> p l k", p=128), [], [RC], "cst", nc_ok=True)
        spdma(gln[:, 0:nl, :], gla_norm.rearrange("l (h v) -> v l h", v=96), [], [RC], "cst", nc_ok=True)
        spdma(psc[:, 0:nl, :], pool_scale.rearrange("l (j p) -> p l j", p=128), [], [RC], "cst", nc_ok=True)

        def gain(i, l):
            return gall[:, i, l, :]

        def stats(src3, rsrc, lo, n, scale):
            b = bank()
            for k in range(8):
                P.mm(ps[b][:, 0:n], onesb, src3[:, k, lo:lo + n], k == 0, k == 7, [rsrc, RC], [RP[b]], k == 7)
            P.act(rstd[:, lo:lo + n], ps[b][:, 0:n], AF.Ln, [RP[b], RC], [RRS], scale=scale, bias=epsb[:, 0:1])
            P.act(rstd[:, lo:lo + n], rstd[:, lo:lo + n], AF.Exp, [RRS], [RRS], scale=-0.5)

        def norm_to_h(xv, rx, sbl, g):
            for lo, n in sbl:
                P.act(hT[:, :, lo:lo + n], xv[:, :, lo:lo + n], AF.Square, [rx], [RH])
                stats(hT, RH, lo, n, 1.0 / 1024)
                for k in range(8):
                    P.dve("scalar_tensor_tensor", [rx, RRS, RC], [RH], out=hT[:, k, lo:lo + n], in0=xv[:, k, lo:lo + n],
                          scalar=g[:, k:k + 1], in1=rstd[:, lo:lo + n], op0=ALU.mult, op1=ALU.mult)

        def post_norm_add(c, g):
            xv = xT[:, :, 512 * c:512 * c + CW]
            for lo, n in subs(c):
                stats(sqb, RSQ, lo, n, 1.0 / 1024)
                for k in range(8):
                    P.dve("tensor_tensor", [RST, RRS], [RST], out=stg[:, k, lo:lo + n], in0=stg[:, k, lo:lo + n], in1=rstd[:, lo:lo + n], op=ALU.mult)
                    P.dve("scalar_tensor_tensor", [RST, RC, RX[c]], [RX[c]], out=xv[:, k, lo:lo + n], in0=stg[:, k, lo:lo + n],
                          scalar=g[:, k:k + 1], in1=xv[:, k, lo:lo + n], op0=ALU.mult, op1=ALU.add)

        def proj_to_stg(c, key, wsrc2d, src3, rsrc):
            for hf in range(2):
                si = get_w((key, hf), [(v3(8, 512), rows(wsrc2d)[:, :, hf * 512:(hf + 1) * 512])])
                wv = v3(8, 512)(ring[si])
                for lo, n in subs(c):
                    for mm_ in range(4):
                        m = hf * 4 + mm_
                        b = bank()
                        for k in range(8):
                            P.mm(ps[b][:, 0:n], wv[:, k, mm_ * 128:(mm_ + 1) * 128], src3[:, k, lo:lo + n], k == 0, k == 7, [RRING[si], rsrc], [RP[b]], k == 7)
                        P.act(stg[:, m, lo:lo + n], ps[b][:, 0:n], AF.Copy, [RP[b]], [RST])
                        P.act(sqb[:, m, lo:lo + n], stg[:, m, lo:lo + n], AF.Square, [RST], [RSQ])

        for t in range(17):
            rn = 128 if t < 16 else 32
            src = xp[t * 128:(t + 1) * 128, :] if t < 16 else xs
            spdma(stg[:rn, 0:2, 0:512], src.rearrange("p (a b) -> p a b", a=2), [], [RST], "xin")
            c = min(t // 4, 3)
            col = t * 128
            for hb in range(2):
                b = bank()
                for j in range(4):
                    P.tr(ps[b][:, j * rn:(j + 1) * rn], stg[:rn, hb, j * 128:(j + 1) * 128], ident[:rn, :rn], [RST, RC], [RP[b]], j == 3)
                P.act(xT[:, hb * 4:(hb + 1) * 4, col:col + rn], ps[b][:, 0:4 * rn].rearrange("p (a b) -> p a b", a=4), AF.Copy, [RP[b]], [RX[c]])

        def layer(l):
            wr = rows(w_in[l])
            for t in range(2):
                spdma(stg[:, 0:2, 0:512], mem[t * 128:(t + 1) * 128, :].rearrange("p (a b) -> p a b", a=2), [], [RST], "xin")
                for hb in range(2):
                    b = bank()
                    for j in range(4):
                        P.tr(ps[b][:, j * 128:(j + 1) * 128], stg[:, hb, j * 128:(j + 1) * 128], ident, [RST, RC], [RP[b]], j == 3)
                    P.act(memT[:, hb * 4:(hb + 1) * 4, t * 128:(t + 1) * 128], ps[b][:, 0:512].rearrange("p (a b) -> p a b", a=4), AF.Copy, [RP[b]], [RMEM])
            ck("mem1")
            P.act(mnT[:, :, :], memT[:, :, :], AF.Square, [RMEM], [RMN])
            stats(mnT, RMN, 0, 256, 1.0 / 1024)
            for k in range(8):
                P.dve("scalar_tensor_tensor", [RMEM, RC, RRS], [RMN], out=mnT[:, k, :], in0=memT[:, k, :], scalar=gain(6, l)[:, k:k + 1],
                      in1=rstd[:, 0:256], op0=ALU.mult, op1=ALU.mult)
            ck("mem2")
            for which, wsrc, odram in ((0, w_xk, omk), (1, w_xv, omv)):
                if which == 1:
                    ck("mem3")
                for hf in range(2):
                    si = get_w(("xkv", which, hf), [(v3(8, 512), rows(wsrc[l])[:, :, hf * 512:(hf + 1) * 512])])
                    wv = v3(8, 512)(ring[si])
                    if which == 0:
                        for jj in range(4):
                            b = bank()
                            for k in range(8):
                                P.mm(ps[b][:, 0:256], wv[:, k, jj * 128:(jj + 1) * 128], mnT[:, k, :], k == 0, k == 7, [RRING[si], RMN], [RP[b]], k == 7)
                            P.act(mkT[:, hf * 4 + jj, :], ps[b][:, 0:256], AF.Copy, [RP[b]], [RMK])
                    for t in range(2):
                        b = bank()
                        for k in range(8):
                            P.mm(ps[b][:, :], mnT[:, k, t * 128:(t + 1) * 128], wv[:, k, :], k == 0, k == 7, [RRING[si], RMN], [RP[b]], k == 7)
                        P.act(stg[:, t, 0:512], ps[b][:, :], AF.Copy, [RP[b]], [RST])
                        if which == 1:
                            P.dve("tensor_copy", [RP[b]], [RMV], out=mvt[:, t, hf * 512:(hf + 1) * 512], in_=ps[b][:, :])
                    spdma(odram[l, :, hf * 512:(hf + 1) * 512].rearrange("(t p) f -> p t f", p=128), stg[:, 0:2, 0:512], [RST], [], "omem", out=True)
            ck("memkv")
            for c in range(4):
                xv = xT[:, :, 512 * c:512 * c + CW]
                norm_to_h(xv, RX[c], subs(c), gain(0, l))
                sk = get_w(("wk", l), [(v3(8, 384), wr[:, :, O_K:O_K + 384])])
                wk = v3(8, 384)(ring[sk])
                for lo, n in subs(c):
                    for j in range(3):
                        b = bank()
                        for k in range(8):
                            P.mm(ps[b][:, 0:n], wk[:, k, j * 128:(j + 1) * 128], hT[:, k, lo:lo + n], k == 0, k == 7, [RRING[sk], RH], [RP[b]], k == 7)
                        P.act(kT[:, j, 512 * c + lo:512 * c + lo + n], ps[b][:, 0:n], AF.Copy, [RP[b]], [RKT])
                sv_ = get_w(("wv", l), [(v3(8, 384), wr[:, :, O_V:O_V + 384])])
                wvv = v3(8, 384)(ring[sv_])
                tiles = [(t * 128, 128, 4 * c + t) for t in range(4)]
                if c == 3:
                    tiles += [(512, 16, 16), (528, 16, 17)]
                for lo, n, gt in tiles:
                    for which, wsl, rsl in ((0, wk, sk), (1, wvv, sv_)):
                        b = bank()
                        for k in range(8):
                            P.mm(ps[b][:n, 0:384], hT[:, k, lo:lo + n], wsl[:, k, :], k == 0, k == 7, [RRING[rsl], RH], [RP[b]], k == 7)
                        P.act(stg[:n, which, 0:384], ps[b][:n, 0:384], AF.Copy, [RP[b]], [RST])
                        if which == 1:
                            if gt < 16:
                                P.dve("tensor_copy", [RP[b]], [RVT], out=vtok[:, gt, :], in_=ps[b][:, 0:384])
                            else:
                                P.dve("tensor_copy", [RP[b]], [RVTS], out=vtoks[:, gt - 16, :], in_=ps[b][:16, 0:384])
                    if gt < 16:
                        spdma(okp[l, gt * 128:(gt + 1) * 128, :], stg[:, 0, 0:384], [RST], [], "okv", out=True)
                        spdma(ovp[l, gt * 128:(gt + 1) * 128, :], stg[:, 1, 0:384], [RST], [], "okv", out=True)
                    else:
                        bb = gt - 16
                        spdma(oks[l, bb * 16:(bb + 1) * 16, :], stg[:16, 0, 0:384], [RST], [], "okv", out=True)
                        spdma(ovs[l, bb * 16:(bb + 1) * 16, :], stg[:16, 1, 0:384], [RST], [], "okv", out=True)
            ck("prekv")
            for g4 in range(4):
                j, hh = g4 // 2, g4 % 2
                pooldma(wpbd[hh * 64:(hh + 1) * 64, j, hh * 64:(hh + 1) * 64], w_pool[l, g4], [RWP], "wp")
            pooldma(wa2[0:16, :], w_a2[l], [RWA2], "wa2")
            pooldma(wa2[16:17, :], b_a[l:l + 1, :], [RWA2], "wa2")
            P.op("dve", lambda e: e.memset(Sst[:], 0.0), writes=[RS])
            for c in range(4):
                chunk(l, c, wr)

        def chunk(l, c, wr):
            xv = xT[:, :, 512 * c:512 * c + CW]
            sbl = subs(c)
            norm_to_h(xv, RX[c], sbl, gain(0, l))
            si = get_w(("wq", l), [(v3(8, 384), wr[:, :, O_Q:O_Q + 384])])
            wq = v3(8, 384)(ring[si])
            for lo, n in sbl:
                for j in range(3):
                    b = bank()
                    for k in range(8):
                        P.mm(ps[b][:, 0:n], wq[:, k, j * 128:(j + 1) * 128], hT[:, k, lo:lo + n], k == 0, k == 7, [RRING[si], RH], [RP[b]], k == 7)
                    P.act(qT[:, j, lo:lo + n], ps[b][:, 0:n], AF.Copy, [RP[b]], [RQ])
            ck("q")
            nkt = 4 * c + 4
            kts = list(range(nkt - 1, -1, -1))
            for h in range(6):
                j, pb = h // 2, 64 * (h % 2)
                bo = bank(hold=True)
                st = {}

                def S1(ti):
                    kt = kts[ti]
                    i = kt - 4 * c
                    c0 = 128 * i if i > 0 else 0
                    ncl = 512 - c0
                    r = ti % 3
                    bz = bank()
                    P.mm(ps[bz][:, 0:ncl], kT[pb:pb + 64, j, kt * 128:(kt + 1) * 128], qT[pb:pb + 64, j, c0:512], True, True, [RKT, RQ], [RP[bz]], True)
                    P.act(ez[r][:, 0:ncl], ps[bz][:, 0:ncl], AF.Exp, [RP[bz]], [REZ[r]], scale=0.125)
                    if i >= 0:
                        P.dve("tensor_tensor", [REZ[r], RC], [REZ[r]], out=ez[r][:, 0:ncl], in0=ez[r][:, 0:ncl], in1=maskb[:, 0:ncl], op=ALU.mult)
                    P.act(spb[r][:, 0:ncl], ez[r][:, 0:ncl], AF.Ln, [REZ[r]], [RSP[r]], bias=1.0)
                    st[ti] = (kt, c0, ncl, r)

                def S2(ti):
                    kt, c0, ncl, r = st[ti]
                    r2 = ti % 2
                    bc = bank()
                    lastk = (kt == nkt - 1)
                    P.mm(ps[bc][:, 0:ncl], trib, spb[r][:, 0:ncl], True, lastk, [RSP[r], RC], [RP[bc]], lastk)
                    if not lastk:
                        P.mm(ps[bc][:, 0:ncl], onesb, Rb[:, c0:512], False, True, [RR, RC], [RP[bc]], True)
                    if kt > 0:
                        if lastk:
                            P.op("dve", lambda e: e.memset(Rb[:, 0:384], 0.0), writes=[RR])
                            P.dve("tensor_copy", [RSP[r]], [RR], out=Rb[:, c0:512], in_=spb[r][:, 0:ncl])
                        else:
                            P.dve("tensor_tensor", [RSP[r], RR], [RR], out=Rb[:, c0:512], in0=Rb[:, c0:512], in1=spb[r][:, 0:ncl], op=ALU.add)
                    P.act(enb[r2][:, 0:ncl], ps[bc][:, 0:ncl], AF.Exp, [RP[bc]], [REN[r2]], scale=-1.0)
                    P.dve("tensor_tensor", [REZ[r], REN[r2]], [RWB[r]], out=wb[r][:, 0:ncl], in0=ez[r][:, 0:ncl], in1=enb[r2][:, 0:ncl], op=ALU.mult)

                def S3(ti):
                    kt, c0, ncl, r = st[ti]
                    P.mm(ps[bo][:, c0:512], vtok[:, kt, j * 128:(j + 1) * 128], wb[r][:, 0:ncl], kt == nkt - 1, kt == 0, [RVT, RWB[r]], [RP[bo]], kt == 0, sgc=True)

                nt_ = len(kts)
                for step in range(nt_ + 2):
                    if step < nt_:
                        S1(step)
                    if 0 <= step - 1 < nt_:
                        S2(step - 1)
                    if 0 <= step - 2 < nt_:
                        S3(step - 2)
                P.act(oaT[pb:pb + 64, j, 0:512], ps[bo][pb:pb + 64, 0:512], AF.Copy, [RP[bo]], [ROA])
                held.discard(bo)
            ck("sb%d" % c)
            if c == 3:
                for b_ in range(2):
                    sb_sample(l, b_)
            ck("sbs%d" % c)
            pool_branch(l, c, wr)
            ck("pool%d" % c)
            for hf in range(2):
                gla_half(l, c, hf, wr)
            ck("gla%d" % c)
            merged = big[:, 0:8, :]
            brs = [(w_ba, 3, 128, oaT, ROA, lambda t: t[:, 0:1536].rearrange("p (k c) -> p k c", k=3), lambda w: rows(w)),
                   (w_bb, 2, 128, obT, ROB, lambda t: t[:, 0:1024].rearrange("p (k c) -> p k c", k=2), lambda w: rows(w)),
                   (w_bc, 4, 96, ocT, ROC, lambda t: t[:96, 0:2048].rearrange("p (k c) -> p k c", k=4), lambda w: w.rearrange("(h v) c -> v h c", v=96))]
            for br, (wb_, kc_, kp_, src_, rsrc_, vf_, rf_) in enumerate(brs):
                for hf in range(2):
                    sb_i = get_w(("mb", l, br, hf), [(vf_, rf_(wb_[l])[:, :, hf * 512:(hf + 1) * 512])])
                    wbv = vf_(ring[sb_i])
                    sg_i = get_w(("mg", l, br, hf), [(v3(8, 512), wr[:, :, O_G + br * 1024 + hf * 512:O_G + br * 1024 + (hf + 1) * 512])])
                    gvw = v3(8, 512)(ring[sg_i])
                    for mm_ in range(4):
                        m = hf * 4 + mm_
                        for lo, n in sbl:
                            bg, bb = bank(), bank()
                            for k in range(8):
                                P.mm(ps[bg][:, 0:n], gvw[:, k, mm_ * 128:(mm_ + 1) * 128], hT[:, k, lo:lo + n], k == 0, k == 7, [RRING[sg_i], RH], [RP[bg]], k == 7)
                            for k in range(kc_):
                                P.mm(ps[bb][:, 0:n], wbv[:kp_, k, mm_ * 128:(mm_ + 1) * 128], src_[:kp_, k, lo:lo + n], k == 0, k == kc_ - 1, [RRING[sb_i], rsrc_], [RP[bb]], k == kc_ - 1)
                            sgb = sg3[(m + br) % 3]
                            rsg = RSG3[(m + br) % 3]
                            P.act(sgb[:, 0:n], ps[bg][:, 0:n], AF.Sigmoid, [RP[bg]], [rsg])
                            if br == 0:
                                P.dve("tensor_tensor", [rsg, RP[bb]], [RST], out=stg[:, m, lo:lo + n], in0=sgb[:, 0:n], in1=ps[bb][:, 0:n], op=ALU.mult)
                            else:
                                mt_, rmt_ = (mt1, RMT1) if (m % 2 == 0) else (mt2, RMT2)
                                P.dve("tensor_tensor", [rsg, RP[bb]], [rmt_], out=mt_[:, 0:n], in0=sgb[:, 0:n], in1=ps[bb][:, 0:n], op=ALU.mult)
                                if br == 1:
                                    P.dve("tensor_tensor", [rmt_, RST], [RST], out=stg[:, m, lo:lo + n], in0=stg[:, m, lo:lo + n], in1=mt_[:, 0:n], op=ALU.add)
                                else:
                                    P.dve("tensor_tensor", [rmt_, RST], [RBIG], out=merged[:, m, lo:lo + n], in0=stg[:, m, lo:lo + n], in1=mt_[:, 0:n], op=ALU.add)
            proj_to_stg(c, ("mix", l), w_mix[l], merged, RBIG)
            post_norm_add(c, gain(1, l))
            ck("mix%d" % c)
            xattn(l, c)
            ck("xattn%d" % c)
            norm_to_h(xv, RX[c], sbl, gain(4, l))
            hid = big
            wfr = rows(w_fi[l])
            wor = rows(w_fo[l])
            for jh in range(2):
                for j0, nj in ((0, 2), (2, 2), (4, 2), (6, 2), (8, 2), (10, 1)):
                    jg = jh * 11 + j0
                    si = get_w(("fi", l, jg), [
                        (lambda t, nj=nj: t[:, 0:16 * nj * 128].rearrange("p (k r c) -> p k r c", k=8, r=2)[:, :, 0, :], wfr[:, :, jg * 128:(jg + nj) * 128]),
                        (lambda t, nj=nj: t[:, 0:16 * nj * 128].rearrange("p (k r c) -> p k r c", k=8, r=2)[:, :, 1, :], wfr[:, :, 2816 + jg * 128:2816 + (jg + nj) * 128])])
                    wv = ring[si][:, 0:16 * nj * 128].rearrange("p (k r c) -> p k r c", k=8, r=2)
                    for jj in range(nj):
                        for lo, n in sbl:
                            b1, b2 = bank(), bank()
                            for k in range(8):
                                P.mm(ps[b1][:, 0:n], wv[:, k, 0, jj * 128:(jj + 1) * 128], hT[:, k, lo:lo + n], k == 0, k == 7, [RRING[si], RH], [RP[b1]], k == 7)
                            for k in range(8):
                                P.mm(ps[b2][:, 0:n], wv[:, k, 1, jj * 128:(jj + 1) * 128], hT[:, k, lo:lo + n], k == 0, k == 7, [RRING[si], RH], [RP[b2]], k == 7)
                            P.act(ft1[:, 0:n], ps[b1][:, 0:n], AF.Silu, [RP[b1]], [RFT1])
                            P.dve("tensor_tensor", [RFT1, RP[b2]], [RBIG], out=hid[:, j0 + jj, lo:lo + n], in0=ft1[:, 0:n], in1=ps[b2][:, 0:n], op=ALU.mult)
                for mb in range(4):
                    si = get_w(("fo", l, jh, mb), [(v3(11, 256), wor[:, jh * 11:(jh + 1) * 11, mb * 256:(mb + 1) * 256])])
                    wv = v3(11, 256)(ring[si])
                    for lo, n in sbl:
                        for mm_ in range(2):
                            m = mb * 2 + mm_
                            b = bank()
                            for k in range(11):
                                P.mm(ps[b][:, 0:n], wv[:, k, mm_ * 128:(mm_ + 1) * 128], hid[:, k, lo:lo + n], k == 0, k == 10, [RRING[si], RBIG], [RP[b]], k == 10)
                            if jh == 0:
                                P.act(stg[:, m, lo:lo + n], ps[b][:, 0:n], AF.Copy, [RP[b]], [RST])
                            else:
                                P.dve("tensor_tensor", [RST, RP[b]], [RST], out=stg[:, m, lo:lo + n], in0=stg[:, m, lo:lo + n], in1=ps[b][:, 0:n], op=ALU.add)
                                P.act(sqb[:, m, lo:lo + n], stg[:, m, lo:lo + n], AF.Square, [RST], [RSQ])
            post_norm_add(c, gain(5, l))
            ck("ffn%d" % c)

        def sb_sample(l, b_):
            pooldma(kcs[:, :, :], csk[l, b_].rearrange("(t p) f -> p t f", p=128), [RKCS], "kcs")
            pooldma(vcs[:, :, :], csv[l, b_].rearrange("(t p) f -> p t f", p=128), [RVCS], "vcs")
            for t in range(8):
                bk = bank()
                pv = ps[bk][:, :].bitcast(BF16)
                for j in range(3):
                    P.tr(pv[:, j * 128:(j + 1) * 128], kcs[:, t, j * 128:(j + 1) * 128], identb, [RKCS, RC], [RP[bk]], j == 2)
                P.act(kcsT[:, :, t * 128:(t + 1) * 128], pv[:, 0:384].rearrange("p (a b) -> p a b", a=3), AF.Copy, [RP[bk]], [RKCST])
            ck("ss1")
            bo = bank(hold=True)
            qc0 = 512 + 16 * b_
            NCOL = 96
            Rn = Rb[:16, 256:256 + NCOL]
            for kt in range(8, -1, -1):
                np_ = 16 if kt == 8 else 128
                bzs = [bank(), bank()]
                for par in range(2):
                    for hh in range(3):
                        h = 2 * hh + par
                        j, pb = h // 2, 64 * (h % 2)
                        qv = qT[pb:pb + 64, j, qc0:qc0 + 16]
                        if kt == 8:
                            kv = kT[pb:pb + 64, j, 2048 + 16 * b_:2048 + 16 * b_ + 16]
                        else:
                            kv = kcsT[pb:pb + 64, j, kt * 128:(kt + 1) * 128]
                        P.mm(ps[bzs[par]][:np_, hh * 16:hh * 16 + 16], kv, qv, True, True, [RKT, RKCST, RQ], [RP[bzs[par]]], hh == 2)
                if kt == 7:
                    ck("ss2")
                if kt == 8:
                    ck("ss3")
                r = 0
                for par in range(2):
                    P.act(ez[r][:np_, par * 48:par * 48 + 48], ps[bzs[par]][:np_, 0:48], AF.Exp, [RP[bzs[par]]], [REZ[r]], scale=0.125)
                if kt == 8:
                    for h in range(6):
                        P.dve("tensor_tensor", [REZ[r], RC], [REZ[r]], out=ez[r][:16, h * 16:h * 16 + 16], in0=ez[r][:16, h * 16:h * 16 + 16],
                              in1=maskb[:16, 0:16], op=ALU.mult)
                P.act(spb[r][:np_, 0:NCOL], ez[r][:np_, 0:NCOL], AF.Ln, [REZ[r]], [RSP[r]], bias=1.0)
                bc = bank()
                P.mm(ps[bc][:np_, 0:NCOL], trib[:np_, :np_], spb[r][:np_, 0:NCOL], True, kt == 8, [RSP[r], RC], [RP[bc]], kt == 8)
                if kt <= 7:
                    P.mm(ps[bc][:, 0:NCOL], onesb[:16, :], Rn, False, kt == 7, [RR, RC], [RP[bc]], kt == 7)
                if kt < 7:
                    P.mm(ps[bc][:, 0:NCOL], onesb, Rb[:, 0:NCOL], False, True, [RR, RC], [RP[bc]], True)
                ck("ss4")
                P.act(enb[r][:np_, 0:NCOL], ps[bc][:np_, 0:NCOL], AF.Exp, [RP[bc]], [REN[r]], scale=-1.0)
                P.dve("tensor_tensor", [REZ[r], REN[r]], [RWB[r]], out=wb[r][:np_, 0:NCOL], in0=ez[r][:np_, 0:NCOL], in1=enb[r][:np_, 0:NCOL], op=ALU.mult)
                if kt == 8:
                    P.dve("tensor_copy", [RSP[r]], [RR], out=Rn, in_=spb[r][:16, 0:NCOL])
                elif kt == 7:
                    P.dve("tensor_copy", [RSP[r]], [RR], out=Rb[:, 0:NCOL], in_=spb[r][:, 0:NCOL])
                elif kt > 0:
                    P.dve("tensor_tensor", [RSP[r], RR], [RR], out=Rb[:, 0:NCOL], in0=Rb[:, 0:NCOL], in1=spb[r][:, 0:NCOL], op=ALU.add)
                ck("ss5")
                for h in range(6):
                    j = h // 2
                    if kt == 8:
                        lv = vtoks[:16, b_, j * 128:(j + 1) * 128]
                    else:
                        lv = vcs[:, kt, j * 128:(j + 1) * 128]
                    hc = (h % 2) * 48 + (h // 2) * 16
                    P.mm(ps[bo][:, hc:hc + 16], lv, wb[r][:np_, hc:hc + 16], (kt == 8 and h == 0), kt == 0, [RVTS, RVCS, RWB[r]], [RP[bo]], (h == 5 and kt == 0), sgc=True)
            for h in range(6):
                j, pb = h // 2, 64 * (h % 2)
                hc = (h % 2) * 48 + (h // 2) * 16
                P.act(oaT[pb:pb + 64, j, qc0:qc0 + 16], ps[bo][pb:pb + 64, hc:hc + 16], AF.Copy, [RP[bo]], [ROA])
            held.discard(bo)

        def pool_branch(l, c, wr):
            si = get_w(("wu", l), [(v3(8, 256), wr[:, :, O_U:O_U + 256])])
            wu = v3(8, 256)(ring[si])
            W = 527 if c < 3 else PW
            if c > 0:
                P.act(uT[:, :, 0:15], uhalo[:, :, :], AF.Copy, [RUH], [RU])
            else:
                P.op("dve", lambda e: e.memset(uT[:, :, 0:15], 0.0), writes=[RU])
            segs = [(0, 512, 15)]
            if c == 3:
                segs += [(512, 16, 527 + 15), (528, 16, 527 + 31 + 15)]
                for b_ in range(2):
                    off = 527 + 31 * b_
                    for j in range(2):
                        spdma(uT[:, j, off:off + 15], spool[l, b_, :, j * 128:(j + 1) * 128].rearrange("t p -> p t"), [], [RU], "hist", nc_ok=True)
            for lo, n, dst in segs:
                for j in range(2):
                    b = bank()
                    for k in range(8):
                        P.mm(ps[b][:, 0:n], wu[:, k, j * 128:(j + 1) * 128], hT[:, k, lo:lo + n], k == 0, k == 7, [RRING[si], RH], [RP[b]], k == 7)
                    P.act(uT[:, j, dst:dst + n], ps[b][:, 0:n], AF.Copy, [RP[b]], [RU])
            P.act(uhalo[:, :, :], uT[:, :, 512:527], AF.Copy, [RU], [RUH])
            if c == 3:
                outs = [(512 - 15, opp[l]), (513, ops_[l, 0]), (529, ops_[l, 1])]
                for lo, od in outs:
                    b = bank()
                    for k in range(8):
                        P.mm(ps[b][:15, 0:256], hT[:, k, lo:lo + 15], wu[:, k, :], k == 0, k == 7, [RRING[si], RH], [RP[b]], k == 7)
                    P.act(pD[:15, 0:256], ps[b][:15, 0:256], AF.Copy, [RP[b]], [RPD])
                    spdma(od, pD[:15, 0:256], [RPD], [], "opool", out=True)
            P.dve("tensor_tensor", [RU], [RPA], out=pA[:, :, 1:W], in0=uT[:, :, 1:W], in1=uT[:, :, 0:W - 1], op=ALU.add)
            P.dve("tensor_tensor", [RPA], [RPB], out=pB[:, :, 3:W], in0=pA[:, :, 3:W], in1=pA[:, :, 1:W - 2], op=ALU.add)
            P.dve("tensor_tensor", [RPB], [RPC], out=pC[:, 7:W], in0=pB[:, 1, 7:W], in1=pB[:, 1, 3:W - 4], op=ALU.add)
            P.dve("tensor_tensor", [RPC], [RPD], out=pD[:, 15:W], in0=pC[:, 15:W], in1=pC[:, 7:W - 8], op=ALU.add)
            sel = [(0, 64, 0, pA[0:64, 0, :], 0.5, RPA), (64, 128, 0, pB[64:128, 0, :], 0.25, RPB), (0, 64, 1, pC[0:64, :], 0.125, RPC), (64, 128, 1, pD[64:128, :], 0.0625, RPD)]
            for p0, p1, j, sv, iw, rsv in sel:
                for lo, n, src in segs:
                    P.dve("scalar_tensor_tensor", [rsv, RU], [RPL], out=pooled[p0:p1, j, lo:lo + n], in0=sv[:, src:src + n], scalar=iw,
                          in1=uT[p0:p1, j, src:src + n], op0=ALU.mult, op1=ALU.subtract)
                if c == 0:
                    P.dve("tensor_tensor", [rsv, RC], [RPT_], out=ptmp[p0:p1, 0:15], in0=sv[:, 15:30], in1=invc[p0:p1, j, :], op=ALU.mult)
                    P.dve("tensor_tensor", [RPT_, RU], [RPL], out=pooled[p0:p1, j, 0:15], in0=ptmp[p0:p1, 0:15], in1=uT[p0:p1, j, 15:30], op=ALU.subtract)
            for lo, n in subs(c):
                for j in range(2):
                    b = bank()
                    P.mm(ps[b][:, 0:n], wpbd[:, j, :], pooled[:, j, lo:lo + n], True, True, [RWP, RPL], [RP[b]], True)
                    P.act(obT[:, j, lo:lo + n], ps[b][:, 0:n], AF.Copy, [RP[b], RC], [ROB], scale=psc[:, l, j:j + 1])

        def gla_half(l, c, hf, wr):
            base = 256 * hf
            sbl = [(base, 256, 0)]
            tiles = [(base, 128, 0, 0), (base + 128, 128, 1, 128)]
            samp = (c == 3 and hf == 1)
            if samp:
                sbl += [(512, 32, 256)]
                tiles += [(512, 16, 2, 256), (528, 16, 3, 272)]
            ncl = 288 if samp else 256

            def wG():
                return get_w(("gG", l), [(v3(8, 400), wr[:, :, O_GC:O_GC + 400])])

            def wK():
                return get_w(("gK", l), [(v3(8, 384), wr[:, :, O_KC:O_KC + 384])])

            def wV():
                return get_w(("gV", l), [(v3(8, 384), wr[:, :, O_VC:O_VC + 384])])

            def wQ():
                return get_w(("gQ", l), [(v3(8, 384), wr[:, :, O_QC:O_QC + 384])])

            s = wG(); w = v3(8, 400)(ring[s])
            for lo, n, lc in sbl:
                b = bank()
                for k in range(8):
                    P.mm(ps[b][:16, 0:n], w[:, k, 384:400], hT[:, k, lo:lo + n], k == 0, k == 7, [RRING[s], RH], [RP[b]], k == 7)
                P.act(lrT[:16, lo:lo + n], ps[b][:16, 0:n], AF.Copy, [RP[b]], [RLR])
            ck("g1")
            for lo, n, ti, lc in tiles:
                sm = ti >= 2
                bi = ti - 2
                vc_d = vcts[:, bi, :] if sm else vct[:, ti, :]
                la_d = lats[:, bi, :] if sm else lat[:, ti, :]
                kt_d = ktts[:, bi, :] if sm else ktt[:, ti, :]
                b = bank()
                P.mm(ps[b][:n, 0:384], lrT[0:17, lo:lo + n], wa2[0:17, :], True, True, [RLR, RWA2], [RP[b]], True)
                P.act(gt1[:n, 0:384], ps[b][:n, 0:384], AF.Exp, [RP[b]], [RGT1], scale=-1.0)
                P.act(la_d, gt1[:n, 0:384], AF.Ln, [RGT1], [RLAS if sm else RLA], bias=1.0)
                rla = RLAS if sm else RLA
                b = bank()
                P.mm(ps[b][:n, 0:384], tbd[:n, :n], la_d, True, True, [rla, RC], [RP[b]], True)
                P.act(ekt[:n, :], ps[b][:n, 0:384], AF.Exp, [RP[b]], [REKT])
                s = wK(); w = v3(8, 384)(ring[s])
                b = bank()
                for k in range(8):
                    P.mm(ps[b][:n, 0:384], hT[:, k, lo:lo + n], w[:, k, :], k == 0, k == 7, [RRING[s], RH], [RP[b]], k == 7)
                P.dve("tensor_tensor", [REKT, RP[b]], [RKTTS if sm else RKTT], out=kt_d, in0=ps[b][:n, 0:384], in1=ekt[:n, :], op=ALU.mult)
                s = wV(); w = v3(8, 384)(ring[s])
                b = bank()
                for k in range(8):
                    P.mm(ps[b][:n, 0:384], hT[:, k, lo:lo + n], w[:, k, :], k == 0, k == 7, [RRING[s], RH], [RP[b]], k == 7)
                P.act(vc_d, ps[b][:n, 0:384], AF.Copy, [RP[b]], [RVCTS if sm else RVCT])
                b = bank()
                for h in range(4):
                    P.mm(ps[b][:96, h * 128:h * 128 + n], la_d[:, h * 96:(h + 1) * 96], tbd[:n, :n], True, True, [rla, RC], [RP[b]], h == 3)
                pvw = ps[b][:96, :].rearrange("p (h t) -> p h t", h=4)[:, :, 0:n]
                P.act(eq[:, :, lc:lc + n], pvw, AF.Exp, [RP[b]], [REQ], scale=-1.0)
                P.act(eqi[:, :, lc:lc + n], pvw, AF.Exp, [RP[b]], [REQI])
            ck("g3")
            P.act(ebl[:, :, 0:4], eq[:, :, 63:256:64], AF.Copy, [REQ], [REBL])
            if samp:
                P.act(ebl[:, :, 4:6], eq[:, :, 271:288:16], AF.Copy, [REQ], [REBL])
            ub = []
            for g in range(4):
                ti, h2 = g // 2, g % 2
                b = bank(hold=True)
                ub.append(b)
                for h in range(4):
                    P.mm(ps[b][:96, h * 96:(h + 1) * 96], ktt[h2 * 64:(h2 + 1) * 64, ti, h * 96:(h + 1) * 96], vct[h2 * 64:(h2 + 1) * 64, ti, h * 96:(h + 1) * 96],
                         True, True, [RKTT, RVCT], [RP[b]], h == 3)
            for g in range(4):
                b = ub[g]
                P.act(Sbf[:, g], Sst[:], AF.Copy, [RS], [RSBF])
                P.dve("tensor_tensor", [RS, RP[b]], [RSTM], out=Stm[:, :, :], in0=Sst[:], in1=ps[b][:96, 0:384].rearrange("p (h v) -> p h v", h=4), op=ALU.add)
                held.discard(b)
                for h in range(4):
                    P.dve("tensor_scalar", [RSTM, REBL], [RS], out=Sst[:, h, :], in0=Stm[:, h, :], scalar1=ebl[:, h, g:g + 1], scalar2=None, op0=ALU.mult)
            if samp:
                spdma(ogp[l].rearrange("h k v -> k h v"), Sst[:], [RS], [], "ogp", out=True)
                for b_ in range(2):
                    spdma(Ss[:, :, :], sgla[l, b_].rearrange("h k v -> k h v"), [], [RSS], "sgla")
                    P.act(Sbf[:, 4 + b_], Ss[:, :, :], AF.Copy, [RSS], [RSBF])
                    b = bank()
                    for h in range(4):
                        P.mm(ps[b][:96, h * 96:(h + 1) * 96], ktts[:, b_, h * 96:(h + 1) * 96], vcts[:, b_, h * 96:(h + 1) * 96], True, True, [RKTTS, RVCTS], [RP[b]], h == 3)
                    P.dve("tensor_tensor", [RSS, RP[b]], [RSTM], out=Stm[:, :, :], in0=Ss[:, :, :], in1=ps[b][:96, 0:384].rearrange("p (h v) -> p h v", h=4), op=ALU.add)
                    for h in range(4):
                        P.dve("tensor_scalar", [RSTM, REBL], [RSS], out=Ss[:, h, :], in0=Stm[:, h, :], scalar1=ebl[:, h, 4 + b_:5 + b_], scalar2=None, op0=ALU.mult)
                    spdma(ogs[l, b_].rearrange("h k v -> k h v"), Ss[:, :, :], [RSS], [], "ogs", out=True)
            ck("g2")
            for lo, n, lc in sbl:
                for h in range(4):
                    s = wQ(); w = v3(8, 384)(ring[s])
                    b = bank()
                    for k in range(8):
                        P.mm(ps[b][:96, 0:n], w[:, k, h * 96:(h + 1) * 96], hT[:, k, lo:lo + n], k == 0, k == 7, [RRING[s], RH], [RP[b]], k == 7)
                    P.dve("scalar_tensor_tensor", [RP[b], REQ], [RQT], out=qtT[:, h, lc:lc + n], in0=ps[b][:96, 0:n], scalar=96.0 ** -0.5, in1=eq[:, h, lc:lc + n], op0=ALU.mult, op1=ALU.mult)
                    s = wK(); w = v3(8, 384)(ring[s])
                    b = bank()
                    for k in range(8):
                        P.mm(ps[b][:96, 0:n], w[:, k, h * 96:(h + 1) * 96], hT[:, k, lo:lo + n], k == 0, k == 7, [RRING[s], RH], [RP[b]], k == 7)
                    P.dve("tensor_tensor", [RP[b], REQI], [RKTT_], out=ktT[:, h, lc:lc + n], in0=ps[b][:96, 0:n], in1=eqi[:, h, lc:lc + n], op=ALU.mult)
                    s = wG(); w = v3(8, 400)(ring[s])
                    b = bank()
                    for k in range(8):
                        P.mm(ps[b][:96, 0:n], w[:, k, h * 96:(h + 1) * 96], hT[:, k, lo:lo + n], k == 0, k == 7, [RRING[s], RH], [RP[b]], k == 7)
                    P.act(sgc[:, h, lc:lc + n], ps[b][:96, 0:n], AF.Silu, [RP[b]], [RSG])
            ck("g4")
            bsets = [(attb, RATT, osb, ROS, osq, ROSQ, grs, RGRS), (attb2, RATT2, osb2, ROS2, osq2, ROSQ2, grs2, RGRS2)]

            def i_s1(tile, bs_, stt):
                lo, n, ti, lc = tile
                att_, ratt = bs_[0], bs_[1]
                b = bank()
                for h in range(4):
                    P.mm(ps[b][:n, h * 128:h * 128 + n], ktT[:, h, lc:lc + n], qtT[:, h, lc:lc + n], True, True, [RKTT_, RQT], [RP[b]], h == 3)
                for h in range(4):
                    P.dve("tensor_tensor", [RP[b], RC], [ratt], out=att_[:n, h, 0:n], in0=ps[b][:n, h * 128:h * 128 + n], in1=mbd[:n, 0:n], op=ALU.mult)

            def i_s2(tile, bs_, stt):
                lo, n, ti, lc = tile
                att_, ratt, osb_, ros, osq_, rosq = bs_[0:6]
                sm = ti >= 2
                bi = ti - 2
                vc_d = vcts[:, bi, :] if sm else vct[:, ti, :]
                rvc = RVCTS if sm else RVCT
                bo = bank()
                for h in range(4):
                    P.mm(ps[bo][:96, h * 128:h * 128 + n], vc_d[:, h * 96:(h + 1) * 96], att_[:n, h, 0:n], True, False, [rvc, ratt], [RP[bo]], False)
                    if sm:
                        P.mm(ps[bo][:96, h * 128:h * 128 + n], Sbf[:, 4 + bi, h, :], qtT[:, h, lc:lc + n], False, True, [RSBF, RQT], [RP[bo]], h == 3)
                    else:
                        for h2 in range(2):
                            g = ti * 2 + h2
                            P.mm(ps[bo][:96, h * 128 + h2 * 64:h * 128 + h2 * 64 + 64], Sbf[:, g, h, :], qtT[:, h, lc + h2 * 64:lc + h2 * 64 + 64], False, h2 == 1,
                                 [RSBF, RQT], [RP[bo]], h == 3 and h2 == 1)
                ov = ps[bo][:96, :].rearrange("p (h t) -> p h t", h=4)[:, :, 0:n]
                P.act(osb_[:, :, 0:n], ov, AF.Copy, [RP[bo]], [ros])
                P.act(osq_[:, :, 0:n], osb_[:, :, 0:n], AF.Square, [ros], [rosq])

            def i_s3(tile, bs_, stt):
                lo, n, ti, lc = tile
                osq_, rosq, grs_, rgrs = bs_[4:8]
                bs = bank()
                for h in range(4):
                    P.mm(ps[bs][:96, h * 128:h * 128 + n], onesb[:96, :96], osq_[:, h, 0:n], True, True, [rosq, RC], [RP[bs]], h == 3)
                sv = ps[bs][:96, :].rearrange("p (h t) -> p h t", h=4)[:, :, 0:n]
                gv = grs_[:, :].rearrange("p (h t) -> p h t", h=4)[:, :, 0:n]
                P.act(gv, sv, AF.Ln, [RP[bs], RC], [rgrs], scale=1.0 / 96, bias=epsb[:96, 0:1])
                P.act(gv, gv, AF.Exp, [rgrs], [rgrs], scale=-0.5)

            def i_s4(tile, bs_, stt):
                lo, n, ti, lc = tile
                osb_, ros, grs_, rgrs = bs_[2], bs_[3], bs_[6], bs_[7]
                gv = grs_[:, :].rearrange("p (h t) -> p h t", h=4)[:, :, 0:n]
                P.dve("tensor_tensor", [ros, rgrs], [ros], out=osb_[:, :, 0:n], in0=osb_[:, :, 0:n], in1=gv, op=ALU.mult)
                for h in range(4):
                    P.dve("scalar_tensor_tensor", [ros, RC, RSG], [ROC], out=ocT[:, h, lo:lo + n], in0=osb_[:, h, 0:n], scalar=gln[:, l, h:h + 1],
                          in1=sgc[:, h, lc:lc + n], op0=ALU.mult, op1=ALU.mult)

            for p0 in range(0, len(tiles), 2):
                pair = tiles[p0:p0 + 2]
                for stage in (i_s1, i_s2, i_s3, i_s4):
                    for k_, tile in enumerate(pair):
                        stage(tile, bsets[k_], None)

        def xattn(l, c):
            xv = xT[:, :, 512 * c:512 * c + CW]
            norm_to_h(xv, RX[c], subs(c), gain(2, l))
            qx = big[:, 0:8, :]
            for hf in range(2):
                si = get_w(("xq", l, hf), [(v3(8, 512), rows(w_xq[l])[:, :, hf * 512:(hf + 1) * 512])])
                wv = v3(8, 512)(ring[si])
                for lo, n in subs(c):
                    for mm_ in range(4):
                        b = bank()
                        for k in range(8):
                            P.mm(ps[b][:, 0:n], wv[:, k, mm_ * 128:(mm_ + 1) * 128], hT[:, k, lo:lo + n], k == 0, k == 7, [RRING[si], RH], [RP[b]], k == 7)
                        P.act(qx[:, hf * 4 + mm_, lo:lo + n], ps[b][:, 0:n], AF.Copy, [RP[b]], [RBIG])
            ox = hT
            blocks = [(0, 512, None)]
            if c == 3:
                blocks += [(512, 16, 0), (528, 16, 1)]
            for lo, n, sb_ in blocks:
                if sb_ is None:
                    mk_, mv_, rmk, rmv = mkT, mvt, RMK, RMV
                else:
                    pooldma(mkl[:, :, :], cmk[l, sb_].rearrange("(t p) f -> p t f", p=128), [RMKL], "mkl")
                    pooldma(mvs[:, :, :], cmv[l, sb_].rearrange("(t p) f -> p t f", p=128), [RMVS], "mvs")
                    for t in range(2):
                        for hb in range(2):
                            bk = bank()
                            pv = ps[bk][:, :].bitcast(BF16)
                            for j in range(4):
                                d = hb * 4 + j
                                P.tr(pv[:, j * 128:(j + 1) * 128], mkl[:, t, d * 128:(d + 1) * 128], identb, [RMKL, RC], [RP[bk]], j == 3)
                            P.act(mkTs[:, hb * 4:(hb + 1) * 4, t * 128:(t + 1) * 128], pv[:, 0:512].rearrange("p (a b) -> p a b", a=4), AF.Copy, [RP[bk]], [RMKTS])
                    mk_, mv_, rmk, rmv = mkTs, mvs, RMKTS, RMVS
                pts = [(pTb, RPT, rden, RRD), (pTb2, RPT2, rden2, RRD2)]

                def X1(h):
                    pT_, rpt = pts[h % 2][0], pts[h % 2][1]
                    for mt_ in range(2):
                        b = bank()
                        for dd in range(2):
                            P.mm(ps[b][:, 0:n], mk_[:, 2 * h + dd, mt_ * 128:(mt_ + 1) * 128], qx[:, 2 * h + dd, lo:lo + n], dd == 0, dd == 1, [rmk, RBIG], [RP[b]], dd == 1)
                        P.act(pT_[:, mt_, 0:n], ps[b][:, 0:n], AF.Exp, [RP[b]], [rpt], scale=1.0 / 16)

                def X2(h):
                    pT_, rpt, rd_, rrd = pts[h % 2]
                    bd = bank()
                    for mt_ in range(2):
                        P.mm(ps[bd][:, 0:n], onesb, pT_[:, mt_, 0:n], mt_ == 0, mt_ == 1, [rpt, RC], [RP[bd]], mt_ == 1)
                    P.act(rd_[:, 0:n], ps[bd][:, 0:n], AF.Ln, [RP[bd]], [rrd])
                    P.act(rd_[:, 0:n], rd_[:, 0:n], AF.Exp, [rrd], [rrd], scale=-1.0)
                    for dd in range(2):
                        b = bank()
                        for mt_ in range(2):
                            P.mm(ps[b][:, 0:n], mv_[:, mt_, (2 * h + dd) * 128:(2 * h + dd + 1) * 128], pT_[:, mt_, 0:n], mt_ == 0, mt_ == 1, [rmv, rpt], [RP[b]], mt_ == 1)
                        P.dve("tensor_tensor", [RP[b], rrd], [RH], out=ox[:, 2 * h + dd, lo:lo + n], in0=ps[b][:, 0:n], in1=rd_[:, 0:n], op=ALU.mult)

                for step in range(5):
                    if step < 4:
                        X1(step)
                    if step >= 1:
                        X2(step - 1)
            proj_to_stg(c, ("xo", l), w_xo[l], ox, RH)
            post_norm_add(c, gain(3, l))

        try:
            ck("setup")
            for l in range(nl):
                layer(l)
        except StopBuild as ex:
            print("STOPPED at", ex)

        for t in range(17 if STOP[0] is None else 0):
            rn = 128 if t < 16 else 32
            c = min(t // 4, 3)
            col = t * 128
            for hb in range(2):
                b = bank()
                for j in range(4):
                    k = hb * 4 + j
                    P.tr(ps[b][:rn, j * 128:(j + 1) * 128], xT[:, k, col:col + rn], ident, [RX[c], RC], [RP[b]], j == 3)
                P.act(stg[:rn, hb, 0:512], ps[b][:rn, :], AF.Copy, [RP[b]], [RST])
            dst = yp[t * 128:(t + 1) * 128, :] if t < 16 else ys
            spdma(dst.rearrange("p (a b) -> p a b", a=2), stg[:rn, 0:2, 0:512], [RST], [], "yout", out=True)
        P.emit()
    return nc


def _consts():
    c = np.zeros((128, C_END), np.float32)
    p = np.arange(128)[:, None]
    q = np.arange(128)[None, :]
    c[:, C_ID:C_ID + 128] = (p == q)
    c[:, C_TRI:C_TRI + 128] = (p >= q)
    c[:, C_ONE:C_ONE + 128] = 1.0
    c[:, C_MASK:C_MASK + 512] = (p < np.arange(512)[None, :])
    same = (p // 64) == (q // 64)
    c[:, C_TBD:C_TBD + 128] = np.where((p <= q) & same, 1.0 / 16.0, 0.0)
    c[:, C_MBD:C_MBD + 128] = ((p <= q) & same)
    inv = np.zeros((128, 2, 15), np.float32)
    for pp in range(128):
        for j in range(2):
            w = 2 ** (2 * j + (1 if pp >= 64 else 0) + 1)
            inv[pp, j] = 1.0 / np.minimum(np.arange(15) + 1, w)
    c[:, C_INV:C_INV + 30] = inv.reshape(128, 30)
    return c


_NC_CACHE = {}


def kernel(x_prompt, x_sample, mem_prompt, cache_sb_k, cache_sb_v, state_pool, state_gla,
           cache_mem_k, cache_mem_v, w_in, w_gla_a2, b_gla_a, gla_norm, w_pool, pool_scale,
           w_branch_a, w_branch_b, w_branch_c, w_mix_out, mem_norm, w_xq, w_xk, w_xv, w_xo,
           w_ffn_in, w_ffn_out, norm_mix_pre, norm_mix_post, norm_x_pre, norm_x_post,
           norm_ffn_pre, norm_ffn_post, _nl=NL):
    f = lambda a: np.ascontiguousarray(np.asarray(a, dtype=np.float32))
    fl = lambda a: np.ascontiguousarray(np.asarray(a, dtype=np.float32)[:_nl])
    if _nl not in _NC_CACHE:
        _NC_CACHE[_nl] = build(_nl)
    nc = _NC_CACHE[_nl]
    shared = dict(w_in=fl(w_in), w_gla_a2=fl(w_gla_a2), b_gla_a=fl(b_gla_a), gla_norm=fl(gla_norm), w_pool=fl(w_pool),
                  pool_scale=fl(pool_scale), w_branch_a=fl(w_branch_a), w_branch_b=fl(w_branch_b), w_branch_c=fl(w_branch_c),
                  w_mix_out=fl(w_mix_out), mem_norm=fl(mem_norm), w_xq=fl(w_xq), w_xk=fl(w_xk), w_xv=fl(w_xv), w_xo=fl(w_xo),
                  w_ffn_in=fl(w_ffn_in), w_ffn_out=fl(w_ffn_out), norm_mix_pre=fl(norm_mix_pre), norm_mix_post=fl(norm_mix_post),
                  norm_x_pre=fl(norm_x_pre), norm_x_post=fl(norm_x_post), norm_ffn_pre=fl(norm_ffn_pre), norm_ffn_post=fl(norm_ffn_post),
                  cst=_consts())
    x_prompt = f(x_prompt); x_sample = f(x_sample); mem_prompt = f(mem_prompt)
    cache_sb_k = fl(cache_sb_k); cache_sb_v = fl(cache_sb_v); state_pool = fl(state_pool); state_gla = fl(state_gla)
    cache_mem_k = fl(cache_mem_k); cache_mem_v = fl(cache_mem_v)
    in_maps = []
    for i in range(8):
        s2 = slice(2 * i, 2 * i + 2)
        d = dict(shared)
        d.update(xp=x_prompt[i], xs=np.ascontiguousarray(x_sample[s2].reshape(32, 1024)), mem=mem_prompt[i],
                 csk=np.ascontiguousarray(cache_sb_k[:, s2].reshape(_nl, 2, 1024, 384)),
                 csv=np.ascontiguousarray(cache_sb_v[:, s2].reshape(_nl, 2, 1024, 384)),
                 spool=np.ascontiguousarray(state_pool[:, s2]), sgla=np.ascontiguousarray(state_gla[:, s2]),
                 cmk=np.ascontiguousarray(cache_mem_k[:, s2].reshape(_nl, 2, 256, 1024)),
                 cmv=np.ascontiguousarray(cache_mem_v[:, s2].reshape(_nl, 2, 256, 1024)))
        in_maps.append(d)
    res = run_bass_kernel_spmd(nc, in_maps, core_ids=list(range(8)))
    R = res.results
    def cat(k, ax=0):
        a = np.stack([r[k] for r in R], axis=ax)
        if ax == 1 and a.shape[0] < 4:
            a = np.concatenate([a, np.zeros((4 - a.shape[0],) + a.shape[1:], a.dtype)], axis=0)
        return a
    y_p = cat("yp")
    y_s = cat("ys").reshape(16, 16, 1024)
    kp = cat("okp", 1).reshape(4, 8, 2048, 6, 64)
    vp = cat("ovp", 1).reshape(4, 8, 2048, 6, 64)
    pp = cat("opp", 1)
    gp = cat("ogp", 1)
    mk = cat("omk", 1).reshape(4, 8, 256, 4, 256)
    mv = cat("omv", 1).reshape(4, 8, 256, 4, 256)
    ks = cat("oks", 1).reshape(4, 16, 16, 6, 64)
    vs = cat("ovs", 1).reshape(4, 16, 16, 6, 64)
    pls = cat("ops", 1).reshape(4, 16, 15, 256)
    gs = cat("ogs", 1).reshape(4, 16, 4, 96, 96)
    return (y_p, y_s, kp, vp, pp, gp, mk, mv, ks, vs, pls, gs)
```
